# Optimizing a Trainium2 kernel written in Bass

```python
import math
import jax, jax.numpy as jnp
from jax import lax
import numpy as np

D_MODEL = 4096
BATCH = 4
SEQ = 4096
DEPTH = 1

PLE_DIM = 256
CHUNK = 128
QBLOCK = 128
HEAD_DIM = 128
A_WIDTH = D_MODEL // 2
B_WIDTH = D_MODEL - A_WIDTH
SGU_HEADS = A_WIDTH // HEAD_DIM
SB_HEADS = B_WIDTH // HEAD_DIM
D_IN = 2 * A_WIDTH + 3 * B_WIDTH
D_FF = 4 * D_MODEL
EPS = 1e-6

kernel_name = "hybrid_sgu_stickbreaking_block"


def _rms_norm(x, g):
    x32 = x.astype(jnp.float32)
    y = x32 * lax.rsqrt(jnp.mean(x32 * x32, axis=-1, keepdims=True) + EPS)
    return (y * g.astype(jnp.float32)).astype(x.dtype)


def _head_rms_norm(y, g, n_heads):
    b, s, w = y.shape
    dh = w // n_heads
    y32 = y.reshape(b, s, n_heads, dh).astype(jnp.float32)
    y32 = y32 * lax.rsqrt(jnp.mean(y32 * y32, axis=-1, keepdims=True) + EPS)
    y32 = y32 * g.reshape(n_heads, dh).astype(jnp.float32)
    return y32.reshape(b, s, w).astype(y.dtype)


def _chunked_sgu(u, v, w_s, b_s):
    b, s, w = u.shape
    nc = s // CHUNK
    u = jax.nn.gelu(u)
    v = jax.nn.gelu(v)
    v32 = v.reshape(b, nc, CHUNK, SGU_HEADS, HEAD_DIM).astype(jnp.float32)
    mu = jnp.mean(v32, axis=-1, keepdims=True)
    var = jnp.mean(jnp.square(v32 - mu), axis=-1, keepdims=True)
    vn = (v32 - mu) * lax.rsqrt(var + EPS)
    causal = jnp.tril(jnp.ones((CHUNK, CHUNK), dtype=bool))
    w_masked = jnp.where(causal[None], w_s.astype(jnp.float32), 0.0)
    mixed = jnp.einsum('hts,bnshd->bnthd', w_masked, vn)
    mixed = mixed + b_s.astype(jnp.float32).T[None, None, :, :, None]
    return u * mixed.reshape(b, s, w).astype(u.dtype)


def _stick_breaking(q, k, v):
    b, s, w = q.shape
    nb = s // QBLOCK
    scale = 1.0 / math.sqrt(HEAD_DIM)

    def heads(t):
        return t.reshape(b, s, SB_HEADS, HEAD_DIM).transpose(0, 2, 1, 3).astype(jnp.float32)

    qh = heads(q) * scale
    kh = heads(k)
    vh = heads(v)
    q_blocks = qh.reshape(b, SB_HEADS, nb, QBLOCK, HEAD_DIM).transpose(2, 0, 1, 3, 4)
    key_pos = jnp.arange(s)

    def block(args):
        qb, bi = args
        z = jnp.einsum('bhqd,bhkd->bhqk', qb, kh)
        q_pos = bi * QBLOCK + jnp.arange(QBLOCK)
        causal = key_pos[None, :] < q_pos[:, None]
        log_keep = jnp.where(causal, jax.nn.log_sigmoid(-z), 0.0)
        log_after = lax.cumsum(log_keep, axis=3, reverse=True) - log_keep
        a = jnp.where(causal, jnp.exp(jax.nn.log_sigmoid(z) + log_after), 0.0)
        return jnp.einsum('bhqk,bhkd->bhqd', a, vh)

    o = lax.map(block, (q_blocks, jnp.arange(nb)))
    o = o.transpose(1, 0, 3, 2, 4).reshape(b, s, w)
    return o.astype(q.dtype)


def setup_inputs(seed: int = 0) -> dict:
    key = jax.random.key(seed)
    ks = jax.random.split(key, 18)
    f32 = jnp.float32

    def nrm(k, shape, scale):
        return jax.random.normal(k, shape, f32) * scale

    def gain(k, shape):
        return 1.0 + 0.02 * jax.random.normal(k, shape, f32)

    return {
        "x": nrm(ks[0], (BATCH, SEQ, D_MODEL), 1.0),
        "p": nrm(ks[1], (DEPTH, BATCH, SEQ, PLE_DIM), 1.0),
        "norm_mix": gain(ks[2], (DEPTH, D_MODEL)),
        "w_in": nrm(ks[3], (DEPTH, D_MODEL, D_IN), D_MODEL ** -0.5),
        "w_s": nrm(ks[4], (DEPTH, SGU_HEADS, CHUNK, CHUNK), CHUNK ** -0.5),
        "b_s": gain(ks[5], (DEPTH, SGU_HEADS, CHUNK)),
        "norm_a_out": gain(ks[6], (DEPTH, A_WIDTH)),
        "norm_b_out": gain(ks[7], (DEPTH, B_WIDTH)),
        "w_out": nrm(ks[8], (DEPTH, D_MODEL, D_MODEL), D_MODEL ** -0.5),
        "norm_ffn": gain(ks[9], (DEPTH, D_MODEL)),
        "w_up": nrm(ks[10], (DEPTH, D_MODEL, D_FF), D_MODEL ** -0.5),
        "w_down": nrm(ks[11], (DEPTH, D_FF, D_MODEL), D_FF ** -0.5),
        "norm_ple": gain(ks[12], (DEPTH, D_MODEL)),
        "w_ple_gate": nrm(ks[13], (DEPTH, D_MODEL, D_MODEL), D_MODEL ** -0.5),
        "w_ple_proj": nrm(ks[14], (DEPTH, PLE_DIM, D_MODEL), PLE_DIM ** -0.5),
        "norm_final": gain(ks[15], (D_MODEL,)),
    }


def reference(x, p, norm_mix, w_in, w_s, b_s, norm_a_out, norm_b_out, w_out,
              norm_ffn, w_up, w_down, norm_ple, w_ple_gate, w_ple_proj, norm_final):
    h = x
    for i in range(DEPTH):
        a = _rms_norm(h, norm_mix[i])
        z = jnp.einsum('bsd,de->bse', a, w_in[i])
        o1 = A_WIDTH
        o2 = 2 * A_WIDTH
        o3 = o2 + B_WIDTH
        o4 = o3 + B_WIDTH
        u_a, v_a = z[..., :o1], z[..., o1:o2]
        q_b, k_b, v_b = z[..., o2:o3], z[..., o3:o4], z[..., o4:]
        y_a = _chunked_sgu(u_a, v_a, w_s[i], b_s[i])
        y_b = _stick_breaking(q_b, k_b, v_b)
        y = jnp.concatenate([_head_rms_norm(y_a, norm_a_out[i], SGU_HEADS),
                             _head_rms_norm(y_b, norm_b_out[i], SB_HEADS)], axis=-1)
        h = h + jnp.einsum('bsd,de->bse', y, w_out[i])
        m = _rms_norm(h, norm_ffn[i])
        hid = jnp.square(jax.nn.relu(jnp.einsum('bsd,df->bsf', m, w_up[i])))
        h = h + jnp.einsum('bsf,fd->bsd', hid, w_down[i])
        gate = jax.nn.sigmoid(jnp.einsum('bsd,de->bse', _rms_norm(h, norm_ple[i]), w_ple_gate[i]))
        e = jnp.einsum('bsk,kd->bsd', p[i], w_ple_proj[i])
        h = h + gate * e
    return _rms_norm(h, norm_final)
```

```python
import math
from contextlib import ExitStack

import numpy as np
import ml_dtypes

import concourse.bass as bass
import concourse.mybir as mybir
from concourse.bass_utils import run_bass_kernel_spmd

F32 = mybir.dt.float32
BF16 = mybir.dt.bfloat16
AF = mybir.ActivationFunctionType
ALU = mybir.AluOpType
AX = mybir.AxisListType

EPS = 1e-6
P = 128
TT = 512
NB = TT // P

FULL = dict(T=2048, TP=2048, D=4096, AW=2048, BW=2048, DFF=16384, PLE=256, KC=16, NQ=8)


class Sem:
    _id = 0

    def __init__(self, nc, name):
        self.h = nc.alloc_semaphore(name)
        self.n = 0
        Sem._id += 1
        self.id = Sem._id


class Buf:
    def __init__(self, ap=None, dsem=None):
        self.ap = ap
        self.w = None
        self.r = {}
        self.dsem = dsem


class K:
    def __init__(self, cfg):
        self.cfg = cfg
        self.nc = bass.Bass("TRN2", target_bir_lowering=False)
        nc = self.nc
        self.eng = {"pe": nc.tensor, "act": nc.scalar, "dve": nc.vector, "pool": nc.gpsimd, "sp": nc.sync}
        self.prog = {e: Sem(nc, "p_" + e) for e in ["pe", "act", "dve", "pool"]}
        self.deferred = []
        self.waited = {e: {} for e in self.eng}
        self.allsems = list(self.prog.values())
        self.ps_i = 0
        self.cv_rate = 0
        self.cv_boost = [0, 0]
        self.cv_queues = []
        self.pp_i = 0
        self.pa_i = 0

    def newsem(self, name):
        s = Sem(self.nc, name)
        self.allsems.append(s)
        return s

    def _wait(self, e, deps, own_ok):
        for (s, v) in deps:
            if own_ok and e in self.prog and s is self.prog[e]:
                continue
            if self.waited[e].get(s.id, 0) >= v:
                continue
            self.eng[e].wait_ge(s.h, v)
            self.waited[e][s.id] = v

    def _deps(self, reads, writes):
        deps = []
        for b in reads:
            if b.w:
                deps.append(b.w)
        for b in writes:
            if b.w:
                deps.append(b.w)
            deps.extend(b.r.values())
        return deps

    def _mark(self, ev, reads, writes):
        for b in reads:
            b.r[ev[0].id] = ev
        for b in writes:
            b.w = ev
            b.r = {}

    def op(self, e, fn, reads=(), writes=()):
        self._wait(e, self._deps(reads, writes), e == "pe")
        ins = fn()
        s = self.prog[e]
        s.n += 1
        ins.then_inc(s.h, 1)
        self._mark((s, s.n), reads, writes)
        return ins

    def dma(self, q, fns, sem, reads=(), writes=()):
        self._wait(q, self._deps(reads, writes), False)
        for fn in fns:
            ins = fn()
            ins.then_inc(sem.h, 16)
            sem.n += 16
        self._mark((sem, sem.n), reads, writes)

    def barrier(self):
        evs = [(s, s.n) for s in self.allsems if s.n > 0]
        for e in self.eng:
            self._wait(e, evs, False)

    def cv_meter(self, k, queues):
        todo = []
        for q in queues:
            while q and len(todo) < k:
                todo.append(q.pop(0))
        if not todo:
            return
        self._wait("pool", [(self.prog["pe"], self.prog["pe"].n)], False)
        for (fns, s, b) in todo:
            self.dma("pool", fns, s, writes=[b])

    def cv_emit(self, k):
        for _ in range(min(k, len(self.cvq))):
            fns, s, b = self.cvq.pop(0)
            self.dma("pool", fns, s, writes=[b])

    def ensure_conv(self, bf):
        for q in (self.cvq_in, self.cvq, self.cvq3):
            for item in [it for it in q if it[2] is bf]:
                q.remove(item)
                self.dma("pool", item[0], item[1], writes=[item[2]])
        assert bf.w is not None

    def psum(self):
        b = self.ps[self.ps_i % 8]
        self.ps_i += 1
        return b

    def psum_proj(self):
        b = self.ps[self.pp_i % 6]
        self.pp_i += 1
        return b

    def psum_aux(self):
        b = self.ps[6 + self.pa_i % 2]
        self.pa_i += 1
        return b

    def rsqrt(self, buf, out_ap, in_ap):
        nc = self.nc
        self.op("act", lambda: nc.scalar.activation(out=out_ap, in_=in_ap, func=AF.Sqrt), reads=[buf], writes=[buf])
        self.op("dve", lambda: nc.vector.reciprocal(out=out_ap, in_=out_ap), reads=[buf], writes=[buf])

    def build(self):
        nc, cfg = self.nc, self.cfg
        T, TP, D, AW, BW, DFF, PLE, KC, NQ = (cfg[k] for k in ["T", "TP", "D", "AW", "BW", "DFF", "PLE", "KC", "NQ"])
        self.NCH = NCH = D // P
        NHA, NHB = AW // P, BW // P
        DIN = 2 * AW + 3 * BW
        TK = TP + T
        self.NCS = NCS = 3 * P + 3 * NCH + NHB + NHA
        dt = nc.dram_tensor
        self.x_own = dt("x_own", [T, D], F32, kind="ExternalInput").ap()
        self.x_pre = dt("x_pre", [TP, D], F32, kind="ExternalInput").ap()
        self.p_own = dt("p_own", [T, PLE], F32, kind="ExternalInput").ap()
        wnames = dict(w_in=(D, DIN), w_out=(D, D), w_up=(D, DFF), w_down=(DFF, D), w_gate=(D, D), w_ple=(PLE, D))
        self.wf = {n: dt(n, list(s), F32, kind="ExternalInput").ap() for n, s in wnames.items()}
        self.cst_d = dt("cst", [P, NCS], F32, kind="ExternalInput").ap()
        self.cbf_d = dt("cbf", [P, 2 * P + NB * TT], BF16, kind="ExternalInput").ap()
        self.ga_d = dt("ga_rep", [P, AW], F32, kind="ExternalInput").ap()
        self.gfin_d = dt("gfin_rep", [P, D], F32, kind="ExternalInput").ap()
        self.wsT_d = dt("wsT", [P, NHA * P], F32, kind="ExternalInput").ap()
        self.out_d = dt("out", [T, D], F32, kind="ExternalOutput").ap()
        self.wb = {n: dt("b_" + n, [s[1] // TT, P, s[0] // P, TT], BF16, kind="Internal").ap() for n, s in wnames.items()}
        sk = "ExternalOutput" if cfg.get("debug") else "Internal"
        self.KT_d = dt("KT_s", [NHB, P, TK], BF16, kind=sk).ap()
        self.V_d = dt("V_s", [TK, BW], BF16, kind=sk).ap()
        self.QT_d = dt("QT_s", [NHB, P, T], BF16, kind=sk).ap()
        self.YT_d = dt("YT_s", [NCH, P, T], BF16, kind=sk).ap()

        self.ps = [Buf(nc.alloc_psum_tensor("ps%d" % i, [P, TT], F32).ap()) for i in range(8)]

        self.wev = {}
        self.cvq = []
        self.cvq_in = []
        kv0 = (2 * AW + BW) // TT
        GA_ = AW // TT
        self.cvq3 = []
        CP = 8
        for n in ["w_in", "w_out", "w_up", "w_down", "w_gate", "w_ple"]:
            rows, cols = wnames[n]
            ng, kch = cols // TT, rows // P
            order = list(range(ng))
            if n == "w_in":
                rest = []
                for g in range(GA_):
                    rest += [g, GA_ + g]
                rest += list(range(2 * GA_, kv0))
                order = list(range(kv0, ng)) + rest
            cls = {}
            for g in order:
                key = g if n == "w_in" else "all"
                if key not in cls:
                    cls[key] = (self.newsem("cv_%s_%s" % (n, key)), Buf())
                s, bf = cls[key]
                self.wev[(n, g)] = bf
                for c0 in range(0, kch, CP):
                    c1 = min(kch, c0 + CP)
                    fn = (lambda n=n, g=g, c0=c0, c1=c1: nc.gpsimd.dma_start(
                        out=self.wb[n][g, :, c0:c1, :],
                        in_=self.wf[n][c0 * P:c1 * P, g * TT:(g + 1) * TT].rearrange("(c p) n -> p c n", p=P)))
                    if n == "w_in" and kv0 <= g < kv0 + (ng - kv0) // 2:
                        self.dma("pool", [fn], s, writes=[bf])
                    elif n == "w_in":
                        self.cvq_in.append(([fn], s, bf))
                    elif n == "w_out":
                        self.cvq.append(([fn], s, bf))
                    else:
                        self.cvq3.append(([fn], s, bf))
        self.cv_total = len(self.cvq)

        with ExitStack() as glob:
            def sb(name, shape, dtype, stack=glob):
                return stack.enter_context(nc.sbuf_tensor("s_" + name, shape, dtype)).ap()
            self.sb = sb
            cs = self.newsem("d_cst")
            self.cst = Buf(sb("cst", [P, NCS], F32), cs)
            self.cbf = Buf(sb("cbf", [P, 2 * P + NB * TT], BF16), cs)
            self.dma("sp", [lambda: nc.sync.dma_start(out=self.cst.ap, in_=self.cst_d),
                            lambda: nc.sync.dma_start(out=self.cbf.ap, in_=self.cbf_d)], cs,
                     writes=[self.cst, self.cbf])
            c = self.cst.ap
            self.ident = c[:, 0:P]
            self.onesf = c[:, P:2 * P]
            self.trilT = c[:, 2 * P:3 * P]
            o = 3 * P
            self.g_mix = c[:, o:o + NCH]
            self.g_ffn = c[:, o + NCH:o + 2 * NCH]
            self.g_ple = c[:, o + 2 * NCH:o + 3 * NCH]
            o += 3 * NCH
            self.g_b = c[:, o:o + NHB]
            self.bsT = c[:, o + NHB:o + NHB + NHA]
            cb = self.cbf.ap
            self.tinc = cb[:, 0:P]
            self.tcomp = cb[:, P:2 * P]
            self.dmask = cb[:, 2 * P:].rearrange("p (j q) -> p j q", j=NB)
            self.stat = [Buf(sb("stat%d" % i, [P, 8], F32)) for i in range(8)]
            self.stat_i = 0
            self.wring = [Buf(sb("wr%d" % i, [P, KC, TT], BF16), self.newsem("d_wr%d" % i)) for i in range(3)]
            self.wring_i = 0
            self.wqueue = []
            self.wq_next = 0

            self.stage12()
            self.stage12_end()
            self.barrier()
            self.stage3()
            self.barrier()
            self.stage4()
            self.barrier()
        return nc

    def wq_add(self, name, r0, nrows, c0, ncols):
        self.wqueue.append((name, r0, nrows, c0, ncols))
        return len(self.wqueue) - 1

    def wq_get(self, idx, pf=2):
        nc = self.nc
        last = min(len(self.wqueue) - 1, idx + pf)
        while self.wq_next <= last:
            name, r0, nrows, c0, ncols = self.wqueue[self.wq_next]
            slot = self.wring[self.wq_next % 3]
            kc = nrows // P
            g, cl = c0 // TT, r0 // P
            src = self.wb[name][g, :, cl:cl + kc, :]
            fns = [lambda src=src, slot=slot, kc=kc: nc.sync.dma_start(out=slot.ap[:, 0:kc, :], in_=src)]
            dep = self.wev[(name, g)]
            if dep.w is None:
                self.ensure_conv(dep)
            self.dma("sp", fns, slot.dsem, reads=[dep], writes=[slot])
            self.wq_next += 1
        return self.wring[idx % 3]

    def norm_stats(self, src, junks, junk_aps=None):
        nc = self.nc
        D = self.cfg["D"]
        H = D // 2
        st = self.stat[self.stat_i % len(self.stat)]
        self.stat_i += 1
        self.op("dve", lambda: nc.vector.memset(st.ap[:, 0:2], 0.0), writes=[st])
        for hf in range(2):
            jb = junks[hf]
            j_ap = jb.ap[:, 0:H] if junk_aps is None else junk_aps[hf]
            self.op("act", lambda hf=hf, jb=jb, j_ap=j_ap: nc.scalar.activation(out=j_ap, in_=src.ap[:, hf * H:(hf + 1) * H], func=AF.Square,
                                                                    accum_out=st.ap[:, hf:hf + 1]), reads=[src], writes=[jb, st])
        self.op("dve", lambda: nc.vector.tensor_tensor(out=st.ap[:, 2:3], in0=st.ap[:, 0:1], in1=st.ap[:, 1:2], op=ALU.add),
                reads=[st], writes=[st])
        self.op("dve", lambda: nc.vector.tensor_scalar(out=st.ap[:, 2:3], in0=st.ap[:, 2:3], scalar1=1.0 / D, scalar2=EPS,
                                                       op0=ALU.mult, op1=ALU.add), reads=[st], writes=[st])
        self.rsqrt(st, st.ap[:, 3:4], st.ap[:, 2:3])
        return st

    def norm_scale(self, src, st, hf, dstb, d_ap):
        nc = self.nc
        H = self.cfg["D"] // 2
        s_ap = src.ap[:, hf * H:(hf + 1) * H]
        if hf == 0:
            self.op("dve", lambda: nc.vector.tensor_scalar(out=d_ap, in0=s_ap, scalar1=st.ap[:, 3:4], scalar2=None, op0=ALU.mult),
                    reads=[src, st], writes=[dstb])
        else:
            self.op("act", lambda: nc.scalar.activation(out=d_ap, in_=s_ap, func=AF.Copy, scale=st.ap[:, 3:4]),
                    reads=[src, st], writes=[dstb])

    def norm_pre(self, src):
        H = self.cfg["D"] // 2
        st = self.norm_stats(src, [self.junk, self.junk])
        for hf in range(2):
            self.norm_scale(src, st, hf, src, src.ap[:, hf * H:(hf + 1) * H])

    def norm_tr(self, src, gain, dstT, b):
        H = self.cfg["D"] // 2
        for hf in range(2):
            self.transpose_block(src, src.ap[:, hf * H:(hf + 1) * H], self.NCH // 2, dstT, b, gain, c_off=hf * (self.NCH // 2))

    def norm_block(self, src, gain, dstT, b, tmp=None, st=None):
        H = self.cfg["D"] // 2
        if st is None:
            st = self.norm_stats(src, tmp)
        for hf in range(2):
            self.norm_scale(src, st, hf, tmp[hf], tmp[hf].ap[:, 0:H])
            self.transpose_block(tmp[hf], tmp[hf].ap[:, 0:H], self.NCH // 2, dstT, b, gain, c_off=hf * (self.NCH // 2))
        return st

    def norm_blocks_q(self, hbs, gain, dstT, xq, sts):
        nc = self.nc
        D, NCH = self.cfg["D"], self.NCH
        NQ4 = len(xq)
        Q = D // NQ4
        cq = NCH // NQ4

        def scale(b, q):
            src, st = hbs[b], sts[b]
            s_ap = src.ap[:, q * Q:(q + 1) * Q]
            if q % 2 == 0:
                self.op("dve", lambda: nc.vector.tensor_scalar(out=xq[q].ap, in0=s_ap, scalar1=st.ap[:, 3:4], scalar2=None, op0=ALU.mult),
                        reads=[src, st], writes=[xq[q]])
            else:
                self.op("act", lambda: nc.scalar.activation(out=xq[q].ap, in_=s_ap, func=AF.Copy, scale=st.ap[:, 3:4]),
                        reads=[src, st], writes=[xq[q]])
        for q in range(NQ4):
            scale(0, q)
        for b in range(len(hbs)):
            for q in range(NQ4):
                self.transpose_block(xq[q], xq[q].ap, cq, dstT, b, gain, c_off=q * cq)
                if b + 1 < len(hbs):
                    scale(b + 1, q)

    def transpose_block(self, src, src_ap, nch, dstT, b, gain=None, c_off=0, aux=False):
        nc = self.nc
        k = 0
        for c0 in range(0, nch, 4):
            n = min(4, nch - c0)
            ps = self.psum_aux() if aux else self.psum()
            pv = ps.ap.rearrange("p (j q) -> p j q", j=4)

            def tr(c0=c0, n=n, pv=pv):
                for j in range(n):
                    ins = nc.tensor.transpose(pv[:, j, :], src_ap[:, (c0 + j) * P:(c0 + j + 1) * P], self.ident)
                return ins
            self.op("pe", tr, reads=[src, self.cst], writes=[ps])
            if gain is None:
                e = "act" if (k % 2 == 0) else "dve"
                k += 1
                o = dstT.ap[:, c_off + c0:c_off + c0 + n, b * P:(b + 1) * P]
                if e == "act":
                    self.op("act", lambda o=o, pv=pv, n=n: nc.scalar.copy(out=o, in_=pv[:, 0:n, :]), reads=[ps], writes=[dstT])
                else:
                    self.op("dve", lambda o=o, pv=pv, n=n: nc.vector.tensor_copy(out=o, in_=pv[:, 0:n, :]), reads=[ps], writes=[dstT])
            elif (c0 // 4) % 2 == 0:
                cc0 = c_off + c0
                o = dstT.ap[:, cc0:cc0 + n, b * P:(b + 1) * P]
                gb = gain[:, cc0:cc0 + n].unsqueeze(2).broadcast_to([P, n, P])
                self.op("dve", lambda o=o, pv=pv, n=n, gb=gb: nc.vector.tensor_tensor(out=o, in0=pv[:, 0:n, :], in1=gb, op=ALU.mult),
                        reads=[ps, self.cst], writes=[dstT])
            else:
                for j in range(n):
                    cc = c_off + c0 + j
                    o = dstT.ap[:, cc, b * P:(b + 1) * P]
                    self.op("act", lambda o=o, pv=pv, j=j, cc=cc: nc.scalar.activation(
                        out=o, in_=pv[:, j, :], func=AF.Copy, scale=gain[:, cc:cc + 1]), reads=[ps, self.cst], writes=[dstT])

    def proj(self, tiles, actT, nchunks, mode, ncols=TT, ntok=TT):
        nc = self.nc
        KC = self.cfg["KC"]
        nout = (ntok // P) if mode == "tok" else (ncols // P)
        pss = [self.psum_proj() for _ in range(nout)]
        for ti, widx in enumerate(tiles):
            wt = self.wq_get(widx)
            kc = min(KC, nchunks - ti * KC)
            for o in range(nout):
                def mm(o=o, ti=ti, wt=wt, kc=kc):
                    for k in range(kc):
                        c = ti * KC + k
                        first = (c == 0)
                        lastc = (c == nchunks - 1)
                        if mode == "tok":
                            ins = nc.tensor.matmul(pss[o].ap[:, 0:ncols], lhsT=actT.ap[:, c, o * P:(o + 1) * P],
                                                   rhs=wt.ap[:, k, 0:ncols], start=first, stop=lastc)
                        else:
                            ins = nc.tensor.matmul(pss[o].ap[:, 0:ntok], lhsT=wt.ap[:, k, o * P:(o + 1) * P],
                                                   rhs=actT.ap[:, c, 0:ntok], start=first, stop=lastc)
                    return ins
                self.op("pe", mm, reads=[wt, actT], writes=[pss[o]])
        self.tick()
        if self.cv_rate:
            if self.cv_boost[0] > 0:
                self.cv_boost[0] -= 1
                self.cv_meter(max(self.cv_rate, self.cv_boost[1]), self.cv_queues)
            else:
                self.cv_meter(self.cv_rate if self.cvq_in else 1, self.cv_queues)
        return pss

    def stage12(self):
        nc, cfg = self.nc, self.cfg
        T, TP, D, AW, BW, KC = cfg["T"], cfg["TP"], cfg["D"], cfg["AW"], cfg["BW"], cfg["KC"]
        NCH = self.NCH
        NHA, NHB = AW // P, BW // P
        GA, GB = AW // TT, BW // TT
        nkt = NCH // KC
        with ExitStack() as st:
            sb = lambda n, s, d: self.sb(n, s, d, st)
            xb = [Buf(sb("xb%d" % i, [P, D], F32), self.newsem("d_xb%d" % i)) for i in range(2)]
            self.junk = Buf(sb("junk", [P, D // 2], BF16))
            aT = Buf(sb("aT", [P, NCH, TT], BF16))
            gus = [Buf(sb("gu%d" % i, [P, NB, TT], F32)) for i in range(2)]
            tmpf = [Buf(sb("tf%d" % i, [P, TT], F32)) for i in range(5)]
            gvs = [tmpf[0], tmpf[1], tmpf[2], Buf(sb("gv3", [P, TT], F32))]
            vn = [Buf(sb("vn%d" % i, [P, TT], BF16)) for i in range(NB)]
            yns = [Buf(sb("yn%d" % i, [P, TT], F32)) for i in range(NB)]
            self.deferred = []
            wsT = Buf(sb("wsTf", [P, NHA * P], F32), self.newsem("d_ws"))
            wsm = Buf(sb("wsm", [P, NHA, P], BF16))
            ga = Buf(sb("ga", [P, AW], F32), wsT.dsem)
            yaT = Buf(sb("yaT", [P, NHA, TT], BF16), self.newsem("d_yaT"))
            qst = [Buf(sb("qst%d" % i, [P, NB, TT], BF16), self.newsem("d_qst%d" % i)) for i in range(2)]
            self.dma("sp", [lambda: nc.sync.dma_start(out=wsT.ap, in_=self.wsT_d),
                            lambda: nc.sync.dma_start(out=ga.ap, in_=self.ga_d)], wsT.dsem, writes=[wsT, ga])
            for h in range(NHA):
                self.op("dve", lambda h=h: nc.vector.tensor_tensor(out=wsm.ap[:, h, :], in0=wsT.ap[:, h * P:(h + 1) * P],
                                                                  in1=self.trilT, op=ALU.mult),
                        reads=[wsT, self.cst], writes=[wsm])

            ntile_pre, ntile_own = TP // TT, T // TT
            jobs = []
            def add_group(kind, g, c0):
                return [self.wq_add("w_in", k * KC * P, KC * P, c0, TT) for k in range(nkt)]
            plan = []
            for ti in range(ntile_pre + ntile_own):
                own = ti >= ntile_pre
                groups = []
                if own:
                    for g in range(GA):
                        groups.append(("u", g, add_group("u", g, g * TT)))
                        groups.append(("v", g, add_group("v", g, AW + g * TT)))
                    for g in range(GB):
                        groups.append(("q", g, add_group("q", g, 2 * AW + g * TT)))
                for g in range(GB):
                    groups.append(("k", g, add_group("k", g, 2 * AW + BW + g * TT)))
                for g in range(GB):
                    groups.append(("vb", g, add_group("vb", g, 2 * AW + 2 * BW + g * TT)))
                plan.append((ti, own, groups))

            nproj = sum(len(gr) for (_, _, gr) in plan)
            npieces = len(self.cvq_in) + len(self.cvq)
            self.cv_rate = 2
            self.cv_queues = [self.cvq_in, self.cvq]
            self.cv_boost = [GB, max(1, NCH // 8)]
            xi = 0
            qi = 0
            C2 = 2.0 * math.sqrt(2.0 / math.pi)
            for (ti, own, groups) in plan:
                xsrc = self.x_own if own else self.x_pre
                t0 = (ti - ntile_pre) * TT if own else ti * TT
                tg = TP + t0 if own else t0
                def pre(pl, b):
                    ti_, own_, _ = pl
                    xs_ = self.x_own if own_ else self.x_pre
                    r0 = ((ti_ - ntile_pre) * TT if own_ else ti_ * TT) + b * P
                    xbuf = xb[b % 2]
                    self.dma("sp", [lambda: nc.sync.dma_start(out=xbuf.ap, in_=xs_[r0:r0 + P, :])], xbuf.dsem, writes=[xbuf])
                    self.norm_pre(xbuf)
                pi = plan.index((ti, own, groups))
                if pi == 0:
                    pre(plan[0], 0)
                    pre(plan[0], 1)
                for b in range(NB):
                    self.norm_tr(xb[b % 2], self.g_mix, aT, b)
                    if b + 2 < NB:
                        pre(plan[pi], b + 2)
                    elif pi + 1 < len(plan):
                        pre(plan[pi + 1], b + 2 - NB)
                for (kind, g, widxs) in groups:
                    if kind in ("u", "v", "vb"):
                        pss = self.proj(widxs, aT, NCH, "tok")
                    else:
                        pss = self.proj(widxs, aT, NCH, "feat")
                    if kind == "u":
                        gu_g = gus[g % 2]
                        for b in range(NB):
                            self.gelu(pss[b], gu_g, gu_g.ap[:, b, :], tmpf, C2)
                    elif kind == "v":
                        HG = TT // P
                        gu_g = gus[g % 2]
                        for b in range(NB):
                            self.gelu(pss[b], gvs[b], gvs[b].ap, tmpf, C2)
                        for b in range(NB):
                            vnb, ynb = vn[b], yns[b]
                            self.sgu_A(gvs[b], g, b, tmpf, vnb, C2)
                            self.defer(2, lambda g=g, b=b, vnb=vnb, ynb=ynb, gu_g=gu_g: self.sgu_B(g, b, gu_g, tmpf, vnb, wsm, ga, ynb))
                            self.defer(4, lambda g=g, b=b, ynb=ynb: self.transpose_block(ynb, ynb.ap, HG, yaT, b, None, c_off=g * HG, aux=True))
                    elif kind in ("q", "k"):
                        stg = qst[qi % 2]
                        qi += 1
                        for e in range(NB):
                            if e % 2 == 0:
                                self.op("act", lambda e=e, stg=stg, pss=pss, kind=kind: nc.scalar.activation(
                                    out=stg.ap[:, e, :], in_=pss[e].ap, func=AF.Copy,
                                    scale=(1.0 / math.sqrt(P)) if kind == "q" else 1.0), reads=[pss[e]], writes=[stg])
                            else:
                                self.op("dve", lambda e=e, stg=stg, pss=pss, kind=kind: nc.vector.tensor_scalar(
                                    out=stg.ap[:, e, :], in0=pss[e].ap, scalar1=(1.0 / math.sqrt(P)) if kind == "q" else 1.0,
                                    scalar2=None, op0=ALU.mult), reads=[pss[e]], writes=[stg])
                        if kind == "q":
                            dst = self.QT_d[g * NB:(g + 1) * NB, :, t0:t0 + TT]
                        else:
                            dst = self.KT_d[g * NB:(g + 1) * NB, :, tg:tg + TT]
                        self.dma("pool", [lambda stg=stg, dst=dst: nc.gpsimd.dma_start(out=dst.rearrange("h p t -> p h t"), in_=stg.ap)],
                                 stg.dsem, reads=[stg])
                    else:
                        stg = qst[qi % 2]
                        qi += 1
                        for b in range(NB):
                            if b % 2 == 0:
                                self.op("act", lambda b=b, stg=stg, pss=pss: nc.scalar.copy(out=stg.ap[:, b, :], in_=pss[b].ap),
                                        reads=[pss[b]], writes=[stg])
                            else:
                                self.op("dve", lambda b=b, stg=stg, pss=pss: nc.vector.tensor_copy(out=stg.ap[:, b, :], in_=pss[b].ap),
                                        reads=[pss[b]], writes=[stg])
                        dst = self.V_d[tg:tg + TT, g * TT:(g + 1) * TT].rearrange("(b p) n -> p b n", p=P)
                        self.dma("pool", [lambda stg=stg, dst=dst: nc.gpsimd.dma_start(out=dst, in_=stg.ap)], stg.dsem, reads=[stg])
                self.flush_deferred()
                if own:
                    dst = self.YT_d[0:NHA, :, t0:t0 + TT].rearrange("c p t -> p c t")
                    self.dma("pool", [lambda dst=dst: nc.gpsimd.dma_start(out=dst, in_=yaT.ap)], yaT.dsem, reads=[yaT])

    def stage12_end(self):
        self.cv_meter(len(self.cvq_in) + len(self.cvq), [self.cvq_in, self.cvq])
        self.cv_rate = 0

    def gelu(self, ps, dstbuf, dst_ap, tmpf, C2):
        nc = self.nc
        self.op("act", lambda: nc.scalar.activation(out=dst_ap, in_=ps.ap, func=AF.Gelu_apprx_tanh), reads=[ps], writes=[dstbuf])

    def defer(self, k, fn):
        self.deferred.append([k, fn])

    def tick(self):
        due = []
        for item in self.deferred:
            item[0] -= 1
            if item[0] <= 0:
                due.append(item)
        self.deferred = [it for it in self.deferred if it[0] > 0]
        for it in due:
            it[1]()

    def flush_deferred(self):
        while self.deferred:
            self.tick()

    def sgu_A(self, gv, g, b, tmpf, vnb, C2):
        nc = self.nc
        sq = tmpf[3]
        st = self.stat[self.stat_i % len(self.stat)]
        self.stat_i += 1
        HG = TT // P
        gv3 = gv.ap.rearrange("p (h d) -> p h d", h=HG)
        sq3 = sq.ap.rearrange("p (h d) -> p h d", h=HG)
        self.op("dve", lambda: nc.vector.tensor_reduce(out=st.ap[:, 0:HG], in_=gv3, axis=AX.X, op=ALU.add), reads=[gv], writes=[st])
        self.op("pool", lambda: nc.gpsimd.tensor_tensor(out=sq.ap, in0=gv.ap, in1=gv.ap, op=ALU.mult), reads=[gv], writes=[sq])
        self.op("dve", lambda: nc.vector.tensor_reduce(out=st.ap[:, 4:4 + HG], in_=sq3, axis=AX.X, op=ALU.add), reads=[sq], writes=[st])
        self.op("dve", lambda: nc.vector.tensor_scalar(out=st.ap[:, 0:HG], in0=st.ap[:, 0:HG], scalar1=1.0 / P, scalar2=None,
                                                       op0=ALU.mult), reads=[st], writes=[st])
        st2 = self.stat[self.stat_i % len(self.stat)]
        self.stat_i += 1
        self.op("dve", lambda: nc.vector.tensor_tensor(out=st2.ap[:, 0:HG], in0=st.ap[:, 0:HG], in1=st.ap[:, 0:HG], op=ALU.mult),
                reads=[st], writes=[st2])
        self.op("dve", lambda: nc.vector.scalar_tensor_tensor(out=st.ap[:, 4:4 + HG], in0=st.ap[:, 4:4 + HG], scalar=1.0 / P,
                                                              in1=st2.ap[:, 0:HG], op0=ALU.mult, op1=ALU.subtract),
                reads=[st, st2], writes=[st])
        self.op("dve", lambda: nc.vector.tensor_scalar(out=st.ap[:, 4:4 + HG], in0=st.ap[:, 4:4 + HG], scalar1=EPS, scalar2=None,
                                                       op0=ALU.add), reads=[st], writes=[st])
        self.rsqrt(st, st.ap[:, 4:4 + HG], st.ap[:, 4:4 + HG])
        for h in range(HG):
            self.op("dve", lambda h=h: nc.vector.tensor_scalar(out=vnb.ap[:, h * P:(h + 1) * P], in0=gv3[:, h, :],
                                                               scalar1=st.ap[:, h:h + 1], scalar2=st.ap[:, 4 + h:5 + h],
                                                               op0=ALU.subtract, op1=ALU.mult), reads=[gv, st], writes=[vnb])

    def sgu_B(self, g, b, gu, tmpf, vnb, wsm, ga, yn):
        nc = self.nc
        sq, ya = tmpf[3], tmpf[4]
        HG = TT // P
        sq3 = sq.ap.rearrange("p (h d) -> p h d", h=HG)
        pm = self.psum_aux()

        def mix():
            for h in range(HG):
                ins = nc.tensor.matmul(pm.ap[:, h * P:(h + 1) * P], lhsT=wsm.ap[:, g * HG + h, :], rhs=vnb.ap[:, h * P:(h + 1) * P],
                                       start=True, stop=True)
            return ins
        self.op("pe", mix, reads=[wsm, vnb], writes=[pm])
        for h in range(HG):
            hh = g * HG + h
            self.op("dve", lambda h=h, hh=hh: nc.vector.scalar_tensor_tensor(
                out=ya.ap[:, h * P:(h + 1) * P], in0=pm.ap[:, h * P:(h + 1) * P], scalar=self.bsT[:, hh:hh + 1],
                in1=gu.ap[:, b, h * P:(h + 1) * P], op0=ALU.add, op1=ALU.mult), reads=[pm, gu, self.cst], writes=[ya])
        st3 = self.stat[self.stat_i % len(self.stat)]
        self.stat_i += 1
        self.op("pool", lambda: nc.gpsimd.tensor_tensor(out=sq.ap, in0=ya.ap, in1=ya.ap, op=ALU.mult), reads=[ya], writes=[sq])
        self.op("dve", lambda: nc.vector.tensor_reduce(out=st3.ap[:, 0:HG], in_=sq3, axis=AX.X, op=ALU.add), reads=[sq], writes=[st3])
        self.op("dve", lambda: nc.vector.tensor_scalar(out=st3.ap[:, 0:HG], in0=st3.ap[:, 0:HG], scalar1=1.0 / P, scalar2=EPS,
                                                       op0=ALU.mult, op1=ALU.add), reads=[st3], writes=[st3])
        self.rsqrt(st3, st3.ap[:, 0:HG], st3.ap[:, 0:HG])
        for h in range(HG):
            hh = g * HG + h
            self.op("dve", lambda h=h, hh=hh: nc.vector.scalar_tensor_tensor(
                out=yn.ap[:, h * P:(h + 1) * P], in0=ya.ap[:, h * P:(h + 1) * P], scalar=st3.ap[:, h:h + 1],
                in1=ga.ap[:, hh * P:(hh + 1) * P], op0=ALU.mult, op1=ALU.mult), reads=[ya, st3, ga], writes=[yn])

    def stage3(self):
        nc, cfg = self.nc, self.cfg
        T, TP, BW = cfg["T"], cfg["TP"], cfg["BW"]
        NHB = BW // P
        NHA = cfg["AW"] // P
        TK = TP + T
        NKB = TK // P
        nqt = T // TT
        NS = 2 if NHB % 2 == 0 else 1
        with ExitStack() as st:
            sb = lambda n, s, d: self.sb(n, s, d, st)
            S = []
            for x in range(NS):
                d = dict(
                    KTh=[Buf(sb("KTh%d_%d" % (x, i), [P, TK], BF16), self.newsem("d_KTh%d_%d" % (x, i))) for i in range(2)],
                    Vh=[Buf(sb("Vh%d_%d" % (x, i), [P, NKB, P], BF16), self.newsem("d_Vh%d_%d" % (x, i))) for i in range(2)],
                    QTh=[Buf(sb("QTh%d_%d" % (x, i), [P, T], BF16), self.newsem("d_QTh%d_%d" % (x, i))) for i in range(2)],
                    Eb=[Buf(sb("E%d_%d" % (x, i), [P, TT], F32)) for i in range(3)],
                    Lb=[Buf(sb("L%d_%d" % (x, i), [P, TT], BF16)) for i in range(3)],
                    Gb=[Buf(sb("G%d_%d" % (x, i), [P, TT], F32)) for i in range(2)],
                    Ab=[Buf(sb("A%d_%d" % (x, i), [P, TT], BF16)) for i in range(2)],
                    osb=Buf(sb("osb%d" % x, [P, TT], F32)), osq=Buf(sb("osq%d" % x, [P, TT], F32)), rsd=Buf(sb("rsd%d" % x, [P, TT], F32)),
                    ybs=[Buf(sb("ybs%d_%d" % (x, i), [P, TT], BF16), self.newsem("d_ybs%d_%d" % (x, i))) for i in range(2)],
                    zps=[self.ps[4 * x], self.ps[4 * x + 1]], cp=self.ps[4 * x + 2], op=self.ps[4 * x + 3],
                    heads=list(range(x, NHB, NS)), loaded=set(), its=[], yi=0)
                for hi, h in enumerate(d["heads"]):
                    for qt in range(nqt):
                        kbs = list(range(TP // P + (qt + 1) * NB - 1, -1, -1))
                        for n, kb in enumerate(kbs):
                            j = kb - (TP // P + qt * NB)
                            d["its"].append(dict(h=h, hi=hi, qt=qt, kb=kb, first=(n == 0), last=(n == len(kbs) - 1),
                                                 j=j if j >= 0 else None))
                S.append(d)
            NI = len(S[0]["its"])

            def load_head(d, hi):
                h = d["heads"][hi]
                i = hi % 2
                K_, V_, Q_ = d["KTh"][i], d["Vh"][i], d["QTh"][i]
                self.dma("sp", [lambda: nc.sync.dma_start(out=K_.ap, in_=self.KT_d[h])], K_.dsem, writes=[K_])
                self.dma("sp", [lambda: nc.sync.dma_start(out=V_.ap, in_=self.V_d[:, h * P:(h + 1) * P].rearrange("(n p) d -> p n d", p=P))],
                         V_.dsem, writes=[V_])
                self.dma("sp", [lambda: nc.sync.dma_start(out=Q_.ap, in_=self.QT_d[h])], Q_.dsem, writes=[Q_])
                d["loaded"].add(hi)

            def zT(d, i):
                it = d["its"][i]
                hi, qt, kb = it["hi"], it["qt"], it["kb"]
                if it["first"] and qt == 0 and (hi + 1) < len(d["heads"]) and (hi + 1) not in d["loaded"]:
                    load_head(d, hi + 1)
                z = d["zps"][i % 2]
                k_, q_ = d["KTh"][hi % 2], d["QTh"][hi % 2]
                self.op("pe", lambda: nc.tensor.matmul(z.ap, lhsT=k_.ap[:, kb * P:(kb + 1) * P], rhs=q_.ap[:, qt * TT:(qt + 1) * TT],
                                                       start=True, stop=True), reads=[k_, q_], writes=[z])

            def E(d, i):
                z, e = d["zps"][i % 2], d["Eb"][i % 3]
                self.op("act", lambda: nc.scalar.activation(out=e.ap, in_=z.ap, func=AF.Exp), reads=[z], writes=[e])

            def L(d, i):
                it = d["its"][i]
                e, l = d["Eb"][i % 3], d["Lb"][i % 3]
                self.op("act", lambda: nc.scalar.activation(out=l.ap, in_=e.ap, func=AF.Ln, bias=1.0), reads=[e], writes=[l])
                if it["j"] is not None:
                    j = it["j"]
                    self.op("dve", lambda: nc.vector.tensor_tensor(out=l.ap, in0=l.ap, in1=self.dmask[:, j, :], op=ALU.mult),
                            reads=[l, self.cbf], writes=[l])

            def Tinc(d, i):
                it = d["its"][i]
                cp, l = d["cp"], d["Lb"][i % 3]
                self.op("pe", lambda: nc.tensor.matmul(cp.ap, lhsT=self.tinc, rhs=l.ap, start=it["first"], stop=True,
                                                       skip_group_check=True), reads=[l, self.cbf], writes=[cp])

            def G(d, i):
                cp, g = d["cp"], d["Gb"][i % 2]
                self.op("act", lambda: nc.scalar.activation(out=g.ap, in_=cp.ap, func=AF.Exp, scale=-1.0), reads=[cp], writes=[g])

            def Tcomp(d, i):
                it = d["its"][i]
                if it["last"]:
                    return
                cp, l = d["cp"], d["Lb"][i % 3]
                self.op("pe", lambda: nc.tensor.matmul(cp.ap, lhsT=self.tcomp, rhs=l.ap, start=False, stop=True,
                                                       skip_group_check=True), reads=[l, self.cbf], writes=[cp])

            def A(d, i):
                it = d["its"][i]
                e, g, a = d["Eb"][i % 3], d["Gb"][i % 2], d["Ab"][i % 2]
                if it["j"] is not None:
                    j = it["j"]
                    self.op("dve", lambda: nc.vector.tensor_tensor(out=g.ap, in0=g.ap, in1=self.dmask[:, j, :], op=ALU.mult),
                            reads=[g, self.cbf], writes=[g])
                self.op("dve", lambda: nc.vector.tensor_tensor(out=a.ap, in0=e.ap, in1=g.ap, op=ALU.mult), reads=[e, g], writes=[a])

            def AV(d, i):
                it = d["its"][i]
                h, hi, qt, kb = it["h"], it["hi"], it["qt"], it["kb"]
                a, v_ = d["Ab"][i % 2], d["Vh"][hi % 2]
                op_ = d["op"]
                self.op("pe", lambda: nc.tensor.matmul(op_.ap, lhsT=v_.ap[:, kb, :], rhs=a.ap, start=it["first"], stop=it["last"]),
                        reads=[v_, a], writes=[op_])
                if it["last"]:
                    post(d, h, qt)

            def post(d, h, qt):
                op_, osb, osq, rsd = d["op"], d["osb"], d["osq"], d["rsd"]
                yb = d["ybs"][d["yi"] % 2]
                d["yi"] += 1
                self.op("act", lambda: nc.scalar.copy(out=osb.ap, in_=op_.ap), reads=[op_], writes=[osb])
                self.op("act", lambda: nc.scalar.activation(out=osq.ap, in_=op_.ap, func=AF.Square), reads=[op_], writes=[osq])
                self.op("pe", lambda: nc.tensor.matmul(op_.ap, lhsT=self.onesf, rhs=osq.ap, start=True, stop=True),
                        reads=[osq, self.cst], writes=[op_])
                self.op("dve", lambda: nc.vector.tensor_scalar(out=rsd.ap, in0=op_.ap, scalar1=1.0 / P, scalar2=EPS,
                                                               op0=ALU.mult, op1=ALU.add), reads=[op_], writes=[rsd])
                self.op("act", lambda: nc.scalar.activation(out=rsd.ap, in_=rsd.ap, func=AF.Ln), reads=[rsd], writes=[rsd])
                self.op("act", lambda: nc.scalar.activation(out=rsd.ap, in_=rsd.ap, func=AF.Exp, scale=-0.5), reads=[rsd], writes=[rsd])
                self.op("dve", lambda: nc.vector.scalar_tensor_tensor(out=yb.ap, in0=osb.ap, scalar=self.g_b[:, h:h + 1], in1=rsd.ap,
                                                                      op0=ALU.mult, op1=ALU.mult), reads=[osb, rsd, self.cst], writes=[yb])
                dst = self.YT_d[NHA + h, :, qt * TT:(qt + 1) * TT]
                self.dma("sp", [lambda: nc.sync.dma_start(out=dst, in_=yb.ap)], yb.dsem, reads=[yb])

            for d in S:
                load_head(d, 0)
            ncv = len(self.cvq3)
            cv_every = max(1, (NI * 9 // 10) // max(1, ncv))
            for d in S:
                zT(d, 0)
            for d in S:
                E(d, 0)
            for d in S:
                L(d, 0)
            if NI > 1:
                for d in S:
                    zT(d, 1)
            for d in S:
                Tinc(d, 0)
            for s_ in range(NI):
                if s_ % cv_every == 0:
                    self.cv_meter(1, [self.cvq3])
                if s_ + 2 < NI:
                    for d in S:
                        zT(d, s_ + 2)
                if s_ + 1 < NI:
                    for d in S:
                        E(d, s_ + 1)
                for d in S:
                    G(d, s_)
                if s_ + 1 < NI:
                    for d in S:
                        L(d, s_ + 1)
                for d in S:
                    Tcomp(d, s_)
                if s_ + 1 < NI:
                    for d in S:
                        Tinc(d, s_ + 1)
                for d in S:
                    A(d, s_)
                for d in S:
                    AV(d, s_)
            self.cv_meter(len(self.cvq3), [self.cvq3])

    def stage4(self):
        nc, cfg = self.nc, self.cfg
        T, D, DFF, PLE, KC, NQ = cfg["T"], cfg["D"], cfg["DFF"], cfg["PLE"], cfg["KC"], cfg["NQ"]
        NCH = self.NCH
        nkt = NCH // KC
        FQ = DFF // NQ
        FQC = FQ // P
        nkq = FQC // KC
        NE = D // TT
        NPC = PLE // P
        ntile = T // TT
        with ExitStack() as st:
            sb = lambda n, s, d: self.sb(n, s, d, st)
            hb = [Buf(sb("h%d" % b, [P, D], F32), self.newsem("d_h%d" % b)) for b in range(NB)]
            actT = Buf(sb("actT", [P, NCH, TT], BF16), self.newsem("d_actT"))
            hidT = Buf(sb("hidT", [P, FQC, TT], BF16))
            xq = [Buf(sb("xq%d" % i, [P, D // 4], F32)) for i in range(4)]
            gfin = Buf(sb("gfin", [P, D], F32), self.newsem("d_gfin"))
            pt = Buf(sb("pt", [P, PLE], F32), self.newsem("d_pt"))
            pT = Buf(sb("pT", [P, NPC, TT], BF16))
            tf = [Buf(sb("t4_%d" % i, [P, TT], F32)) for i in range(2)]
            self.dma("sp", [lambda: nc.sync.dma_start(out=gfin.ap, in_=self.gfin_d)], gfin.dsem, writes=[gfin])
            plan = []
            for ti in range(ntile):
                d = {}
                d["out"] = [[self.wq_add("w_out", k * KC * P, KC * P, e * TT, TT) for k in range(nkt)] for e in range(NE)]
                d["ffn"] = []
                for q in range(NQ):
                    ups = [[self.wq_add("w_up", k * KC * P, KC * P, q * FQ + fg * TT, TT) for k in range(nkt)] for fg in range(FQ // TT)]
                    dns = [[self.wq_add("w_down", q * FQ + k * KC * P, KC * P, e * TT, TT) for k in range(nkq)] for e in range(NE)]
                    d["ffn"].append((ups, dns))
                d["gate"] = [([self.wq_add("w_gate", k * KC * P, KC * P, e * TT, TT) for k in range(nkt)],
                              self.wq_add("w_ple", 0, PLE, e * TT, TT)) for e in range(NE)]
                plan.append(d)

            def load_yT(ti_):
                half = max(1, NCH // 2)
                self.dma("sp", [lambda c0=c0: nc.sync.dma_start(out=actT.ap[:, c0:c0 + half, :],
                                                                in_=self.YT_d[c0:c0 + half, :, ti_ * TT:(ti_ + 1) * TT].rearrange("c p t -> p c t"))
                                for c0 in range(0, NCH, half)], actT.dsem, writes=[actT])

            ei = 0
            for ti in range(ntile):
                t0 = ti * TT
                d = plan[ti]
                for b in range(NB):
                    self.dma("sp", [lambda b=b: nc.sync.dma_start(out=hb[b].ap, in_=self.x_own[t0 + b * P:t0 + (b + 1) * P, :])],
                             hb[b].dsem, writes=[hb[b]])
                if ti == 0:
                    load_yT(0)
                for e in range(NE):
                    pss = self.proj(d["out"][e], actT, NCH, "tok")
                    for b in range(NB):
                        self.op("dve", lambda b=b, e=e, pss=pss: nc.vector.tensor_tensor(
                            out=hb[b].ap[:, e * TT:(e + 1) * TT], in0=hb[b].ap[:, e * TT:(e + 1) * TT], in1=pss[b].ap, op=ALU.add),
                            reads=[pss[b], hb[b]], writes=[hb[b]])
                jflat = hidT.ap.rearrange("p c t -> p (c t)")[:, 0:D // 2]
                sts = [self.norm_stats(hb[b], [hidT, hidT], [jflat, jflat]) for b in range(NB)]
                self.norm_blocks_q(hb, self.g_ffn, actT, xq, sts)
                for q in range(NQ):
                    ups, dns = d["ffn"][q]
                    for fg, widxs in enumerate(ups):
                        pss = self.proj(widxs, actT, NCH, "feat")
                        for fc in range(NB):
                            t = tf[ei % 2]
                            ei += 1
                            self.op("act", lambda t=t, fc=fc, pss=pss: nc.scalar.activation(out=t.ap, in_=pss[fc].ap, func=AF.Relu),
                                    reads=[pss[fc]], writes=[t])
                            o = hidT.ap[:, fg * NB + fc, :]
                            eng = "dve" if (fc % 2 == 0) else "pool"
                            if eng == "dve":
                                self.op("dve", lambda t=t, o=o: nc.vector.tensor_tensor(out=o, in0=t.ap, in1=t.ap, op=ALU.mult),
                                        reads=[t], writes=[hidT])
                            else:
                                self.op("pool", lambda t=t, o=o: nc.gpsimd.tensor_tensor(out=o, in0=t.ap, in1=t.ap, op=ALU.mult),
                                        reads=[t], writes=[hidT])
                    for e, widxs in enumerate(dns):
                        pss = self.proj(widxs, hidT, FQC, "tok")
                        for b in range(NB):
                            self.op("dve", lambda b=b, e=e, pss=pss: nc.vector.tensor_tensor(
                                out=hb[b].ap[:, e * TT:(e + 1) * TT], in0=hb[b].ap[:, e * TT:(e + 1) * TT], in1=pss[b].ap, op=ALU.add),
                                reads=[pss[b], hb[b]], writes=[hb[b]])
                sts = [self.norm_stats(hb[b], [hidT, hidT], [jflat, jflat]) for b in range(NB)]
                self.norm_blocks_q(hb, self.g_ple, actT, xq, sts)
                for b in range(NB):
                    self.dma("sp", [lambda b=b: nc.sync.dma_start(out=pt.ap, in_=self.p_own[t0 + b * P:t0 + (b + 1) * P, :])],
                             pt.dsem, writes=[pt])
                    self.transpose_block(pt, pt.ap, NPC, pT, b, None)
                for e in range(NE):
                    pss = self.proj(d["gate"][e][0], actT, NCH, "tok")
                    wple = self.wq_get(d["gate"][e][1])
                    for b in range(NB):
                        pe_ = self.psum_aux()

                        def mm(b=b, e=e, pe_=pe_, wple=wple):
                            for k in range(NPC):
                                ins = nc.tensor.matmul(pe_.ap, lhsT=pT.ap[:, k, b * P:(b + 1) * P], rhs=wple.ap[:, k, :],
                                                       start=(k == 0), stop=(k == NPC - 1))
                            return ins
                        self.op("pe", mm, reads=[pT, wple], writes=[pe_])
                        t = tf[ei % 2]
                        ei += 1
                        self.op("act", lambda t=t, b=b, pss=pss: nc.scalar.activation(out=t.ap, in_=pss[b].ap, func=AF.Sigmoid),
                                reads=[pss[b]], writes=[t])
                        self.op("dve", lambda t=t, pe_=pe_: nc.vector.tensor_tensor(out=t.ap, in0=t.ap, in1=pe_.ap, op=ALU.mult),
                                reads=[t, pe_], writes=[t])
                        self.op("pool", lambda t=t, b=b, e=e: nc.gpsimd.tensor_tensor(
                            out=hb[b].ap[:, e * TT:(e + 1) * TT], in0=hb[b].ap[:, e * TT:(e + 1) * TT], in1=t.ap, op=ALU.add),
                            reads=[t, hb[b]], writes=[hb[b]])
                if ti + 1 < ntile:
                    load_yT(ti + 1)
                for b in range(NB):
                    stt = self.stat[self.stat_i % len(self.stat)]
                    self.stat_i += 1
                    H2 = D // 2
                    self.op("dve", lambda stt=stt: nc.vector.memset(stt.ap[:, 0:8], 0.0), writes=[stt])
                    for hf in range(2):
                        self.op("act", lambda stt=stt, b=b, hf=hf: nc.scalar.activation(
                            out=jflat, in_=hb[b].ap[:, hf * H2:(hf + 1) * H2], func=AF.Square,
                            accum_out=stt.ap[:, 4 + hf:5 + hf]), reads=[hb[b]], writes=[hidT, stt])
                    self.op("dve", lambda stt=stt: nc.vector.tensor_tensor(out=stt.ap[:, 0:1], in0=stt.ap[:, 4:5], in1=stt.ap[:, 5:6],
                                                                           op=ALU.add), reads=[stt], writes=[stt])
                    self.op("dve", lambda stt=stt: nc.vector.tensor_scalar(out=stt.ap[:, 1:2], in0=stt.ap[:, 0:1], scalar1=1.0 / D,
                                                                           scalar2=EPS, op0=ALU.mult, op1=ALU.add), reads=[stt], writes=[stt])
                    self.rsqrt(stt, stt.ap[:, 2:3], stt.ap[:, 1:2])
                    self.op("dve", lambda stt=stt, b=b: nc.vector.scalar_tensor_tensor(
                        out=hb[b].ap, in0=hb[b].ap, scalar=stt.ap[:, 2:3], in1=gfin.ap, op0=ALU.mult, op1=ALU.mult),
                        reads=[hb[b], stt, gfin], writes=[hb[b]])
                    self.dma("pool", [lambda b=b: nc.gpsimd.dma_start(out=self.out_d[t0 + b * P:t0 + (b + 1) * P, :], in_=hb[b].ap)],
                             hb[b].dsem, reads=[hb[b]])


def make_consts(cfg, norm_mix, norm_ffn, norm_ple, norm_b_out, b_s, norm_a_out, norm_final, w_s):
    D, AW, BW = cfg["D"], cfg["AW"], cfg["BW"]
    NCH, NHA, NHB = D // P, AW // P, BW // P
    ar = np.arange(P)
    ident = np.eye(P, dtype=np.float32)
    ones = np.ones((P, P), np.float32)
    trilT = (ar[None, :] >= ar[:, None]).astype(np.float32)
    fm = lambda g, n: np.ascontiguousarray(np.asarray(g, np.float32).reshape(n, P).T)
    cst = np.concatenate([ident, ones, trilT, fm(norm_mix, NCH), fm(norm_ffn, NCH), fm(norm_ple, NCH),
                          fm(norm_b_out, NHB), np.ascontiguousarray(np.asarray(b_s, np.float32).T)], axis=1)
    tinc = (ar[:, None] >= ar[None, :]).astype(np.float32)
    tcomp = 1.0 - tinc
    q = np.arange(TT)
    dm = np.stack([((j * P + ar[:, None]) < q[None, :]).astype(np.float32) for j in range(NB)], axis=1)
    cbf = np.concatenate([tinc, tcomp, dm.reshape(P, NB * TT)], axis=1).astype(ml_dtypes.bfloat16)
    ga_rep = np.ascontiguousarray(np.broadcast_to(np.asarray(norm_a_out, np.float32)[None, :], (P, AW)))
    gfin_rep = np.ascontiguousarray(np.broadcast_to(np.asarray(norm_final, np.float32)[None, :], (P, D)))
    wsT = np.ascontiguousarray(np.transpose(np.asarray(w_s, np.float32), (2, 0, 1)).reshape(P, NHA * P))
    return dict(cst=np.ascontiguousarray(cst), cbf=np.ascontiguousarray(cbf), ga_rep=ga_rep, gfin_rep=gfin_rep, wsT=wsT)


def kernel(x, p, norm_mix, w_in, w_s, b_s, norm_a_out, norm_b_out, w_out, norm_ffn, w_up, w_down,
           norm_ple, w_ple_gate, w_ple_proj, norm_final):
    cfg = FULL
    T, TP, D = cfg["T"], cfg["TP"], cfg["D"]
    x = np.asarray(x, np.float32)
    p = np.asarray(p, np.float32)
    B, S, _ = x.shape
    consts = make_consts(cfg, norm_mix[0], norm_ffn[0], norm_ple[0], norm_b_out[0], b_s[0], norm_a_out[0], norm_final, w_s[0])
    shared = dict(w_in=np.ascontiguousarray(w_in[0], dtype=np.float32), w_out=np.ascontiguousarray(w_out[0], dtype=np.float32),
                  w_up=np.ascontiguousarray(w_up[0], dtype=np.float32), w_down=np.ascontiguousarray(w_down[0], dtype=np.float32),
                  w_gate=np.ascontiguousarray(w_ple_gate[0], dtype=np.float32),
                  w_ple=np.ascontiguousarray(w_ple_proj[0], dtype=np.float32), **consts)
    zeros = np.zeros((TP, D), np.float32)
    in_maps = []
    for core in range(8):
        b, s = core // 2, core % 2
        m = dict(shared)
        m["x_own"] = np.ascontiguousarray(x[b, s * T:(s + 1) * T])
        m["x_pre"] = np.ascontiguousarray(x[b, 0:TP]) if s == 1 else zeros
        m["p_own"] = np.ascontiguousarray(p[0, b, s * T:(s + 1) * T])
        in_maps.append(m)
    nc = K(cfg).build()
    res = run_bass_kernel_spmd(nc, in_maps, core_ids=list(range(8)))
    out = np.empty((B, S, D), np.float32)
    for core in range(8):
        b, s = core // 2, core % 2
        out[b, s * T:(s + 1) * T] = np.asarray(res.results[core]["out"], np.float32)
    return out
```

```python
import math
from contextlib import ExitStack

import numpy as np
import ml_dtypes

import concourse.bass as bass
import concourse.mybir as mybir
from concourse.bass_utils import run_bass_kernel_spmd

F32 = mybir.dt.float32
BF16 = mybir.dt.bfloat16
AF = mybir.ActivationFunctionType
ALU = mybir.AluOpType
AX = mybir.AxisListType

EPS = 1e-6
P = 128
TT = 512
NB = TT // P

FULL = dict(T=2048, TP=2048, D=4096, AW=2048, BW=2048, DFF=16384, PLE=256, KC=16, NQ=8)


class Sem:
    _id = 0

    def __init__(self, nc, name):
        self.h = nc.alloc_semaphore(name)
        self.n = 0
        Sem._id += 1
        self.id = Sem._id


class Buf:
    def __init__(self, ap=None, dsem=None):
        self.ap = ap
        self.w = None
        self.r = {}
        self.dsem = dsem


class K:
    def __init__(self, cfg):
        self.cfg = cfg
        self.nc = bass.Bass("TRN2", target_bir_lowering=False)
        nc = self.nc
        self.eng = {"pe": nc.tensor, "act": nc.scalar, "dve": nc.vector, "pool": nc.gpsimd, "sp": nc.sync}
        self.prog = {e: Sem(nc, "p_" + e) for e in ["pe", "act", "dve", "pool"]}
        self.deferred = []
        self.waited = {e: {} for e in self.eng}
        self.allsems = list(self.prog.values())
        self.ps_i = 0
        self.cv_rate = 0
        self.cv_hist = []
        self.cv_boost = [0, 0]
        self.cv_queues = []
        self.pp_i = 0
        self.pa_i = 0

    def newsem(self, name):
        s = Sem(self.nc, name)
        self.allsems.append(s)
        return s

    def _wait(self, e, deps, own_ok):
        for (s, v) in deps:
            if own_ok and e in self.prog and s is self.prog[e]:
                continue
            if self.waited[e].get(s.id, 0) >= v:
                continue
            self.eng[e].wait_ge(s.h, v)
            self.waited[e][s.id] = v

    def _deps(self, reads, writes):
        deps = []
        for b in reads:
            if b.w:
                deps.append(b.w)
        for b in writes:
            if b.w:
                deps.append(b.w)
            deps.extend(b.r.values())
        return deps

    def _mark(self, ev, reads, writes):
        for b in reads:
            b.r[ev[0].id] = ev
        for b in writes:
            b.w = ev
            b.r = {}

    def op(self, e, fn, reads=(), writes=()):
        self._wait(e, self._deps(reads, writes), e == "pe")
        ins = fn()
        s = self.prog[e]
        s.n += 1
        ins.then_inc(s.h, 1)
        self._mark((s, s.n), reads, writes)
        return ins

    def dma(self, q, fns, sem, reads=(), writes=()):
        self._wait(q, self._deps(reads, writes), False)
        for fn in fns:
            ins = fn()
            ins.then_inc(sem.h, 16)
            sem.n += 16
        self._mark((sem, sem.n), reads, writes)

    def barrier(self):
        evs = [(s, s.n) for s in self.allsems if s.n > 0]
        for e in self.eng:
            self._wait(e, evs, False)

    def cv_meter(self, k, queues):
        todo = []
        for q in queues:
            while q and len(todo) < k:
                todo.append(q.pop(0))
        if not todo:
            return
        self._wait("pool", [(self.prog["pe"], self.prog["pe"].n)], False)
        for (fns, s, b) in todo:
            if len(self.cv_hist) >= 3:
                self._wait("pool", [self.cv_hist[-3]], False)
            self.dma("pool", fns, s, writes=[b])
            self.cv_hist.append((s, s.n))

    def cv_emit(self, k):
        for _ in range(min(k, len(self.cvq))):
            fns, s, b = self.cvq.pop(0)
            self.dma("pool", fns, s, writes=[b])

    def ensure_conv(self, bf):
        for q in (self.cvq_in, self.cvq, self.cvq3):
            for item in [it for it in q if it[2] is bf]:
                q.remove(item)
                self.dma("pool", item[0], item[1], writes=[item[2]])
        assert bf.w is not None

    def psum(self):
        b = self.ps[self.ps_i % 8]
        self.ps_i += 1
        return b

    def psum_proj(self):
        b = self.ps[self.pp_i % 6]
        self.pp_i += 1
        return b

    def psum_aux(self):
        b = self.ps[6 + self.pa_i % 2]
        self.pa_i += 1
        return b

    def rsqrt(self, buf, out_ap, in_ap):
        nc = self.nc
        self.op("act", lambda: nc.scalar.activation(out=out_ap, in_=in_ap, func=AF.Sqrt), reads=[buf], writes=[buf])
        self.op("dve", lambda: nc.vector.reciprocal(out=out_ap, in_=out_ap), reads=[buf], writes=[buf])

    def build(self):
        nc, cfg = self.nc, self.cfg
        T, TP, D, AW, BW, DFF, PLE, KC, NQ = (cfg[k] for k in ["T", "TP", "D", "AW", "BW", "DFF", "PLE", "KC", "NQ"])
        self.NCH = NCH = D // P
        NHA, NHB = AW // P, BW // P
        DIN = 2 * AW + 3 * BW
        TK = TP + T
        self.NCS = NCS = 3 * P + 3 * NCH + NHB + NHA
        dt = nc.dram_tensor
        self.x_own = dt("x_own", [T, D], F32, kind="ExternalInput").ap()
        self.x_pre = dt("x_pre", [TP, D], F32, kind="ExternalInput").ap()
        self.p_own = dt("p_own", [T, PLE], F32, kind="ExternalInput").ap()
        wnames = dict(w_in=(D, DIN), w_out=(D, D), w_up=(D, DFF), w_down=(DFF, D), w_gate=(D, D), w_ple=(PLE, D))
        self.wf = {n: dt(n, list(s), F32, kind="ExternalInput").ap() for n, s in wnames.items()}
        self.cst_d = dt("cst", [P, NCS], F32, kind="ExternalInput").ap()
        self.cbf_d = dt("cbf", [P, 2 * P + NB * TT], BF16, kind="ExternalInput").ap()
        self.ga_d = dt("ga_rep", [P, AW], F32, kind="ExternalInput").ap()
        self.gfin_d = dt("gfin_rep", [P, D], F32, kind="ExternalInput").ap()
        self.wsT_d = dt("wsT", [P, NHA * P], F32, kind="ExternalInput").ap()
        self.out_d = dt("out", [T, D], F32, kind="ExternalOutput").ap()
        self.wb = {n: dt("b_" + n, [s[1] // TT, P, s[0] // P, TT], BF16, kind="Internal").ap() for n, s in wnames.items()}
        sk = "ExternalOutput" if cfg.get("debug") else "Internal"
        self.KT_d = dt("KT_s", [NHB, P, TK], BF16, kind=sk).ap()
        self.V_d = dt("V_s", [TK, BW], BF16, kind=sk).ap()
        self.QT_d = dt("QT_s", [NHB, P, T], BF16, kind=sk).ap()
        self.YT_d = dt("YT_s", [NCH, P, T], BF16, kind=sk).ap()

        self.ps = [Buf(nc.alloc_psum_tensor("ps%d" % i, [P, TT], F32).ap()) for i in range(8)]

        self.wev = {}
        self.cvq = []
        self.cvq_in = []
        kv0 = (2 * AW + BW) // TT
        GA_ = AW // TT
        self.cvq3 = []
        CP = 8
        for n in ["w_in", "w_out", "w_up", "w_down", "w_gate", "w_ple"]:
            rows, cols = wnames[n]
            ng, kch = cols // TT, rows // P
            order = list(range(ng))
            if n == "w_in":
                rest = []
                for g in range(GA_):
                    rest += [g, GA_ + g]
                rest += list(range(2 * GA_, kv0))
                order = list(range(kv0, ng)) + rest
            cls = {}
            for g in order:
                key = g if n == "w_in" else "all"
                if key not in cls:
                    cls[key] = (self.newsem("cv_%s_%s" % (n, key)), Buf())
                s, bf = cls[key]
                self.wev[(n, g)] = bf
                for c0 in range(0, kch, CP):
                    c1 = min(kch, c0 + CP)
                    fn = (lambda n=n, g=g, c0=c0, c1=c1: nc.gpsimd.dma_start(
                        out=self.wb[n][g, :, c0:c1, :],
                        in_=self.wf[n][c0 * P:c1 * P, g * TT:(g + 1) * TT].rearrange("(c p) n -> p c n", p=P)))
                    if n == "w_in" and kv0 <= g < kv0 + (ng - kv0) // 2:
                        self.dma("pool", [fn], s, writes=[bf])
                    elif n == "w_in":
                        self.cvq_in.append(([fn], s, bf))
                    elif n == "w_out" or (n == "w_up" and g < (14 * ng) // 32):
                        self.cvq.append(([fn], s, bf))
                    else:
                        self.cvq3.append(([fn], s, bf))
        self.cv_total = len(self.cvq)

        with ExitStack() as glob:
            def sb(name, shape, dtype, stack=glob):
                return stack.enter_context(nc.sbuf_tensor("s_" + name, shape, dtype)).ap()
            self.sb = sb
            cs = self.newsem("d_cst")
            self.cst = Buf(sb("cst", [P, NCS], F32), cs)
            self.cbf = Buf(sb("cbf", [P, 2 * P + NB * TT], BF16), cs)
            self.dma("sp", [lambda: nc.sync.dma_start(out=self.cst.ap, in_=self.cst_d),
                            lambda: nc.sync.dma_start(out=self.cbf.ap, in_=self.cbf_d)], cs,
                     writes=[self.cst, self.cbf])
            c = self.cst.ap
            self.ident = c[:, 0:P]
            self.onesf = c[:, P:2 * P]
            self.trilT = c[:, 2 * P:3 * P]
            o = 3 * P
            self.g_mix = c[:, o:o + NCH]
            self.g_ffn = c[:, o + NCH:o + 2 * NCH]
            self.g_ple = c[:, o + 2 * NCH:o + 3 * NCH]
            o += 3 * NCH
            self.g_b = c[:, o:o + NHB]
            self.bsT = c[:, o + NHB:o + NHB + NHA]
            cb = self.cbf.ap
            self.tinc = cb[:, 0:P]
            self.tcomp = cb[:, P:2 * P]
            self.dmask = cb[:, 2 * P:].rearrange("p (j q) -> p j q", j=NB)
            self.stat = [Buf(sb("stat%d" % i, [P, 8], F32)) for i in range(8)]
            self.stat_i = 0
            self.wring = [Buf(sb("wr%d" % i, [P, KC, TT], BF16), self.newsem("d_wr%d" % i)) for i in range(3)]
            self.wring_i = 0
            self.wqueue = []
            self.wq_next = 0

            self.stage12()
            self.stage12_end()
            self.barrier()
            self.stage3()
            self.barrier()
            self.stage4()
            self.barrier()
        return nc

    def wq_add(self, name, r0, nrows, c0, ncols):
        self.wqueue.append((name, r0, nrows, c0, ncols))
        return len(self.wqueue) - 1

    def wq_get(self, idx, pf=2):
        nc = self.nc
        last = min(len(self.wqueue) - 1, idx + pf)
        while self.wq_next <= last:
            name, r0, nrows, c0, ncols = self.wqueue[self.wq_next]
            slot = self.wring[self.wq_next % 3]
            kc = nrows // P
            g, cl = c0 // TT, r0 // P
            src = self.wb[name][g, :, cl:cl + kc, :]
            fns = [lambda src=src, slot=slot, kc=kc: nc.sync.dma_start(out=slot.ap[:, 0:kc, :], in_=src)]
            dep = self.wev[(name, g)]
            if dep.w is None:
                self.ensure_conv(dep)
            self.dma("sp", fns, slot.dsem, reads=[dep], writes=[slot])
            self.wq_next += 1
        return self.wring[idx % 3]

    def norm_stats(self, src, junks, junk_aps=None):
        nc = self.nc
        D = self.cfg["D"]
        H = D // 2
        st = self.stat[self.stat_i % len(self.stat)]
        self.stat_i += 1
        self.op("dve", lambda: nc.vector.memset(st.ap[:, 0:2], 0.0), writes=[st])
        for hf in range(2):
            jb = junks[hf]
            j_ap = jb.ap[:, 0:H] if junk_aps is None else junk_aps[hf]
            self.op("act", lambda hf=hf, jb=jb, j_ap=j_ap: nc.scalar.activation(out=j_ap, in_=src.ap[:, hf * H:(hf + 1) * H], func=AF.Square,
                                                                    accum_out=st.ap[:, hf:hf + 1]), reads=[src], writes=[jb, st])
        self.op("dve", lambda: nc.vector.tensor_tensor(out=st.ap[:, 2:3], in0=st.ap[:, 0:1], in1=st.ap[:, 1:2], op=ALU.add),
                reads=[st], writes=[st])
        self.op("dve", lambda: nc.vector.tensor_scalar(out=st.ap[:, 2:3], in0=st.ap[:, 2:3], scalar1=1.0 / D, scalar2=EPS,
                                                       op0=ALU.mult, op1=ALU.add), reads=[st], writes=[st])
        self.rsqrt(st, st.ap[:, 3:4], st.ap[:, 2:3])
        return st

    def norm_scale(self, src, st, hf, dstb, d_ap):
        nc = self.nc
        H = self.cfg["D"] // 2
        s_ap = src.ap[:, hf * H:(hf + 1) * H]
        if hf == 0:
            self.op("dve", lambda: nc.vector.tensor_scalar(out=d_ap, in0=s_ap, scalar1=st.ap[:, 3:4], scalar2=None, op0=ALU.mult),
                    reads=[src, st], writes=[dstb])
        else:
            self.op("act", lambda: nc.scalar.activation(out=d_ap, in_=s_ap, func=AF.Copy, scale=st.ap[:, 3:4]),
                    reads=[src, st], writes=[dstb])

    def norm_pre(self, src):
        H = self.cfg["D"] // 2
        st = self.norm_stats(src, [self.junk, self.junk])
        for hf in range(2):
            self.norm_scale(src, st, hf, src, src.ap[:, hf * H:(hf + 1) * H])

    def norm_tr(self, src, gain, dstT, b):
        H = self.cfg["D"] // 2
        for hf in range(2):
            self.transpose_block(src, src.ap[:, hf * H:(hf + 1) * H], self.NCH // 2, dstT, b, gain, c_off=hf * (self.NCH // 2))

    def norm_block(self, src, gain, dstT, b, tmp=None, st=None):
        H = self.cfg["D"] // 2
        if st is None:
            st = self.norm_stats(src, tmp)
        for hf in range(2):
            self.norm_scale(src, st, hf, tmp[hf], tmp[hf].ap[:, 0:H])
            self.transpose_block(tmp[hf], tmp[hf].ap[:, 0:H], self.NCH // 2, dstT, b, gain, c_off=hf * (self.NCH // 2))
        return st

    def norm_blocks_q(self, hbs, gain, dstT, xq, sts):
        nc = self.nc
        D, NCH = self.cfg["D"], self.NCH
        NQ4 = len(xq)
        Q = D // NQ4
        cq = NCH // NQ4

        def scale(b, q):
            src, st = hbs[b], sts[b]
            s_ap = src.ap[:, q * Q:(q + 1) * Q]
            if q % 2 == 0:
                self.op("dve", lambda: nc.vector.tensor_scalar(out=xq[q].ap, in0=s_ap, scalar1=st.ap[:, 3:4], scalar2=None, op0=ALU.mult),
                        reads=[src, st], writes=[xq[q]])
            else:
                self.op("act", lambda: nc.scalar.activation(out=xq[q].ap, in_=s_ap, func=AF.Copy, scale=st.ap[:, 3:4]),
                        reads=[src, st], writes=[xq[q]])
        for q in range(NQ4):
            scale(0, q)
        for b in range(len(hbs)):
            for q in range(NQ4):
                self.transpose_block(xq[q], xq[q].ap, cq, dstT, b, gain, c_off=q * cq)
                if b + 1 < len(hbs):
                    scale(b + 1, q)

    def transpose_block(self, src, src_ap, nch, dstT, b, gain=None, c_off=0, aux=False):
        nc = self.nc
        k = 0
        for c0 in range(0, nch, 4):
            n = min(4, nch - c0)
            ps = self.psum_aux() if aux else self.psum()
            pv = ps.ap.rearrange("p (j q) -> p j q", j=4)

            def tr(c0=c0, n=n, pv=pv):
                for j in range(n):
                    ins = nc.tensor.transpose(pv[:, j, :], src_ap[:, (c0 + j) * P:(c0 + j + 1) * P], self.ident)
                return ins
            self.op("pe", tr, reads=[src, self.cst], writes=[ps])
            if gain is None:
                e = "act" if (k % 2 == 0) else "dve"
                k += 1
                o = dstT.ap[:, c_off + c0:c_off + c0 + n, b * P:(b + 1) * P]
                if e == "act":
                    self.op("act", lambda o=o, pv=pv, n=n: nc.scalar.copy(out=o, in_=pv[:, 0:n, :]), reads=[ps], writes=[dstT])
                else:
                    self.op("dve", lambda o=o, pv=pv, n=n: nc.vector.tensor_copy(out=o, in_=pv[:, 0:n, :]), reads=[ps], writes=[dstT])
            elif (c0 // 4) % 2 == 0:
                cc0 = c_off + c0
                o = dstT.ap[:, cc0:cc0 + n, b * P:(b + 1) * P]
                gb = gain[:, cc0:cc0 + n].unsqueeze(2).broadcast_to([P, n, P])
                self.op("dve", lambda o=o, pv=pv, n=n, gb=gb: nc.vector.tensor_tensor(out=o, in0=pv[:, 0:n, :], in1=gb, op=ALU.mult),
                        reads=[ps, self.cst], writes=[dstT])
            else:
                for j in range(n):
                    cc = c_off + c0 + j
                    o = dstT.ap[:, cc, b * P:(b + 1) * P]
                    self.op("act", lambda o=o, pv=pv, j=j, cc=cc: nc.scalar.activation(
                        out=o, in_=pv[:, j, :], func=AF.Copy, scale=gain[:, cc:cc + 1]), reads=[ps, self.cst], writes=[dstT])

    def proj(self, tiles, actT, nchunks, mode, ncols=TT, ntok=TT):
        nc = self.nc
        KC = self.cfg["KC"]
        nout = (ntok // P) if mode == "tok" else (ncols // P)
        pss = [self.psum_proj() for _ in range(nout)]
        for ti, widx in enumerate(tiles):
            wt = self.wq_get(widx)
            kc = min(KC, nchunks - ti * KC)
            for o in range(nout):
                def mm(o=o, ti=ti, wt=wt, kc=kc):
                    for k in range(kc):
                        c = ti * KC + k
                        first = (c == 0)
                        lastc = (c == nchunks - 1)
                        if mode == "tok":
                            ins = nc.tensor.matmul(pss[o].ap[:, 0:ncols], lhsT=actT.ap[:, c, o * P:(o + 1) * P],
                                                   rhs=wt.ap[:, k, 0:ncols], start=first, stop=lastc)
                        else:
                            ins = nc.tensor.matmul(pss[o].ap[:, 0:ntok], lhsT=wt.ap[:, k, o * P:(o + 1) * P],
                                                   rhs=actT.ap[:, c, 0:ntok], start=first, stop=lastc)
                    return ins
                self.op("pe", mm, reads=[wt, actT], writes=[pss[o]])
        self.tick()
        if self.cv_rate:
            if self.cv_boost[0] > 0:
                self.cv_boost[0] -= 1
                self.cv_meter(max(self.cv_rate, self.cv_boost[1]), self.cv_queues)
            else:
                self.cv_meter(self.cv_rate if self.cvq_in else 1, self.cv_queues)
        return pss

    def stage12(self):
        nc, cfg = self.nc, self.cfg
        T, TP, D, AW, BW, KC = cfg["T"], cfg["TP"], cfg["D"], cfg["AW"], cfg["BW"], cfg["KC"]
        NCH = self.NCH
        NHA, NHB = AW // P, BW // P
        GA, GB = AW // TT, BW // TT
        nkt = NCH // KC
        with ExitStack() as st:
            sb = lambda n, s, d: self.sb(n, s, d, st)
            xb = [Buf(sb("xb%d" % i, [P, D], F32), self.newsem("d_xb%d" % i)) for i in range(2)]
            self.junk = Buf(sb("junk", [P, D // 2], BF16))
            aT = Buf(sb("aT", [P, NCH, TT], BF16))
            gus = [Buf(sb("gu%d" % i, [P, NB, TT], F32)) for i in range(2)]
            tmpf = [Buf(sb("tf%d" % i, [P, TT], F32)) for i in range(5)]
            gvs = [tmpf[0], tmpf[1], tmpf[2], Buf(sb("gv3", [P, TT], F32))]
            vn = [Buf(sb("vn%d" % i, [P, TT], BF16)) for i in range(NB)]
            yns = [Buf(sb("yn%d" % i, [P, TT], F32)) for i in range(NB)]
            self.deferred = []
            wsT = Buf(sb("wsTf", [P, NHA * P], F32), self.newsem("d_ws"))
            wsm = Buf(sb("wsm", [P, NHA, P], BF16))
            ga = Buf(sb("ga", [P, AW], F32), wsT.dsem)
            yaT = Buf(sb("yaT", [P, NHA, TT], BF16), self.newsem("d_yaT"))
            qst = [Buf(sb("qst%d" % i, [P, NB, TT], BF16), self.newsem("d_qst%d" % i)) for i in range(2)]
            self.dma("sp", [lambda: nc.sync.dma_start(out=wsT.ap, in_=self.wsT_d),
                            lambda: nc.sync.dma_start(out=ga.ap, in_=self.ga_d)], wsT.dsem, writes=[wsT, ga])
            for h in range(NHA):
                self.op("dve", lambda h=h: nc.vector.tensor_tensor(out=wsm.ap[:, h, :], in0=wsT.ap[:, h * P:(h + 1) * P],
                                                                  in1=self.trilT, op=ALU.mult),
                        reads=[wsT, self.cst], writes=[wsm])

            ntile_pre, ntile_own = TP // TT, T // TT
            jobs = []
            def add_group(kind, g, c0):
                return [self.wq_add("w_in", k * KC * P, KC * P, c0, TT) for k in range(nkt)]
            plan = []
            for ti in range(ntile_pre + ntile_own):
                own = ti >= ntile_pre
                groups = []
                if own:
                    for g in range(GA):
                        groups.append(("u", g, add_group("u", g, g * TT)))
                        groups.append(("v", g, add_group("v", g, AW + g * TT)))
                    for g in range(GB):
                        groups.append(("q", g, add_group("q", g, 2 * AW + g * TT)))
                for g in range(GB):
                    groups.append(("k", g, add_group("k", g, 2 * AW + BW + g * TT)))
                for g in range(GB):
                    groups.append(("vb", g, add_group("vb", g, 2 * AW + 2 * BW + g * TT)))
                plan.append((ti, own, groups))

            nproj = sum(len(gr) for (_, _, gr) in plan)
            npieces = len(self.cvq_in) + len(self.cvq)
            self.cv_rate = 2
            self.cv_queues = [self.cvq_in, self.cvq]
            self.cv_boost = [GB, max(1, NCH // 8)]
            xi = 0
            qi = 0
            C2 = 2.0 * math.sqrt(2.0 / math.pi)
            for (ti, own, groups) in plan:
                xsrc = self.x_own if own else self.x_pre
                t0 = (ti - ntile_pre) * TT if own else ti * TT
                tg = TP + t0 if own else t0
                def pre(pl, b):
                    ti_, own_, _ = pl
                    xs_ = self.x_own if own_ else self.x_pre
                    r0 = ((ti_ - ntile_pre) * TT if own_ else ti_ * TT) + b * P
                    xbuf = xb[b % 2]
                    self.dma("sp", [lambda: nc.sync.dma_start(out=xbuf.ap, in_=xs_[r0:r0 + P, :])], xbuf.dsem, writes=[xbuf])
                    self.norm_pre(xbuf)
                pi = plan.index((ti, own, groups))
                if pi == 0:
                    pre(plan[0], 0)
                    pre(plan[0], 1)
                for b in range(NB):
                    self.norm_tr(xb[b % 2], self.g_mix, aT, b)
                    if b + 2 < NB:
                        pre(plan[pi], b + 2)
                    elif pi + 1 < len(plan):
                        pre(plan[pi + 1], b + 2 - NB)
                for (kind, g, widxs) in groups:
                    if kind in ("u", "v", "vb"):
                        pss = self.proj(widxs, aT, NCH, "tok")
                    else:
                        pss = self.proj(widxs, aT, NCH, "feat")
                    if kind == "u":
                        gu_g = gus[g % 2]
                        for b in range(NB):
                            self.gelu(pss[b], gu_g, gu_g.ap[:, b, :], tmpf, C2)
                    elif kind == "v":
                        HG = TT // P
                        gu_g = gus[g % 2]
                        for b in range(NB):
                            self.gelu(pss[b], gvs[b], gvs[b].ap, tmpf, C2)
                        for b in range(NB):
                            vnb, ynb = vn[b], yns[b]
                            self.sgu_A(gvs[b], g, b, tmpf, vnb, C2)
                            self.defer(2, lambda g=g, b=b, vnb=vnb, ynb=ynb, gu_g=gu_g: self.sgu_B(g, b, gu_g, tmpf, vnb, wsm, ga, ynb))
                            self.defer(4, lambda g=g, b=b, ynb=ynb: self.transpose_block(ynb, ynb.ap, HG, yaT, b, None, c_off=g * HG, aux=True))
                    elif kind in ("q", "k"):
                        stg = qst[qi % 2]
                        qi += 1
                        for e in range(NB):
                            if e % 2 == 0:
                                self.op("act", lambda e=e, stg=stg, pss=pss, kind=kind: nc.scalar.activation(
                                    out=stg.ap[:, e, :], in_=pss[e].ap, func=AF.Copy,
                                    scale=(1.0 / math.sqrt(P)) if kind == "q" else 1.0), reads=[pss[e]], writes=[stg])
                            else:
                                self.op("dve", lambda e=e, stg=stg, pss=pss, kind=kind: nc.vector.tensor_scalar(
                                    out=stg.ap[:, e, :], in0=pss[e].ap, scalar1=(1.0 / math.sqrt(P)) if kind == "q" else 1.0,
                                    scalar2=None, op0=ALU.mult), reads=[pss[e]], writes=[stg])
                        if kind == "q":
                            dst = self.QT_d[g * NB:(g + 1) * NB, :, t0:t0 + TT]
                        else:
                            dst = self.KT_d[g * NB:(g + 1) * NB, :, tg:tg + TT]
                        self.dma("pool", [lambda stg=stg, dst=dst: nc.gpsimd.dma_start(out=dst.rearrange("h p t -> p h t"), in_=stg.ap)],
                                 stg.dsem, reads=[stg])
                    else:
                        stg = qst[qi % 2]
                        qi += 1
                        for b in range(NB):
                            if b % 2 == 0:
                                self.op("act", lambda b=b, stg=stg, pss=pss: nc.scalar.copy(out=stg.ap[:, b, :], in_=pss[b].ap),
                                        reads=[pss[b]], writes=[stg])
                            else:
                                self.op("dve", lambda b=b, stg=stg, pss=pss: nc.vector.tensor_copy(out=stg.ap[:, b, :], in_=pss[b].ap),
                                        reads=[pss[b]], writes=[stg])
                        dst = self.V_d[tg:tg + TT, g * TT:(g + 1) * TT].rearrange("(b p) n -> p b n", p=P)
                        self.dma("pool", [lambda stg=stg, dst=dst: nc.gpsimd.dma_start(out=dst, in_=stg.ap)], stg.dsem, reads=[stg])
                self.flush_deferred()
                if own:
                    dst = self.YT_d[0:NHA, :, t0:t0 + TT].rearrange("c p t -> p c t")
                    self.dma("pool", [lambda dst=dst: nc.gpsimd.dma_start(out=dst, in_=yaT.ap)], yaT.dsem, reads=[yaT])

    def stage12_end(self):
        self.cv_meter(len(self.cvq_in) + len(self.cvq), [self.cvq_in, self.cvq])
        self.cv_rate = 0

    def gelu(self, ps, dstbuf, dst_ap, tmpf, C2):
        nc = self.nc
        self.op("act", lambda: nc.scalar.activation(out=dst_ap, in_=ps.ap, func=AF.Gelu_apprx_tanh), reads=[ps], writes=[dstbuf])

    def defer(self, k, fn):
        self.deferred.append([k, fn])

    def tick(self):
        due = []
        for item in self.deferred:
            item[0] -= 1
            if item[0] <= 0:
                due.append(item)
        self.deferred = [it for it in self.deferred if it[0] > 0]
        for it in due:
            it[1]()

    def flush_deferred(self):
        while self.deferred:
            self.tick()

    def sgu_A(self, gv, g, b, tmpf, vnb, C2):
        nc = self.nc
        sq = tmpf[3]
        st = self.stat[self.stat_i % len(self.stat)]
        self.stat_i += 1
        HG = TT // P
        gv3 = gv.ap.rearrange("p (h d) -> p h d", h=HG)
        sq3 = sq.ap.rearrange("p (h d) -> p h d", h=HG)
        self.op("dve", lambda: nc.vector.tensor_reduce(out=st.ap[:, 0:HG], in_=gv3, axis=AX.X, op=ALU.add), reads=[gv], writes=[st])
        self.op("pool", lambda: nc.gpsimd.tensor_tensor(out=sq.ap, in0=gv.ap, in1=gv.ap, op=ALU.mult), reads=[gv], writes=[sq])
        self.op("dve", lambda: nc.vector.tensor_reduce(out=st.ap[:, 4:4 + HG], in_=sq3, axis=AX.X, op=ALU.add), reads=[sq], writes=[st])
        self.op("dve", lambda: nc.vector.tensor_scalar(out=st.ap[:, 0:HG], in0=st.ap[:, 0:HG], scalar1=1.0 / P, scalar2=None,
                                                       op0=ALU.mult), reads=[st], writes=[st])
        st2 = self.stat[self.stat_i % len(self.stat)]
        self.stat_i += 1
        self.op("dve", lambda: nc.vector.tensor_tensor(out=st2.ap[:, 0:HG], in0=st.ap[:, 0:HG], in1=st.ap[:, 0:HG], op=ALU.mult),
                reads=[st], writes=[st2])
        self.op("dve", lambda: nc.vector.scalar_tensor_tensor(out=st.ap[:, 4:4 + HG], in0=st.ap[:, 4:4 + HG], scalar=1.0 / P,
                                                              in1=st2.ap[:, 0:HG], op0=ALU.mult, op1=ALU.subtract),
                reads=[st, st2], writes=[st])
        self.op("dve", lambda: nc.vector.tensor_scalar(out=st.ap[:, 4:4 + HG], in0=st.ap[:, 4:4 + HG], scalar1=EPS, scalar2=None,
                                                       op0=ALU.add), reads=[st], writes=[st])
        self.rsqrt(st, st.ap[:, 4:4 + HG], st.ap[:, 4:4 + HG])
        for h in range(HG):
            self.op("dve", lambda h=h: nc.vector.tensor_scalar(out=vnb.ap[:, h * P:(h + 1) * P], in0=gv3[:, h, :],
                                                               scalar1=st.ap[:, h:h + 1], scalar2=st.ap[:, 4 + h:5 + h],
                                                               op0=ALU.subtract, op1=ALU.mult), reads=[gv, st], writes=[vnb])

    def sgu_B(self, g, b, gu, tmpf, vnb, wsm, ga, yn):
        nc = self.nc
        sq, ya = tmpf[3], tmpf[4]
        HG = TT // P
        sq3 = sq.ap.rearrange("p (h d) -> p h d", h=HG)
        pm = self.psum_aux()

        def mix():
            for h in range(HG):
                ins = nc.tensor.matmul(pm.ap[:, h * P:(h + 1) * P], lhsT=wsm.ap[:, g * HG + h, :], rhs=vnb.ap[:, h * P:(h + 1) * P],
                                       start=True, stop=True)
            return ins
        self.op("pe", mix, reads=[wsm, vnb], writes=[pm])
        for h in range(HG):
            hh = g * HG + h
            self.op("dve", lambda h=h, hh=hh: nc.vector.scalar_tensor_tensor(
                out=ya.ap[:, h * P:(h + 1) * P], in0=pm.ap[:, h * P:(h + 1) * P], scalar=self.bsT[:, hh:hh + 1],
                in1=gu.ap[:, b, h * P:(h + 1) * P], op0=ALU.add, op1=ALU.mult), reads=[pm, gu, self.cst], writes=[ya])
        st3 = self.stat[self.stat_i % len(self.stat)]
        self.stat_i += 1
        self.op("pool", lambda: nc.gpsimd.tensor_tensor(out=sq.ap, in0=ya.ap, in1=ya.ap, op=ALU.mult), reads=[ya], writes=[sq])
        self.op("dve", lambda: nc.vector.tensor_reduce(out=st3.ap[:, 0:HG], in_=sq3, axis=AX.X, op=ALU.add), reads=[sq], writes=[st3])
        self.op("dve", lambda: nc.vector.tensor_scalar(out=st3.ap[:, 0:HG], in0=st3.ap[:, 0:HG], scalar1=1.0 / P, scalar2=EPS,
                                                       op0=ALU.mult, op1=ALU.add), reads=[st3], writes=[st3])
        self.rsqrt(st3, st3.ap[:, 0:HG], st3.ap[:, 0:HG])
        for h in range(HG):
            hh = g * HG + h
            self.op("dve", lambda h=h, hh=hh: nc.vector.scalar_tensor_tensor(
                out=yn.ap[:, h * P:(h + 1) * P], in0=ya.ap[:, h * P:(h + 1) * P], scalar=st3.ap[:, h:h + 1],
                in1=ga.ap[:, hh * P:(hh + 1) * P], op0=ALU.mult, op1=ALU.mult), reads=[ya, st3, ga], writes=[yn])

    def stage3(self):
        nc, cfg = self.nc, self.cfg
        T, TP, BW = cfg["T"], cfg["TP"], cfg["BW"]
        NHB = BW // P
        NHA = cfg["AW"] // P
        TK = TP + T
        NKB = TK // P
        nqt = T // TT
        NS = 2 if NHB % 2 == 0 else 1
        with ExitStack() as st:
            sb = lambda n, s, d: self.sb(n, s, d, st)
            S = []
            for x in range(NS):
                d = dict(
                    KTh=[Buf(sb("KTh%d_%d" % (x, i), [P, TK], BF16), self.newsem("d_KTh%d_%d" % (x, i))) for i in range(2)],
                    Vh=[Buf(sb("Vh%d_%d" % (x, i), [P, NKB, P], BF16), self.newsem("d_Vh%d_%d" % (x, i))) for i in range(2)],
                    QTh=[Buf(sb("QTh%d_%d" % (x, i), [P, T], BF16), self.newsem("d_QTh%d_%d" % (x, i))) for i in range(2)],
                    Eb=[Buf(sb("E%d_%d" % (x, i), [P, TT], F32)) for i in range(3)],
                    Lb=[Buf(sb("L%d_%d" % (x, i), [P, TT], BF16)) for i in range(3)],
                    Gb=[Buf(sb("G%d_%d" % (x, i), [P, TT], F32)) for i in range(2)],
                    Ab=[Buf(sb("A%d_%d" % (x, i), [P, TT], BF16)) for i in range(2)],
                    osb=Buf(sb("osb%d" % x, [P, TT], F32)), osq=Buf(sb("osq%d" % x, [P, TT], F32)), rsd=Buf(sb("rsd%d" % x, [P, TT], F32)),
                    ybs=[Buf(sb("ybs%d_%d" % (x, i), [P, TT], BF16), self.newsem("d_ybs%d_%d" % (x, i))) for i in range(2)],
                    zps=[self.ps[4 * x], self.ps[4 * x + 1]], cp=self.ps[4 * x + 2], op=self.ps[4 * x + 3],
                    heads=list(range(x, NHB, NS)), loaded=set(), its=[], yi=0)
                for hi, h in enumerate(d["heads"]):
                    for qt in range(nqt):
                        kbs = list(range(TP // P + (qt + 1) * NB - 1, -1, -1))
                        for n, kb in enumerate(kbs):
                            j = kb - (TP // P + qt * NB)
                            d["its"].append(dict(h=h, hi=hi, qt=qt, kb=kb, first=(n == 0), last=(n == len(kbs) - 1),
                                                 j=j if j >= 0 else None))
                S.append(d)
            NI = len(S[0]["its"])

            def load_head(d, hi):
                h = d["heads"][hi]
                i = hi % 2
                K_, V_, Q_ = d["KTh"][i], d["Vh"][i], d["QTh"][i]
                self.dma("sp", [lambda: nc.sync.dma_start(out=K_.ap, in_=self.KT_d[h])], K_.dsem, writes=[K_])
                self.dma("sp", [lambda: nc.sync.dma_start(out=V_.ap, in_=self.V_d[:, h * P:(h + 1) * P].rearrange("(n p) d -> p n d", p=P))],
                         V_.dsem, writes=[V_])
                self.dma("sp", [lambda: nc.sync.dma_start(out=Q_.ap, in_=self.QT_d[h])], Q_.dsem, writes=[Q_])
                d["loaded"].add(hi)

            def zT(d, i):
                it = d["its"][i]
                hi, qt, kb = it["hi"], it["qt"], it["kb"]
                if it["first"] and qt == 0 and (hi + 1) < len(d["heads"]) and (hi + 1) not in d["loaded"]:
                    load_head(d, hi + 1)
                z = d["zps"][i % 2]
                k_, q_ = d["KTh"][hi % 2], d["QTh"][hi % 2]
                self.op("pe", lambda: nc.tensor.matmul(z.ap, lhsT=k_.ap[:, kb * P:(kb + 1) * P], rhs=q_.ap[:, qt * TT:(qt + 1) * TT],
                                                       start=True, stop=True), reads=[k_, q_], writes=[z])

            def E(d, i):
                z, e = d["zps"][i % 2], d["Eb"][i % 3]
                self.op("act", lambda: nc.scalar.activation(out=e.ap, in_=z.ap, func=AF.Exp), reads=[z], writes=[e])

            def L(d, i):
                it = d["its"][i]
                e, l = d["Eb"][i % 3], d["Lb"][i % 3]
                self.op("act", lambda: nc.scalar.activation(out=l.ap, in_=e.ap, func=AF.Ln, bias=1.0), reads=[e], writes=[l])
                if it["j"] is not None:
                    j = it["j"]
                    self.op("dve", lambda: nc.vector.tensor_tensor(out=l.ap, in0=l.ap, in1=self.dmask[:, j, :], op=ALU.mult),
                            reads=[l, self.cbf], writes=[l])

            def Tinc(d, i):
                it = d["its"][i]
                cp, l = d["cp"], d["Lb"][i % 3]
                self.op("pe", lambda: nc.tensor.matmul(cp.ap, lhsT=self.tinc, rhs=l.ap, start=it["first"], stop=True,
                                                       skip_group_check=True), reads=[l, self.cbf], writes=[cp])

            def G(d, i):
                cp, g = d["cp"], d["Gb"][i % 2]
                self.op("act", lambda: nc.scalar.activation(out=g.ap, in_=cp.ap, func=AF.Exp, scale=-1.0), reads=[cp], writes=[g])

            def Tcomp(d, i):
                it = d["its"][i]
                if it["last"]:
                    return
                cp, l = d["cp"], d["Lb"][i % 3]
                self.op("pe", lambda: nc.tensor.matmul(cp.ap, lhsT=self.tcomp, rhs=l.ap, start=False, stop=True,
                                                       skip_group_check=True), reads=[l, self.cbf], writes=[cp])

            def A(d, i):
                it = d["its"][i]
                e, g, a = d["Eb"][i % 3], d["Gb"][i % 2], d["Ab"][i % 2]
                if it["j"] is not None:
                    j = it["j"]
                    self.op("dve", lambda: nc.vector.tensor_tensor(out=g.ap, in0=g.ap, in1=self.dmask[:, j, :], op=ALU.mult),
                            reads=[g, self.cbf], writes=[g])
                self.op("dve", lambda: nc.vector.tensor_tensor(out=a.ap, in0=e.ap, in1=g.ap, op=ALU.mult), reads=[e, g], writes=[a])

            def AV(d, i):
                it = d["its"][i]
                h, hi, qt, kb = it["h"], it["hi"], it["qt"], it["kb"]
                a, v_ = d["Ab"][i % 2], d["Vh"][hi % 2]
                op_ = d["op"]
                self.op("pe", lambda: nc.tensor.matmul(op_.ap, lhsT=v_.ap[:, kb, :], rhs=a.ap, start=it["first"], stop=it["last"]),
                        reads=[v_, a], writes=[op_])
                if it["last"]:
                    post(d, h, qt)

            def post(d, h, qt):
                op_, osb, osq, rsd = d["op"], d["osb"], d["osq"], d["rsd"]
                yb = d["ybs"][d["yi"] % 2]
                d["yi"] += 1
                self.op("act", lambda: nc.scalar.copy(out=osb.ap, in_=op_.ap), reads=[op_], writes=[osb])
                self.op("act", lambda: nc.scalar.activation(out=osq.ap, in_=op_.ap, func=AF.Square), reads=[op_], writes=[osq])
                self.op("pe", lambda: nc.tensor.matmul(op_.ap, lhsT=self.onesf, rhs=osq.ap, start=True, stop=True),
                        reads=[osq, self.cst], writes=[op_])
                self.op("dve", lambda: nc.vector.tensor_scalar(out=rsd.ap, in0=op_.ap, scalar1=1.0 / P, scalar2=EPS,
                                                               op0=ALU.mult, op1=ALU.add), reads=[op_], writes=[rsd])
                self.op("act", lambda: nc.scalar.activation(out=rsd.ap, in_=rsd.ap, func=AF.Ln), reads=[rsd], writes=[rsd])
                self.op("act", lambda: nc.scalar.activation(out=rsd.ap, in_=rsd.ap, func=AF.Exp, scale=-0.5), reads=[rsd], writes=[rsd])
                self.op("dve", lambda: nc.vector.scalar_tensor_tensor(out=yb.ap, in0=osb.ap, scalar=self.g_b[:, h:h + 1], in1=rsd.ap,
                                                                      op0=ALU.mult, op1=ALU.mult), reads=[osb, rsd, self.cst], writes=[yb])
                dst = self.YT_d[NHA + h, :, qt * TT:(qt + 1) * TT]
                self.dma("sp", [lambda: nc.sync.dma_start(out=dst, in_=yb.ap)], yb.dsem, reads=[yb])

            for d in S:
                load_head(d, 0)
            ncv = len(self.cvq3)
            cv_every = max(1, (NI * 9 // 10) // max(1, ncv))
            for d in S:
                zT(d, 0)
            for d in S:
                E(d, 0)
            for d in S:
                L(d, 0)
            if NI > 1:
                for d in S:
                    zT(d, 1)
            for d in S:
                Tinc(d, 0)
            for s_ in range(NI):
                if s_ % cv_every == 0:
                    self.cv_meter(1, [self.cvq3])
                if s_ + 2 < NI:
                    for d in S:
                        zT(d, s_ + 2)
                if s_ + 1 < NI:
                    for d in S:
                        E(d, s_ + 1)
                for d in S:
                    G(d, s_)
                if s_ + 1 < NI:
                    for d in S:
                        L(d, s_ + 1)
                for d in S:
                    Tcomp(d, s_)
                if s_ + 1 < NI:
                    for d in S:
                        Tinc(d, s_ + 1)
                for d in S:
                    A(d, s_)
                for d in S:
                    AV(d, s_)
            self.cv_meter(len(self.cvq3), [self.cvq3])

    def stage4(self):
        nc, cfg = self.nc, self.cfg
        T, D, DFF, PLE, KC, NQ = cfg["T"], cfg["D"], cfg["DFF"], cfg["PLE"], cfg["KC"], cfg["NQ"]
        NCH = self.NCH
        nkt = NCH // KC
        FQ = DFF // NQ
        FQC = FQ // P
        nkq = FQC // KC
        NE = D // TT
        NPC = PLE // P
        ntile = T // TT
        with ExitStack() as st:
            sb = lambda n, s, d: self.sb(n, s, d, st)
            hb = [Buf(sb("h%d" % b, [P, D], F32), self.newsem("d_h%d" % b)) for b in range(NB)]
            actT = Buf(sb("actT", [P, NCH, TT], BF16), self.newsem("d_actT"))
            hidT = Buf(sb("hidT", [P, FQC, TT], BF16))
            xq = [Buf(sb("xq%d" % i, [P, D // 4], F32)) for i in range(4)]
            gfin = Buf(sb("gfin", [P, D], F32), self.newsem("d_gfin"))
            pt = Buf(sb("pt", [P, PLE], F32), self.newsem("d_pt"))
            pT = Buf(sb("pT", [P, NPC, TT], BF16))
            tf = [Buf(sb("t4_%d" % i, [P, TT], F32)) for i in range(2)]
            self.dma("sp", [lambda: nc.sync.dma_start(out=gfin.ap, in_=self.gfin_d)], gfin.dsem, writes=[gfin])
            plan = []
            for ti in range(ntile):
                d = {}
                d["out"] = [[self.wq_add("w_out", k * KC * P, KC * P, e * TT, TT) for k in range(nkt)] for e in range(NE)]
                d["ffn"] = []
                for q in range(NQ):
                    ups = [[self.wq_add("w_up", k * KC * P, KC * P, q * FQ + fg * TT, TT) for k in range(nkt)] for fg in range(FQ // TT)]
                    dns = [[self.wq_add("w_down", q * FQ + k * KC * P, KC * P, e * TT, TT) for k in range(nkq)] for e in range(NE)]
                    d["ffn"].append((ups, dns))
                d["gate"] = [([self.wq_add("w_gate", k * KC * P, KC * P, e * TT, TT) for k in range(nkt)],
                              self.wq_add("w_ple", 0, PLE, e * TT, TT)) for e in range(NE)]
                plan.append(d)

            def load_yT(ti_):
                half = max(1, NCH // 2)
                self.dma("sp", [lambda c0=c0: nc.sync.dma_start(out=actT.ap[:, c0:c0 + half, :],
                                                                in_=self.YT_d[c0:c0 + half, :, ti_ * TT:(ti_ + 1) * TT].rearrange("c p t -> p c t"))
                                for c0 in range(0, NCH, half)], actT.dsem, writes=[actT])

            ei = 0
            for ti in range(ntile):
                t0 = ti * TT
                d = plan[ti]
                for b in range(NB):
                    self.dma("sp", [lambda b=b: nc.sync.dma_start(out=hb[b].ap, in_=self.x_own[t0 + b * P:t0 + (b + 1) * P, :])],
                             hb[b].dsem, writes=[hb[b]])
                if ti == 0:
                    load_yT(0)
                for e in range(NE):
                    pss = self.proj(d["out"][e], actT, NCH, "tok")
                    for b in range(NB):
                        self.op("dve", lambda b=b, e=e, pss=pss: nc.vector.tensor_tensor(
                            out=hb[b].ap[:, e * TT:(e + 1) * TT], in0=hb[b].ap[:, e * TT:(e + 1) * TT], in1=pss[b].ap, op=ALU.add),
                            reads=[pss[b], hb[b]], writes=[hb[b]])
                jflat = hidT.ap.rearrange("p c t -> p (c t)")[:, 0:D // 2]
                sts = [self.norm_stats(hb[b], [hidT, hidT], [jflat, jflat]) for b in range(NB)]
                self.norm_blocks_q(hb, self.g_ffn, actT, xq, sts)
                for q in range(NQ):
                    ups, dns = d["ffn"][q]
                    for fg, widxs in enumerate(ups):
                        pss = self.proj(widxs, actT, NCH, "feat")
                        for fc in range(NB):
                            t = tf[ei % 2]
                            ei += 1
                            self.op("act", lambda t=t, fc=fc, pss=pss: nc.scalar.activation(out=t.ap, in_=pss[fc].ap, func=AF.Relu),
                                    reads=[pss[fc]], writes=[t])
                            o = hidT.ap[:, fg * NB + fc, :]
                            eng = "dve" if (fc % 2 == 0) else "pool"
                            if eng == "dve":
                                self.op("dve", lambda t=t, o=o: nc.vector.tensor_tensor(out=o, in0=t.ap, in1=t.ap, op=ALU.mult),
                                        reads=[t], writes=[hidT])
                            else:
                                self.op("pool", lambda t=t, o=o: nc.gpsimd.tensor_tensor(out=o, in0=t.ap, in1=t.ap, op=ALU.mult),
                                        reads=[t], writes=[hidT])
                    for e, widxs in enumerate(dns):
                        pss = self.proj(widxs, hidT, FQC, "tok")
                        for b in range(NB):
                            self.op("dve", lambda b=b, e=e, pss=pss: nc.vector.tensor_tensor(
                                out=hb[b].ap[:, e * TT:(e + 1) * TT], in0=hb[b].ap[:, e * TT:(e + 1) * TT], in1=pss[b].ap, op=ALU.add),
                                reads=[pss[b], hb[b]], writes=[hb[b]])
                sts = [self.norm_stats(hb[b], [hidT, hidT], [jflat, jflat]) for b in range(NB)]
                self.norm_blocks_q(hb, self.g_ple, actT, xq, sts)
                for b in range(NB):
                    self.dma("sp", [lambda b=b: nc.sync.dma_start(out=pt.ap, in_=self.p_own[t0 + b * P:t0 + (b + 1) * P, :])],
                             pt.dsem, writes=[pt])
                    self.transpose_block(pt, pt.ap, NPC, pT, b, None)
                for e in range(NE):
                    pss = self.proj(d["gate"][e][0], actT, NCH, "tok")
                    wple = self.wq_get(d["gate"][e][1])
                    for b in range(NB):
                        pe_ = self.psum_aux()

                        def mm(b=b, e=e, pe_=pe_, wple=wple):
                            for k in range(NPC):
                                ins = nc.tensor.matmul(pe_.ap, lhsT=pT.ap[:, k, b * P:(b + 1) * P], rhs=wple.ap[:, k, :],
                                                       start=(k == 0), stop=(k == NPC - 1))
                            return ins
                        self.op("pe", mm, reads=[pT, wple], writes=[pe_])
                        t = tf[ei % 2]
                        ei += 1
                        self.op("act", lambda t=t, b=b, pss=pss: nc.scalar.activation(out=t.ap, in_=pss[b].ap, func=AF.Sigmoid),
                                reads=[pss[b]], writes=[t])
                        self.op("dve", lambda t=t, pe_=pe_: nc.vector.tensor_tensor(out=t.ap, in0=t.ap, in1=pe_.ap, op=ALU.mult),
                                reads=[t, pe_], writes=[t])
                        self.op("pool", lambda t=t, b=b, e=e: nc.gpsimd.tensor_tensor(
                            out=hb[b].ap[:, e * TT:(e + 1) * TT], in0=hb[b].ap[:, e * TT:(e + 1) * TT], in1=t.ap, op=ALU.add),
                            reads=[t, hb[b]], writes=[hb[b]])
                if ti + 1 < ntile:
                    load_yT(ti + 1)
                for b in range(NB):
                    stt = self.stat[self.stat_i % len(self.stat)]
                    self.stat_i += 1
                    H2 = D // 2
                    self.op("dve", lambda stt=stt: nc.vector.memset(stt.ap[:, 0:8], 0.0), writes=[stt])
                    for hf in range(2):
                        self.op("act", lambda stt=stt, b=b, hf=hf: nc.scalar.activation(
                            out=jflat, in_=hb[b].ap[:, hf * H2:(hf + 1) * H2], func=AF.Square,
                            accum_out=stt.ap[:, 4 + hf:5 + hf]), reads=[hb[b]], writes=[hidT, stt])
                    self.op("dve", lambda stt=stt: nc.vector.tensor_tensor(out=stt.ap[:, 0:1], in0=stt.ap[:, 4:5], in1=stt.ap[:, 5:6],
                                                                           op=ALU.add), reads=[stt], writes=[stt])
                    self.op("dve", lambda stt=stt: nc.vector.tensor_scalar(out=stt.ap[:, 1:2], in0=stt.ap[:, 0:1], scalar1=1.0 / D,
                                                                           scalar2=EPS, op0=ALU.mult, op1=ALU.add), reads=[stt], writes=[stt])
                    self.rsqrt(stt, stt.ap[:, 2:3], stt.ap[:, 1:2])
                    self.op("dve", lambda stt=stt, b=b: nc.vector.scalar_tensor_tensor(
                        out=hb[b].ap, in0=hb[b].ap, scalar=stt.ap[:, 2:3], in1=gfin.ap, op0=ALU.mult, op1=ALU.mult),
                        reads=[hb[b], stt, gfin], writes=[hb[b]])
                    self.dma("pool", [lambda b=b: nc.gpsimd.dma_start(out=self.out_d[t0 + b * P:t0 + (b + 1) * P, :], in_=hb[b].ap)],
                             hb[b].dsem, reads=[hb[b]])


def make_consts(cfg, norm_mix, norm_ffn, norm_ple, norm_b_out, b_s, norm_a_out, norm_final, w_s):
    D, AW, BW = cfg["D"], cfg["AW"], cfg["BW"]
    NCH, NHA, NHB = D // P, AW // P, BW // P
    ar = np.arange(P)
    ident = np.eye(P, dtype=np.float32)
    ones = np.ones((P, P), np.float32)
    trilT = (ar[None, :] >= ar[:, None]).astype(np.float32)
    fm = lambda g, n: np.ascontiguousarray(np.asarray(g, np.float32).reshape(n, P).T)
    cst = np.concatenate([ident, ones, trilT, fm(norm_mix, NCH), fm(norm_ffn, NCH), fm(norm_ple, NCH),
                          fm(norm_b_out, NHB), np.ascontiguousarray(np.asarray(b_s, np.float32).T)], axis=1)
    tinc = (ar[:, None] >= ar[None, :]).astype(np.float32)
    tcomp = 1.0 - tinc
    q = np.arange(TT)
    dm = np.stack([((j * P + ar[:, None]) < q[None, :]).astype(np.float32) for j in range(NB)], axis=1)
    cbf = np.concatenate([tinc, tcomp, dm.reshape(P, NB * TT)], axis=1).astype(ml_dtypes.bfloat16)
    ga_rep = np.ascontiguousarray(np.broadcast_to(np.asarray(norm_a_out, np.float32)[None, :], (P, AW)))
    gfin_rep = np.ascontiguousarray(np.broadcast_to(np.asarray(norm_final, np.float32)[None, :], (P, D)))
    wsT = np.ascontiguousarray(np.transpose(np.asarray(w_s, np.float32), (2, 0, 1)).reshape(P, NHA * P))
    return dict(cst=np.ascontiguousarray(cst), cbf=np.ascontiguousarray(cbf), ga_rep=ga_rep, gfin_rep=gfin_rep, wsT=wsT)


def kernel(x, p, norm_mix, w_in, w_s, b_s, norm_a_out, norm_b_out, w_out, norm_ffn, w_up, w_down,
           norm_ple, w_ple_gate, w_ple_proj, norm_final):
    cfg = FULL
    T, TP, D = cfg["T"], cfg["TP"], cfg["D"]
    x = np.asarray(x, np.float32)
    p = np.asarray(p, np.float32)
    B, S, _ = x.shape
    consts = make_consts(cfg, norm_mix[0], norm_ffn[0], norm_ple[0], norm_b_out[0], b_s[0], norm_a_out[0], norm_final, w_s[0])
    shared = dict(w_in=np.ascontiguousarray(w_in[0], dtype=np.float32), w_out=np.ascontiguousarray(w_out[0], dtype=np.float32),
                  w_up=np.ascontiguousarray(w_up[0], dtype=np.float32), w_down=np.ascontiguousarray(w_down[0], dtype=np.float32),
                  w_gate=np.ascontiguousarray(w_ple_gate[0], dtype=np.float32),
                  w_ple=np.ascontiguousarray(w_ple_proj[0], dtype=np.float32), **consts)
    zeros = np.zeros((TP, D), np.float32)
    in_maps = []
    for core in range(8):
        b, s = core // 2, core % 2
        m = dict(shared)
        m["x_own"] = np.ascontiguousarray(x[b, s * T:(s + 1) * T])
        m["x_pre"] = np.ascontiguousarray(x[b, 0:TP]) if s == 1 else zeros
        m["p_own"] = np.ascontiguousarray(p[0, b, s * T:(s + 1) * T])
        in_maps.append(m)
    nc = K(cfg).build()
    res = run_bass_kernel_spmd(nc, in_maps, core_ids=list(range(8)))
    out = np.empty((B, S, D), np.float32)
    for core in range(8):
        b, s = core // 2, core % 2
        out[b, s * T:(s + 1) * T] = np.asarray(res.results[core]["out"], np.float32)
    return out
```

```python
import math
from contextlib import ExitStack

import numpy as np
import ml_dtypes

import concourse.bass as bass
import concourse.mybir as mybir
from concourse.bass_utils import run_bass_kernel_spmd

F32 = mybir.dt.float32
BF16 = mybir.dt.bfloat16
AF = mybir.ActivationFunctionType
ALU = mybir.AluOpType
AX = mybir.AxisListType

EPS = 1e-6
P = 128
TT = 512
NB = TT // P

FULL = dict(T=2048, TP=2048, D=4096, AW=2048, BW=2048, DFF=16384, PLE=256, KC=16, NQ=8)


class Sem:
    _id = 0

    def __init__(self, nc, name):
        self.h = nc.alloc_semaphore(name)
        self.n = 0
        Sem._id += 1
        self.id = Sem._id


class Buf:
    def __init__(self, ap=None, dsem=None):
        self.ap = ap
        self.w = None
        self.r = {}
        self.dsem = dsem


class K:
    def __init__(self, cfg):
        self.cfg = cfg
        self.nc = bass.Bass("TRN2", target_bir_lowering=False)
        nc = self.nc
        self.eng = {"pe": nc.tensor, "act": nc.scalar, "dve": nc.vector, "pool": nc.gpsimd, "sp": nc.sync}
        self.prog = {e: Sem(nc, "p_" + e) for e in ["pe", "act", "dve", "pool"]}
        self.deferred = []
        self.waited = {e: {} for e in self.eng}
        self.allsems = list(self.prog.values())
        self.ps_i = 0
        self.cv_rate = 0
        self.cv_hist = []
        self.cv_boost = [0, 0]
        self.cv_queues = []
        self.pp_i = 0
        self.pa_i = 0

    def newsem(self, name):
        s = Sem(self.nc, name)
        self.allsems.append(s)
        return s

    def _wait(self, e, deps, own_ok):
        for (s, v) in deps:
            if own_ok and e in self.prog and s is self.prog[e]:
                continue
            if self.waited[e].get(s.id, 0) >= v:
                continue
            self.eng[e].wait_ge(s.h, v)
            self.waited[e][s.id] = v

    def _deps(self, reads, writes):
        deps = []
        for b in reads:
            if b.w:
                deps.append(b.w)
        for b in writes:
            if b.w:
                deps.append(b.w)
            deps.extend(b.r.values())
        return deps

    def _mark(self, ev, reads, writes):
        for b in reads:
            b.r[ev[0].id] = ev
        for b in writes:
            b.w = ev
            b.r = {}

    def op(self, e, fn, reads=(), writes=()):
        self._wait(e, self._deps(reads, writes), e == "pe")
        ins = fn()
        s = self.prog[e]
        s.n += 1
        ins.then_inc(s.h, 1)
        self._mark((s, s.n), reads, writes)
        return ins

    def dma(self, q, fns, sem, reads=(), writes=()):
        self._wait(q, self._deps(reads, writes), False)
        for fn in fns:
            ins = fn()
            ins.then_inc(sem.h, 16)
            sem.n += 16
        self._mark((sem, sem.n), reads, writes)

    def barrier(self):
        evs = [(s, s.n) for s in self.allsems if s.n > 0]
        for e in self.eng:
            self._wait(e, evs, False)

    def cv_meter(self, k, queues):
        todo = []
        for q in queues:
            while q and len(todo) < k:
                todo.append(q.pop(0))
        if not todo:
            return
        self._wait("pool", [(self.prog["pe"], self.prog["pe"].n)], False)
        for (fns, s, b) in todo:
            if len(self.cv_hist) >= 3:
                self._wait("pool", [self.cv_hist[-3]], False)
            self.dma("pool", fns, s, writes=[b])
            self.cv_hist.append((s, s.n))

    def cv_emit(self, k):
        for _ in range(min(k, len(self.cvq))):
            fns, s, b = self.cvq.pop(0)
            self.dma("pool", fns, s, writes=[b])

    def ensure_conv(self, bf):
        for q in (self.cvq_in, self.cvq, self.cvq3):
            for item in [it for it in q if it[2] is bf]:
                q.remove(item)
                self.dma("pool", item[0], item[1], writes=[item[2]])
        assert bf.w is not None

    def psum(self):
        b = self.ps[self.ps_i % 8]
        self.ps_i += 1
        return b

    def psum_proj(self):
        b = self.ps[self.pp_i % 6]
        self.pp_i += 1
        return b

    def psum_aux(self):
        b = self.ps[6 + self.pa_i % 2]
        self.pa_i += 1
        return b

    def rsqrt(self, buf, out_ap, in_ap):
        nc = self.nc
        self.op("act", lambda: nc.scalar.activation(out=out_ap, in_=in_ap, func=AF.Sqrt), reads=[buf], writes=[buf])
        self.op("dve", lambda: nc.vector.reciprocal(out=out_ap, in_=out_ap), reads=[buf], writes=[buf])

    def build(self):
        nc, cfg = self.nc, self.cfg
        T, TP, D, AW, BW, DFF, PLE, KC, NQ = (cfg[k] for k in ["T", "TP", "D", "AW", "BW", "DFF", "PLE", "KC", "NQ"])
        self.NCH = NCH = D // P
        NHA, NHB = AW // P, BW // P
        DIN = 2 * AW + 3 * BW
        TK = TP + T
        self.NCS = NCS = 3 * P + 3 * NCH + NHB + NHA
        dt = nc.dram_tensor
        self.x_own = dt("x_own", [T, D], F32, kind="ExternalInput").ap()
        self.x_pre = dt("x_pre", [TP, D], F32, kind="ExternalInput").ap()
        self.p_own = dt("p_own", [T, PLE], F32, kind="ExternalInput").ap()
        wnames = dict(w_in=(D, DIN), w_out=(D, D), w_up=(D, DFF), w_down=(DFF, D), w_gate=(D, D), w_ple=(PLE, D))
        self.wf = {n: dt(n, list(s), F32, kind="ExternalInput").ap() for n, s in wnames.items()}
        self.cst_d = dt("cst", [P, NCS], F32, kind="ExternalInput").ap()
        self.cbf_d = dt("cbf", [P, 2 * P + NB * TT], BF16, kind="ExternalInput").ap()
        self.ga_d = dt("ga_rep", [P, AW], F32, kind="ExternalInput").ap()
        self.gfin_d = dt("gfin_rep", [P, D], F32, kind="ExternalInput").ap()
        self.wsT_d = dt("wsT", [P, NHA * P], F32, kind="ExternalInput").ap()
        self.out_d = dt("out", [T, D], F32, kind="ExternalOutput").ap()
        self.wb = {n: dt("b_" + n, [s[1] // TT, P, s[0] // P, TT], BF16, kind="Internal").ap() for n, s in wnames.items()}
        sk = "ExternalOutput" if cfg.get("debug") else "Internal"
        self.KT_d = dt("KT_s", [NHB, P, TK], BF16, kind=sk).ap()
        self.V_d = dt("V_s", [TK, BW], BF16, kind=sk).ap()
        self.QT_d = dt("QT_s", [NHB, P, T], BF16, kind=sk).ap()
        self.YT_d = dt("YT_s", [NCH, P, T], BF16, kind=sk).ap()

        self.ps = [Buf(nc.alloc_psum_tensor("ps%d" % i, [P, TT], F32).ap()) for i in range(8)]

        self.wev = {}
        self.cvq = []
        self.cvq_in = []
        kv0 = (2 * AW + BW) // TT
        GA_ = AW // TT
        self.cvq3 = []
        CP = 8
        for n in ["w_in", "w_out", "w_up", "w_down", "w_gate", "w_ple"]:
            rows, cols = wnames[n]
            ng, kch = cols // TT, rows // P
            order = list(range(ng))
            if n == "w_in":
                rest = []
                for g in range(GA_):
                    rest += [g, GA_ + g]
                rest += list(range(2 * GA_, kv0))
                order = list(range(kv0, ng)) + rest
            cls = {}
            for g in order:
                key = g if n == "w_in" else "all"
                if key not in cls:
                    cls[key] = (self.newsem("cv_%s_%s" % (n, key)), Buf())
                s, bf = cls[key]
                self.wev[(n, g)] = bf
                for c0 in range(0, kch, CP):
                    c1 = min(kch, c0 + CP)
                    fn = (lambda n=n, g=g, c0=c0, c1=c1: nc.gpsimd.dma_start(
                        out=self.wb[n][g, :, c0:c1, :],
                        in_=self.wf[n][c0 * P:c1 * P, g * TT:(g + 1) * TT].rearrange("(c p) n -> p c n", p=P)))
                    if n == "w_in" and kv0 <= g < kv0 + (ng - kv0) // 2:
                        self.dma("pool", [fn], s, writes=[bf])
                    elif n == "w_in":
                        self.cvq_in.append(([fn], s, bf))
                    elif n == "w_out" or (n == "w_up" and g < (14 * ng) // 32):
                        self.cvq.append(([fn], s, bf))
                    else:
                        self.cvq3.append(([fn], s, bf))
        self.cv_total = len(self.cvq)

        with ExitStack() as glob:
            def sb(name, shape, dtype, stack=glob):
                return stack.enter_context(nc.sbuf_tensor("s_" + name, shape, dtype)).ap()
            self.sb = sb
            cs = self.newsem("d_cst")
            self.cst = Buf(sb("cst", [P, NCS], F32), cs)
            self.cbf = Buf(sb("cbf", [P, 2 * P + NB * TT], BF16), cs)
            self.dma("sp", [lambda: nc.sync.dma_start(out=self.cst.ap, in_=self.cst_d),
                            lambda: nc.sync.dma_start(out=self.cbf.ap, in_=self.cbf_d)], cs,
                     writes=[self.cst, self.cbf])
            c = self.cst.ap
            self.ident = c[:, 0:P]
            self.onesf = c[:, P:2 * P]
            self.trilT = c[:, 2 * P:3 * P]
            o = 3 * P
            self.g_mix = c[:, o:o + NCH]
            self.g_ffn = c[:, o + NCH:o + 2 * NCH]
            self.g_ple = c[:, o + 2 * NCH:o + 3 * NCH]
            o += 3 * NCH
            self.g_b = c[:, o:o + NHB]
            self.bsT = c[:, o + NHB:o + NHB + NHA]
            cb = self.cbf.ap
            self.tinc = cb[:, 0:P]
            self.tcomp = cb[:, P:2 * P]
            self.dmask = cb[:, 2 * P:].rearrange("p (j q) -> p j q", j=NB)
            self.stat = [Buf(sb("stat%d" % i, [P, 8], F32)) for i in range(8)]
            self.stat_i = 0
            self.wring = [Buf(sb("wr%d" % i, [P, KC, TT], BF16), self.newsem("d_wr%d" % i)) for i in range(3)]
            self.wring_i = 0
            self.wqueue = []
            self.wq_next = 0

            self.stage12()
            self.stage12_end()
            self.barrier()
            self.stage3()
            self.barrier()
            self.stage4()
            self.barrier()
        return nc

    def wq_add(self, name, r0, nrows, c0, ncols):
        self.wqueue.append((name, r0, nrows, c0, ncols))
        return len(self.wqueue) - 1

    def wq_get(self, idx, pf=2):
        nc = self.nc
        last = min(len(self.wqueue) - 1, idx + pf)
        while self.wq_next <= last:
            name, r0, nrows, c0, ncols = self.wqueue[self.wq_next]
            slot = self.wring[self.wq_next % 3]
            kc = nrows // P
            g, cl = c0 // TT, r0 // P
            src = self.wb[name][g, :, cl:cl + kc, :]
            fns = [lambda src=src, slot=slot, kc=kc: nc.sync.dma_start(out=slot.ap[:, 0:kc, :], in_=src)]
            dep = self.wev[(name, g)]
            if dep.w is None:
                self.ensure_conv(dep)
            self.dma("sp", fns, slot.dsem, reads=[dep], writes=[slot])
            self.wq_next += 1
        return self.wring[idx % 3]

    def norm_stats(self, src, junks, junk_aps=None):
        nc = self.nc
        D = self.cfg["D"]
        H = D // 2
        st = self.stat[self.stat_i % len(self.stat)]
        self.stat_i += 1
        self.op("dve", lambda: nc.vector.memset(st.ap[:, 0:2], 0.0), writes=[st])
        for hf in range(2):
            jb = junks[hf]
            j_ap = jb.ap[:, 0:H] if junk_aps is None else junk_aps[hf]
            self.op("act", lambda hf=hf, jb=jb, j_ap=j_ap: nc.scalar.activation(out=j_ap, in_=src.ap[:, hf * H:(hf + 1) * H], func=AF.Square,
                                                                    accum_out=st.ap[:, hf:hf + 1]), reads=[src], writes=[jb, st])
        self.op("dve", lambda: nc.vector.tensor_tensor(out=st.ap[:, 2:3], in0=st.ap[:, 0:1], in1=st.ap[:, 1:2], op=ALU.add),
                reads=[st], writes=[st])
        self.op("dve", lambda: nc.vector.tensor_scalar(out=st.ap[:, 2:3], in0=st.ap[:, 2:3], scalar1=1.0 / D, scalar2=EPS,
                                                       op0=ALU.mult, op1=ALU.add), reads=[st], writes=[st])
        self.rsqrt(st, st.ap[:, 3:4], st.ap[:, 2:3])
        return st

    def norm_scale(self, src, st, hf, dstb, d_ap):
        nc = self.nc
        H = self.cfg["D"] // 2
        s_ap = src.ap[:, hf * H:(hf + 1) * H]
        if hf == 0:
            self.op("dve", lambda: nc.vector.tensor_scalar(out=d_ap, in0=s_ap, scalar1=st.ap[:, 3:4], scalar2=None, op0=ALU.mult),
                    reads=[src, st], writes=[dstb])
        else:
            self.op("act", lambda: nc.scalar.activation(out=d_ap, in_=s_ap, func=AF.Copy, scale=st.ap[:, 3:4]),
                    reads=[src, st], writes=[dstb])

    def norm_pre(self, src):
        H = self.cfg["D"] // 2
        st = self.norm_stats(src, [self.junk, self.junk])
        for hf in range(2):
            self.norm_scale(src, st, hf, src, src.ap[:, hf * H:(hf + 1) * H])

    def norm_tr(self, src, gain, dstT, b):
        H = self.cfg["D"] // 2
        for hf in range(2):
            self.transpose_block(src, src.ap[:, hf * H:(hf + 1) * H], self.NCH // 2, dstT, b, gain, c_off=hf * (self.NCH // 2))

    def norm_block(self, src, gain, dstT, b, tmp=None, st=None):
        H = self.cfg["D"] // 2
        if st is None:
            st = self.norm_stats(src, tmp)
        for hf in range(2):
            self.norm_scale(src, st, hf, tmp[hf], tmp[hf].ap[:, 0:H])
            self.transpose_block(tmp[hf], tmp[hf].ap[:, 0:H], self.NCH // 2, dstT, b, gain, c_off=hf * (self.NCH // 2))
        return st

    def norm_blocks_q(self, hbs, gain, dstT, xq, sts):
        nc = self.nc
        D, NCH = self.cfg["D"], self.NCH
        NQ4 = len(xq)
        Q = D // NQ4
        cq = NCH // NQ4

        def scale(b, q):
            src, st = hbs[b], sts[b]
            s_ap = src.ap[:, q * Q:(q + 1) * Q]
            if q % 2 == 0:
                self.op("dve", lambda: nc.vector.tensor_scalar(out=xq[q].ap, in0=s_ap, scalar1=st.ap[:, 3:4], scalar2=None, op0=ALU.mult),
                        reads=[src, st], writes=[xq[q]])
            else:
                self.op("act", lambda: nc.scalar.activation(out=xq[q].ap, in_=s_ap, func=AF.Copy, scale=st.ap[:, 3:4]),
                        reads=[src, st], writes=[xq[q]])
        for q in range(NQ4):
            scale(0, q)
        for b in range(len(hbs)):
            for q in range(NQ4):
                self.transpose_block(xq[q], xq[q].ap, cq, dstT, b, gain, c_off=q * cq)
                if b + 1 < len(hbs):
                    scale(b + 1, q)

    def transpose_block(self, src, src_ap, nch, dstT, b, gain=None, c_off=0, aux=False):
        nc = self.nc
        k = 0
        for c0 in range(0, nch, 4):
            n = min(4, nch - c0)
            ps = self.psum_aux() if aux else self.psum()
            pv = ps.ap.rearrange("p (j q) -> p j q", j=4)

            def tr(c0=c0, n=n, pv=pv):
                for j in range(n):
                    ins = nc.tensor.transpose(pv[:, j, :], src_ap[:, (c0 + j) * P:(c0 + j + 1) * P], self.ident)
                return ins
            self.op("pe", tr, reads=[src, self.cst], writes=[ps])
            if gain is None:
                e = "act" if (k % 2 == 0) else "dve"
                k += 1
                o = dstT.ap[:, c_off + c0:c_off + c0 + n, b * P:(b + 1) * P]
                if e == "act":
                    self.op("act", lambda o=o, pv=pv, n=n: nc.scalar.copy(out=o, in_=pv[:, 0:n, :]), reads=[ps], writes=[dstT])
                else:
                    self.op("dve", lambda o=o, pv=pv, n=n: nc.vector.tensor_copy(out=o, in_=pv[:, 0:n, :]), reads=[ps], writes=[dstT])
            elif (c0 // 4) % 2 == 0:
                cc0 = c_off + c0
                o = dstT.ap[:, cc0:cc0 + n, b * P:(b + 1) * P]
                gb = gain[:, cc0:cc0 + n].unsqueeze(2).broadcast_to([P, n, P])
                self.op("dve", lambda o=o, pv=pv, n=n, gb=gb: nc.vector.tensor_tensor(out=o, in0=pv[:, 0:n, :], in1=gb, op=ALU.mult),
                        reads=[ps, self.cst], writes=[dstT])
            else:
                for j in range(n):
                    cc = c_off + c0 + j
                    o = dstT.ap[:, cc, b * P:(b + 1) * P]
                    self.op("act", lambda o=o, pv=pv, j=j, cc=cc: nc.scalar.activation(
                        out=o, in_=pv[:, j, :], func=AF.Copy, scale=gain[:, cc:cc + 1]), reads=[ps, self.cst], writes=[dstT])

    def proj(self, tiles, actT, nchunks, mode, ncols=TT, ntok=TT):
        nc = self.nc
        KC = self.cfg["KC"]
        nout = (ntok // P) if mode == "tok" else (ncols // P)
        pss = [self.psum_proj() for _ in range(nout)]
        for ti, widx in enumerate(tiles):
            wt = self.wq_get(widx)
            kc = min(KC, nchunks - ti * KC)
            for o in range(nout):
                def mm(o=o, ti=ti, wt=wt, kc=kc):
                    for k in range(kc):
                        c = ti * KC + k
                        first = (c == 0)
                        lastc = (c == nchunks - 1)
                        if mode == "tok":
                            ins = nc.tensor.matmul(pss[o].ap[:, 0:ncols], lhsT=actT.ap[:, c, o * P:(o + 1) * P],
                                                   rhs=wt.ap[:, k, 0:ncols], start=first, stop=lastc)
                        else:
                            ins = nc.tensor.matmul(pss[o].ap[:, 0:ntok], lhsT=wt.ap[:, k, o * P:(o + 1) * P],
                                                   rhs=actT.ap[:, c, 0:ntok], start=first, stop=lastc)
                    return ins
                self.op("pe", mm, reads=[wt, actT], writes=[pss[o]])
            if self.cv_rate:
                if self.cv_boost[0] > 0:
                    k_call = max(self.cv_rate, self.cv_boost[1])
                else:
                    k_call = self.cv_rate if self.cvq_in else 1
                nt = len(tiles)
                k = k_call // nt + (1 if ti < k_call % nt else 0)
                if k:
                    self.cv_meter(k, self.cv_queues)
                if ti == nt - 1 and self.cv_boost[0] > 0:
                    self.cv_boost[0] -= 1
        self.tick()
        return pss

    def stage12(self):
        nc, cfg = self.nc, self.cfg
        T, TP, D, AW, BW, KC = cfg["T"], cfg["TP"], cfg["D"], cfg["AW"], cfg["BW"], cfg["KC"]
        NCH = self.NCH
        NHA, NHB = AW // P, BW // P
        GA, GB = AW // TT, BW // TT
        nkt = NCH // KC
        with ExitStack() as st:
            sb = lambda n, s, d: self.sb(n, s, d, st)
            xb = [Buf(sb("xb%d" % i, [P, D], F32), self.newsem("d_xb%d" % i)) for i in range(2)]
            self.junk = Buf(sb("junk", [P, D // 2], BF16))
            aT = Buf(sb("aT", [P, NCH, TT], BF16))
            gus = [Buf(sb("gu%d" % i, [P, NB, TT], F32)) for i in range(2)]
            tmpf = [Buf(sb("tf%d" % i, [P, TT], F32)) for i in range(5)]
            gvs = [tmpf[0], tmpf[1], tmpf[2], Buf(sb("gv3", [P, TT], F32))]
            vn = [Buf(sb("vn%d" % i, [P, TT], BF16)) for i in range(NB)]
            yns = [Buf(sb("yn%d" % i, [P, TT], F32)) for i in range(NB)]
            self.deferred = []
            wsT = Buf(sb("wsTf", [P, NHA * P], F32), self.newsem("d_ws"))
            wsm = Buf(sb("wsm", [P, NHA, P], BF16))
            ga = Buf(sb("ga", [P, AW], F32), wsT.dsem)
            yaT = Buf(sb("yaT", [P, NHA, TT], BF16), self.newsem("d_yaT"))
            qst = [Buf(sb("qst%d" % i, [P, NB, TT], BF16), self.newsem("d_qst%d" % i)) for i in range(2)]
            self.dma("sp", [lambda: nc.sync.dma_start(out=wsT.ap, in_=self.wsT_d),
                            lambda: nc.sync.dma_start(out=ga.ap, in_=self.ga_d)], wsT.dsem, writes=[wsT, ga])
            for h in range(NHA):
                self.op("dve", lambda h=h: nc.vector.tensor_tensor(out=wsm.ap[:, h, :], in0=wsT.ap[:, h * P:(h + 1) * P],
                                                                  in1=self.trilT, op=ALU.mult),
                        reads=[wsT, self.cst], writes=[wsm])

            ntile_pre, ntile_own = TP // TT, T // TT
            jobs = []
            def add_group(kind, g, c0):
                return [self.wq_add("w_in", k * KC * P, KC * P, c0, TT) for k in range(nkt)]
            plan = []
            for ti in range(ntile_pre + ntile_own):
                own = ti >= ntile_pre
                groups = []
                if own:
                    for g in range(GA):
                        groups.append(("u", g, add_group("u", g, g * TT)))
                        groups.append(("v", g, add_group("v", g, AW + g * TT)))
                    for g in range(GB):
                        groups.append(("q", g, add_group("q", g, 2 * AW + g * TT)))
                for g in range(GB):
                    groups.append(("k", g, add_group("k", g, 2 * AW + BW + g * TT)))
                for g in range(GB):
                    groups.append(("vb", g, add_group("vb", g, 2 * AW + 2 * BW + g * TT)))
                plan.append((ti, own, groups))

            nproj = sum(len(gr) for (_, _, gr) in plan)
            npieces = len(self.cvq_in) + len(self.cvq)
            self.cv_rate = 2
            self.cv_queues = [self.cvq_in, self.cvq]
            self.cv_boost = [GB, max(1, NCH // 8)]
            xi = 0
            qi = 0
            C2 = 2.0 * math.sqrt(2.0 / math.pi)
            for (ti, own, groups) in plan:
                xsrc = self.x_own if own else self.x_pre
                t0 = (ti - ntile_pre) * TT if own else ti * TT
                tg = TP + t0 if own else t0
                def pre(pl, b):
                    ti_, own_, _ = pl
                    xs_ = self.x_own if own_ else self.x_pre
                    r0 = ((ti_ - ntile_pre) * TT if own_ else ti_ * TT) + b * P
                    xbuf = xb[b % 2]
                    self.dma("sp", [lambda: nc.sync.dma_start(out=xbuf.ap, in_=xs_[r0:r0 + P, :])], xbuf.dsem, writes=[xbuf])
                    self.norm_pre(xbuf)
                pi = plan.index((ti, own, groups))
                if pi == 0:
                    pre(plan[0], 0)
                    pre(plan[0], 1)
                for b in range(NB):
                    self.norm_tr(xb[b % 2], self.g_mix, aT, b)
                    if b + 2 < NB:
                        pre(plan[pi], b + 2)
                    elif pi + 1 < len(plan):
                        pre(plan[pi + 1], b + 2 - NB)
                for (kind, g, widxs) in groups:
                    if kind in ("u", "v", "vb"):
                        pss = self.proj(widxs, aT, NCH, "tok")
                    else:
                        pss = self.proj(widxs, aT, NCH, "feat")
                    if kind == "u":
                        gu_g = gus[g % 2]
                        for b in range(NB):
                            self.gelu(pss[b], gu_g, gu_g.ap[:, b, :], tmpf, C2)
                    elif kind == "v":
                        HG = TT // P
                        gu_g = gus[g % 2]
                        for b in range(NB):
                            self.gelu(pss[b], gvs[b], gvs[b].ap, tmpf, C2)
                        for b in range(NB):
                            vnb, ynb = vn[b], yns[b]
                            self.sgu_A(gvs[b], g, b, tmpf, vnb, C2)
                            self.defer(2, lambda g=g, b=b, vnb=vnb, ynb=ynb, gu_g=gu_g: self.sgu_B(g, b, gu_g, tmpf, vnb, wsm, ga, ynb))
                            self.defer(4, lambda g=g, b=b, ynb=ynb: self.transpose_block(ynb, ynb.ap, HG, yaT, b, None, c_off=g * HG, aux=True))
                    elif kind in ("q", "k"):
                        stg = qst[qi % 2]
                        qi += 1
                        for e in range(NB):
                            if e % 2 == 0:
                                self.op("act", lambda e=e, stg=stg, pss=pss, kind=kind: nc.scalar.activation(
                                    out=stg.ap[:, e, :], in_=pss[e].ap, func=AF.Copy,
                                    scale=(1.0 / math.sqrt(P)) if kind == "q" else 1.0), reads=[pss[e]], writes=[stg])
                            else:
                                self.op("dve", lambda e=e, stg=stg, pss=pss, kind=kind: nc.vector.tensor_scalar(
                                    out=stg.ap[:, e, :], in0=pss[e].ap, scalar1=(1.0 / math.sqrt(P)) if kind == "q" else 1.0,
                                    scalar2=None, op0=ALU.mult), reads=[pss[e]], writes=[stg])
                        if kind == "q":
                            dst = self.QT_d[g * NB:(g + 1) * NB, :, t0:t0 + TT]
                        else:
                            dst = self.KT_d[g * NB:(g + 1) * NB, :, tg:tg + TT]
                        self.dma("pool", [lambda stg=stg, dst=dst: nc.gpsimd.dma_start(out=dst.rearrange("h p t -> p h t"), in_=stg.ap)],
                                 stg.dsem, reads=[stg])
                    else:
                        stg = qst[qi % 2]
                        qi += 1
                        for b in range(NB):
                            if b % 2 == 0:
                                self.op("act", lambda b=b, stg=stg, pss=pss: nc.scalar.copy(out=stg.ap[:, b, :], in_=pss[b].ap),
                                        reads=[pss[b]], writes=[stg])
                            else:
                                self.op("dve", lambda b=b, stg=stg, pss=pss: nc.vector.tensor_copy(out=stg.ap[:, b, :], in_=pss[b].ap),
                                        reads=[pss[b]], writes=[stg])
                        dst = self.V_d[tg:tg + TT, g * TT:(g + 1) * TT].rearrange("(b p) n -> p b n", p=P)
                        self.dma("pool", [lambda stg=stg, dst=dst: nc.gpsimd.dma_start(out=dst, in_=stg.ap)], stg.dsem, reads=[stg])
                self.flush_deferred()
                if own:
                    dst = self.YT_d[0:NHA, :, t0:t0 + TT].rearrange("c p t -> p c t")
                    self.dma("pool", [lambda dst=dst: nc.gpsimd.dma_start(out=dst, in_=yaT.ap)], yaT.dsem, reads=[yaT])

    def stage12_end(self):
        self.cv_meter(len(self.cvq_in) + len(self.cvq), [self.cvq_in, self.cvq])
        self.cv_rate = 0

    def gelu(self, ps, dstbuf, dst_ap, tmpf, C2):
        nc = self.nc
        self.op("act", lambda: nc.scalar.activation(out=dst_ap, in_=ps.ap, func=AF.Gelu_apprx_tanh), reads=[ps], writes=[dstbuf])

    def defer(self, k, fn):
        self.deferred.append([k, fn])

    def tick(self):
        due = []
        for item in self.deferred:
            item[0] -= 1
            if item[0] <= 0:
                due.append(item)
        self.deferred = [it for it in self.deferred if it[0] > 0]
        for it in due:
            it[1]()

    def flush_deferred(self):
        while self.deferred:
            self.tick()

    def sgu_A(self, gv, g, b, tmpf, vnb, C2):
        nc = self.nc
        sq = tmpf[3]
        st = self.stat[self.stat_i % len(self.stat)]
        self.stat_i += 1
        HG = TT // P
        gv3 = gv.ap.rearrange("p (h d) -> p h d", h=HG)
        sq3 = sq.ap.rearrange("p (h d) -> p h d", h=HG)
        self.op("dve", lambda: nc.vector.tensor_reduce(out=st.ap[:, 0:HG], in_=gv3, axis=AX.X, op=ALU.add), reads=[gv], writes=[st])
        self.op("pool", lambda: nc.gpsimd.tensor_tensor(out=sq.ap, in0=gv.ap, in1=gv.ap, op=ALU.mult), reads=[gv], writes=[sq])
        self.op("dve", lambda: nc.vector.tensor_reduce(out=st.ap[:, 4:4 + HG], in_=sq3, axis=AX.X, op=ALU.add), reads=[sq], writes=[st])
        self.op("dve", lambda: nc.vector.tensor_scalar(out=st.ap[:, 0:HG], in0=st.ap[:, 0:HG], scalar1=1.0 / P, scalar2=None,
                                                       op0=ALU.mult), reads=[st], writes=[st])
        st2 = self.stat[self.stat_i % len(self.stat)]
        self.stat_i += 1
        self.op("dve", lambda: nc.vector.tensor_tensor(out=st2.ap[:, 0:HG], in0=st.ap[:, 0:HG], in1=st.ap[:, 0:HG], op=ALU.mult),
                reads=[st], writes=[st2])
        self.op("dve", lambda: nc.vector.scalar_tensor_tensor(out=st.ap[:, 4:4 + HG], in0=st.ap[:, 4:4 + HG], scalar=1.0 / P,
                                                              in1=st2.ap[:, 0:HG], op0=ALU.mult, op1=ALU.subtract),
                reads=[st, st2], writes=[st])
        self.op("dve", lambda: nc.vector.tensor_scalar(out=st.ap[:, 4:4 + HG], in0=st.ap[:, 4:4 + HG], scalar1=EPS, scalar2=None,
                                                       op0=ALU.add), reads=[st], writes=[st])
        self.rsqrt(st, st.ap[:, 4:4 + HG], st.ap[:, 4:4 + HG])
        for h in range(HG):
            self.op("dve", lambda h=h: nc.vector.tensor_scalar(out=vnb.ap[:, h * P:(h + 1) * P], in0=gv3[:, h, :],
                                                               scalar1=st.ap[:, h:h + 1], scalar2=st.ap[:, 4 + h:5 + h],
                                                               op0=ALU.subtract, op1=ALU.mult), reads=[gv, st], writes=[vnb])

    def sgu_B(self, g, b, gu, tmpf, vnb, wsm, ga, yn):
        nc = self.nc
        sq, ya = tmpf[3], tmpf[4]
        HG = TT // P
        sq3 = sq.ap.rearrange("p (h d) -> p h d", h=HG)
        pm = self.psum_aux()

        def mix():
            for h in range(HG):
                ins = nc.tensor.matmul(pm.ap[:, h * P:(h + 1) * P], lhsT=wsm.ap[:, g * HG + h, :], rhs=vnb.ap[:, h * P:(h + 1) * P],
                                       start=True, stop=True)
            return ins
        self.op("pe", mix, reads=[wsm, vnb], writes=[pm])
        for h in range(HG):
            hh = g * HG + h
            self.op("dve", lambda h=h, hh=hh: nc.vector.scalar_tensor_tensor(
                out=ya.ap[:, h * P:(h + 1) * P], in0=pm.ap[:, h * P:(h + 1) * P], scalar=self.bsT[:, hh:hh + 1],
                in1=gu.ap[:, b, h * P:(h + 1) * P], op0=ALU.add, op1=ALU.mult), reads=[pm, gu, self.cst], writes=[ya])
        st3 = self.stat[self.stat_i % len(self.stat)]
        self.stat_i += 1
        self.op("pool", lambda: nc.gpsimd.tensor_tensor(out=sq.ap, in0=ya.ap, in1=ya.ap, op=ALU.mult), reads=[ya], writes=[sq])
        self.op("dve", lambda: nc.vector.tensor_reduce(out=st3.ap[:, 0:HG], in_=sq3, axis=AX.X, op=ALU.add), reads=[sq], writes=[st3])
        self.op("dve", lambda: nc.vector.tensor_scalar(out=st3.ap[:, 0:HG], in0=st3.ap[:, 0:HG], scalar1=1.0 / P, scalar2=EPS,
                                                       op0=ALU.mult, op1=ALU.add), reads=[st3], writes=[st3])
        self.rsqrt(st3, st3.ap[:, 0:HG], st3.ap[:, 0:HG])
        for h in range(HG):
            hh = g * HG + h
            self.op("dve", lambda h=h, hh=hh: nc.vector.scalar_tensor_tensor(
                out=yn.ap[:, h * P:(h + 1) * P], in0=ya.ap[:, h * P:(h + 1) * P], scalar=st3.ap[:, h:h + 1],
                in1=ga.ap[:, hh * P:(hh + 1) * P], op0=ALU.mult, op1=ALU.mult), reads=[ya, st3, ga], writes=[yn])

    def stage3(self):
        nc, cfg = self.nc, self.cfg
        T, TP, BW = cfg["T"], cfg["TP"], cfg["BW"]
        NHB = BW // P
        NHA = cfg["AW"] // P
        TK = TP + T
        NKB = TK // P
        nqt = T // TT
        NS = 2 if NHB % 2 == 0 else 1
        with ExitStack() as st:
            sb = lambda n, s, d: self.sb(n, s, d, st)
            S = []
            for x in range(NS):
                d = dict(
                    KTh=[Buf(sb("KTh%d_%d" % (x, i), [P, TK], BF16), self.newsem("d_KTh%d_%d" % (x, i))) for i in range(2)],
                    Vh=[Buf(sb("Vh%d_%d" % (x, i), [P, NKB, P], BF16), self.newsem("d_Vh%d_%d" % (x, i))) for i in range(2)],
                    QTh=[Buf(sb("QTh%d_%d" % (x, i), [P, T], BF16), self.newsem("d_QTh%d_%d" % (x, i))) for i in range(2)],
                    Eb=[Buf(sb("E%d_%d" % (x, i), [P, TT], F32)) for i in range(3)],
                    Lb=[Buf(sb("L%d_%d" % (x, i), [P, TT], BF16)) for i in range(3)],
                    Gb=[Buf(sb("G%d_%d" % (x, i), [P, TT], F32)) for i in range(2)],
                    Ab=[Buf(sb("A%d_%d" % (x, i), [P, TT], BF16)) for i in range(2)],
                    osb=Buf(sb("osb%d" % x, [P, TT], F32)), osq=Buf(sb("osq%d" % x, [P, TT], F32)), rsd=Buf(sb("rsd%d" % x, [P, TT], F32)),
                    ybs=[Buf(sb("ybs%d_%d" % (x, i), [P, TT], BF16), self.newsem("d_ybs%d_%d" % (x, i))) for i in range(2)],
                    zps=[self.ps[4 * x], self.ps[4 * x + 1]], cp=self.ps[4 * x + 2], op=self.ps[4 * x + 3],
                    heads=list(range(x, NHB, NS)), loaded=set(), its=[], yi=0)
                for hi, h in enumerate(d["heads"]):
                    for qt in range(nqt):
                        kbs = list(range(TP // P + (qt + 1) * NB - 1, -1, -1))
                        for n, kb in enumerate(kbs):
                            j = kb - (TP // P + qt * NB)
                            d["its"].append(dict(h=h, hi=hi, qt=qt, kb=kb, first=(n == 0), last=(n == len(kbs) - 1),
                                                 j=j if j >= 0 else None))
                S.append(d)
            NI = len(S[0]["its"])

            def load_head(d, hi):
                h = d["heads"][hi]
                i = hi % 2
                K_, V_, Q_ = d["KTh"][i], d["Vh"][i], d["QTh"][i]
                self.dma("sp", [lambda: nc.sync.dma_start(out=K_.ap, in_=self.KT_d[h])], K_.dsem, writes=[K_])
                self.dma("sp", [lambda: nc.sync.dma_start(out=V_.ap, in_=self.V_d[:, h * P:(h + 1) * P].rearrange("(n p) d -> p n d", p=P))],
                         V_.dsem, writes=[V_])
                self.dma("sp", [lambda: nc.sync.dma_start(out=Q_.ap, in_=self.QT_d[h])], Q_.dsem, writes=[Q_])
                d["loaded"].add(hi)

            def zT(d, i):
                it = d["its"][i]
                hi, qt, kb = it["hi"], it["qt"], it["kb"]
                if it["first"] and qt == 0 and (hi + 1) < len(d["heads"]) and (hi + 1) not in d["loaded"]:
                    load_head(d, hi + 1)
                z = d["zps"][i % 2]
                k_, q_ = d["KTh"][hi % 2], d["QTh"][hi % 2]
                self.op("pe", lambda: nc.tensor.matmul(z.ap, lhsT=k_.ap[:, kb * P:(kb + 1) * P], rhs=q_.ap[:, qt * TT:(qt + 1) * TT],
                                                       start=True, stop=True), reads=[k_, q_], writes=[z])

            def E(d, i):
                z, e = d["zps"][i % 2], d["Eb"][i % 3]
                self.op("act", lambda: nc.scalar.activation(out=e.ap, in_=z.ap, func=AF.Exp), reads=[z], writes=[e])

            def L(d, i):
                it = d["its"][i]
                e, l = d["Eb"][i % 3], d["Lb"][i % 3]
                self.op("act", lambda: nc.scalar.activation(out=l.ap, in_=e.ap, func=AF.Ln, bias=1.0), reads=[e], writes=[l])
                if it["j"] is not None:
                    j = it["j"]
                    self.op("dve", lambda: nc.vector.tensor_tensor(out=l.ap, in0=l.ap, in1=self.dmask[:, j, :], op=ALU.mult),
                            reads=[l, self.cbf], writes=[l])

            def Tinc(d, i):
                it = d["its"][i]
                cp, l = d["cp"], d["Lb"][i % 3]
                self.op("pe", lambda: nc.tensor.matmul(cp.ap, lhsT=self.tinc, rhs=l.ap, start=it["first"], stop=True,
                                                       skip_group_check=True), reads=[l, self.cbf], writes=[cp])

            def G(d, i):
                cp, g = d["cp"], d["Gb"][i % 2]
                self.op("act", lambda: nc.scalar.activation(out=g.ap, in_=cp.ap, func=AF.Exp, scale=-1.0), reads=[cp], writes=[g])

            def Tcomp(d, i):
                it = d["its"][i]
                if it["last"]:
                    return
                cp, l = d["cp"], d["Lb"][i % 3]
                self.op("pe", lambda: nc.tensor.matmul(cp.ap, lhsT=self.tcomp, rhs=l.ap, start=False, stop=True,
                                                       skip_group_check=True), reads=[l, self.cbf], writes=[cp])

            def A(d, i):
                it = d["its"][i]
                e, g, a = d["Eb"][i % 3], d["Gb"][i % 2], d["Ab"][i % 2]
                if it["j"] is not None:
                    j = it["j"]
                    self.op("dve", lambda: nc.vector.tensor_tensor(out=g.ap, in0=g.ap, in1=self.dmask[:, j, :], op=ALU.mult),
                            reads=[g, self.cbf], writes=[g])
                self.op("dve", lambda: nc.vector.tensor_tensor(out=a.ap, in0=e.ap, in1=g.ap, op=ALU.mult), reads=[e, g], writes=[a])

            def AV(d, i):
                it = d["its"][i]
                h, hi, qt, kb = it["h"], it["hi"], it["qt"], it["kb"]
                a, v_ = d["Ab"][i % 2], d["Vh"][hi % 2]
                op_ = d["op"]
                self.op("pe", lambda: nc.tensor.matmul(op_.ap, lhsT=v_.ap[:, kb, :], rhs=a.ap, start=it["first"], stop=it["last"]),
                        reads=[v_, a], writes=[op_])
                if it["last"]:
                    post(d, h, qt)

            def post(d, h, qt):
                op_, osb, osq, rsd = d["op"], d["osb"], d["osq"], d["rsd"]
                yb = d["ybs"][d["yi"] % 2]
                d["yi"] += 1
                self.op("act", lambda: nc.scalar.copy(out=osb.ap, in_=op_.ap), reads=[op_], writes=[osb])
                self.op("act", lambda: nc.scalar.activation(out=osq.ap, in_=op_.ap, func=AF.Square), reads=[op_], writes=[osq])
                self.op("pe", lambda: nc.tensor.matmul(op_.ap, lhsT=self.onesf, rhs=osq.ap, start=True, stop=True),
                        reads=[osq, self.cst], writes=[op_])
                self.op("dve", lambda: nc.vector.tensor_scalar(out=rsd.ap, in0=op_.ap, scalar1=1.0 / P, scalar2=EPS,
                                                               op0=ALU.mult, op1=ALU.add), reads=[op_], writes=[rsd])
                self.op("act", lambda: nc.scalar.activation(out=rsd.ap, in_=rsd.ap, func=AF.Ln), reads=[rsd], writes=[rsd])
                self.op("act", lambda: nc.scalar.activation(out=rsd.ap, in_=rsd.ap, func=AF.Exp, scale=-0.5), reads=[rsd], writes=[rsd])
                self.op("dve", lambda: nc.vector.scalar_tensor_tensor(out=yb.ap, in0=osb.ap, scalar=self.g_b[:, h:h + 1], in1=rsd.ap,
                                                                      op0=ALU.mult, op1=ALU.mult), reads=[osb, rsd, self.cst], writes=[yb])
                dst = self.YT_d[NHA + h, :, qt * TT:(qt + 1) * TT]
                self.dma("sp", [lambda: nc.sync.dma_start(out=dst, in_=yb.ap)], yb.dsem, reads=[yb])

            for d in S:
                load_head(d, 0)
            ncv = len(self.cvq3)
            cv_every = max(1, (NI * 9 // 10) // max(1, ncv))
            for d in S:
                zT(d, 0)
            for d in S:
                E(d, 0)
            for d in S:
                L(d, 0)
            if NI > 1:
                for d in S:
                    zT(d, 1)
            for d in S:
                Tinc(d, 0)
            for s_ in range(NI):
                if s_ % cv_every == 0:
                    self.cv_meter(1, [self.cvq3])
                if s_ + 2 < NI:
                    for d in S:
                        zT(d, s_ + 2)
                if s_ + 1 < NI:
                    for d in S:
                        E(d, s_ + 1)
                for d in S:
                    G(d, s_)
                if s_ + 1 < NI:
                    for d in S:
                        L(d, s_ + 1)
                for d in S:
                    Tcomp(d, s_)
                if s_ + 1 < NI:
                    for d in S:
                        Tinc(d, s_ + 1)
                for d in S:
                    A(d, s_)
                for d in S:
                    AV(d, s_)
            self.cv_meter(len(self.cvq3), [self.cvq3])

    def stage4(self):
        nc, cfg = self.nc, self.cfg
        T, D, DFF, PLE, KC, NQ = cfg["T"], cfg["D"], cfg["DFF"], cfg["PLE"], cfg["KC"], cfg["NQ"]
        NCH = self.NCH
        nkt = NCH // KC
        FQ = DFF // NQ
        FQC = FQ // P
        nkq = FQC // KC
        NE = D // TT
        NPC = PLE // P
        ntile = T // TT
        with ExitStack() as st:
            sb = lambda n, s, d: self.sb(n, s, d, st)
            hb = [Buf(sb("h%d" % b, [P, D], F32), self.newsem("d_h%d" % b)) for b in range(NB)]
            actT = Buf(sb("actT", [P, NCH, TT], BF16), self.newsem("d_actT"))
            hidT = Buf(sb("hidT", [P, FQC, TT], BF16))
            xq = [Buf(sb("xq%d" % i, [P, D // 4], F32)) for i in range(4)]
            gfin = Buf(sb("gfin", [P, D], F32), self.newsem("d_gfin"))
            pt = Buf(sb("pt", [P, PLE], F32), self.newsem("d_pt"))
            pT = Buf(sb("pT", [P, NPC, TT], BF16))
            tf = [Buf(sb("t4_%d" % i, [P, TT], F32)) for i in range(2)]
            self.dma("sp", [lambda: nc.sync.dma_start(out=gfin.ap, in_=self.gfin_d)], gfin.dsem, writes=[gfin])
            plan = []
            for ti in range(ntile):
                d = {}
                d["out"] = [[self.wq_add("w_out", k * KC * P, KC * P, e * TT, TT) for k in range(nkt)] for e in range(NE)]
                d["ffn"] = []
                for q in range(NQ):
                    ups = [[self.wq_add("w_up", k * KC * P, KC * P, q * FQ + fg * TT, TT) for k in range(nkt)] for fg in range(FQ // TT)]
                    dns = [[self.wq_add("w_down", q * FQ + k * KC * P, KC * P, e * TT, TT) for k in range(nkq)] for e in range(NE)]
                    d["ffn"].append((ups, dns))
                d["gate"] = [([self.wq_add("w_gate", k * KC * P, KC * P, e * TT, TT) for k in range(nkt)],
                              self.wq_add("w_ple", 0, PLE, e * TT, TT)) for e in range(NE)]
                plan.append(d)

            def load_yT(ti_):
                half = max(1, NCH // 2)
                self.dma("sp", [lambda c0=c0: nc.sync.dma_start(out=actT.ap[:, c0:c0 + half, :],
                                                                in_=self.YT_d[c0:c0 + half, :, ti_ * TT:(ti_ + 1) * TT].rearrange("c p t -> p c t"))
                                for c0 in range(0, NCH, half)], actT.dsem, writes=[actT])

            ei = 0
            for ti in range(ntile):
                t0 = ti * TT
                d = plan[ti]
                for b in range(NB):
                    self.dma("sp", [lambda b=b: nc.sync.dma_start(out=hb[b].ap, in_=self.x_own[t0 + b * P:t0 + (b + 1) * P, :])],
                             hb[b].dsem, writes=[hb[b]])
                if ti == 0:
                    load_yT(0)
                for e in range(NE):
                    pss = self.proj(d["out"][e], actT, NCH, "tok")
                    for b in range(NB):
                        self.op("dve", lambda b=b, e=e, pss=pss: nc.vector.tensor_tensor(
                            out=hb[b].ap[:, e * TT:(e + 1) * TT], in0=hb[b].ap[:, e * TT:(e + 1) * TT], in1=pss[b].ap, op=ALU.add),
                            reads=[pss[b], hb[b]], writes=[hb[b]])
                jflat = hidT.ap.rearrange("p c t -> p (c t)")[:, 0:D // 2]
                sts = [self.norm_stats(hb[b], [hidT, hidT], [jflat, jflat]) for b in range(NB)]
                self.norm_blocks_q(hb, self.g_ffn, actT, xq, sts)
                for q in range(NQ):
                    ups, dns = d["ffn"][q]
                    for fg, widxs in enumerate(ups):
                        pss = self.proj(widxs, actT, NCH, "feat")
                        for fc in range(NB):
                            t = tf[ei % 2]
                            ei += 1
                            self.op("act", lambda t=t, fc=fc, pss=pss: nc.scalar.activation(out=t.ap, in_=pss[fc].ap, func=AF.Relu),
                                    reads=[pss[fc]], writes=[t])
                            o = hidT.ap[:, fg * NB + fc, :]
                            eng = "dve" if (fc % 2 == 0) else "pool"
                            if eng == "dve":
                                self.op("dve", lambda t=t, o=o: nc.vector.tensor_tensor(out=o, in0=t.ap, in1=t.ap, op=ALU.mult),
                                        reads=[t], writes=[hidT])
                            else:
                                self.op("pool", lambda t=t, o=o: nc.gpsimd.tensor_tensor(out=o, in0=t.ap, in1=t.ap, op=ALU.mult),
                                        reads=[t], writes=[hidT])
                    for e, widxs in enumerate(dns):
                        pss = self.proj(widxs, hidT, FQC, "tok")
                        for b in range(NB):
                            self.op("dve", lambda b=b, e=e, pss=pss: nc.vector.tensor_tensor(
                                out=hb[b].ap[:, e * TT:(e + 1) * TT], in0=hb[b].ap[:, e * TT:(e + 1) * TT], in1=pss[b].ap, op=ALU.add),
                                reads=[pss[b], hb[b]], writes=[hb[b]])
                sts = [self.norm_stats(hb[b], [hidT, hidT], [jflat, jflat]) for b in range(NB)]
                self.norm_blocks_q(hb, self.g_ple, actT, xq, sts)
                for b in range(NB):
                    self.dma("sp", [lambda b=b: nc.sync.dma_start(out=pt.ap, in_=self.p_own[t0 + b * P:t0 + (b + 1) * P, :])],
                             pt.dsem, writes=[pt])
                    self.transpose_block(pt, pt.ap, NPC, pT, b, None)
                for e in range(NE):
                    pss = self.proj(d["gate"][e][0], actT, NCH, "tok")
                    wple = self.wq_get(d["gate"][e][1])
                    for b in range(NB):
                        pe_ = self.psum_aux()

                        def mm(b=b, e=e, pe_=pe_, wple=wple):
                            for k in range(NPC):
                                ins = nc.tensor.matmul(pe_.ap, lhsT=pT.ap[:, k, b * P:(b + 1) * P], rhs=wple.ap[:, k, :],
                                                       start=(k == 0), stop=(k == NPC - 1))
                            return ins
                        self.op("pe", mm, reads=[pT, wple], writes=[pe_])
                        t = tf[ei % 2]
                        ei += 1
                        self.op("act", lambda t=t, b=b, pss=pss: nc.scalar.activation(out=t.ap, in_=pss[b].ap, func=AF.Sigmoid),
                                reads=[pss[b]], writes=[t])
                        self.op("dve", lambda t=t, pe_=pe_: nc.vector.tensor_tensor(out=t.ap, in0=t.ap, in1=pe_.ap, op=ALU.mult),
                                reads=[t, pe_], writes=[t])
                        self.op("pool", lambda t=t, b=b, e=e: nc.gpsimd.tensor_tensor(
                            out=hb[b].ap[:, e * TT:(e + 1) * TT], in0=hb[b].ap[:, e * TT:(e + 1) * TT], in1=t.ap, op=ALU.add),
                            reads=[t, hb[b]], writes=[hb[b]])
                if ti + 1 < ntile:
                    load_yT(ti + 1)
                for b in range(NB):
                    stt = self.stat[self.stat_i % len(self.stat)]
                    self.stat_i += 1
                    H2 = D // 2
                    self.op("dve", lambda stt=stt: nc.vector.memset(stt.ap[:, 0:8], 0.0), writes=[stt])
                    for hf in range(2):
                        self.op("act", lambda stt=stt, b=b, hf=hf: nc.scalar.activation(
                            out=jflat, in_=hb[b].ap[:, hf * H2:(hf + 1) * H2], func=AF.Square,
                            accum_out=stt.ap[:, 4 + hf:5 + hf]), reads=[hb[b]], writes=[hidT, stt])
                    self.op("dve", lambda stt=stt: nc.vector.tensor_tensor(out=stt.ap[:, 0:1], in0=stt.ap[:, 4:5], in1=stt.ap[:, 5:6],
                                                                           op=ALU.add), reads=[stt], writes=[stt])
                    self.op("dve", lambda stt=stt: nc.vector.tensor_scalar(out=stt.ap[:, 1:2], in0=stt.ap[:, 0:1], scalar1=1.0 / D,
                                                                           scalar2=EPS, op0=ALU.mult, op1=ALU.add), reads=[stt], writes=[stt])
                    self.rsqrt(stt, stt.ap[:, 2:3], stt.ap[:, 1:2])
                    self.op("dve", lambda stt=stt, b=b: nc.vector.scalar_tensor_tensor(
                        out=hb[b].ap, in0=hb[b].ap, scalar=stt.ap[:, 2:3], in1=gfin.ap, op0=ALU.mult, op1=ALU.mult),
                        reads=[hb[b], stt, gfin], writes=[hb[b]])
                    self.dma("pool", [lambda b=b: nc.gpsimd.dma_start(out=self.out_d[t0 + b * P:t0 + (b + 1) * P, :], in_=hb[b].ap)],
                             hb[b].dsem, reads=[hb[b]])


def make_consts(cfg, norm_mix, norm_ffn, norm_ple, norm_b_out, b_s, norm_a_out, norm_final, w_s):
    D, AW, BW = cfg["D"], cfg["AW"], cfg["BW"]
    NCH, NHA, NHB = D // P, AW // P, BW // P
    ar = np.arange(P)
    ident = np.eye(P, dtype=np.float32)
    ones = np.ones((P, P), np.float32)
    trilT = (ar[None, :] >= ar[:, None]).astype(np.float32)
    fm = lambda g, n: np.ascontiguousarray(np.asarray(g, np.float32).reshape(n, P).T)
    cst = np.concatenate([ident, ones, trilT, fm(norm_mix, NCH), fm(norm_ffn, NCH), fm(norm_ple, NCH),
                          fm(norm_b_out, NHB), np.ascontiguousarray(np.asarray(b_s, np.float32).T)], axis=1)
    tinc = (ar[:, None] >= ar[None, :]).astype(np.float32)
    tcomp = 1.0 - tinc
    q = np.arange(TT)
    dm = np.stack([((j * P + ar[:, None]) < q[None, :]).astype(np.float32) for j in range(NB)], axis=1)
    cbf = np.concatenate([tinc, tcomp, dm.reshape(P, NB * TT)], axis=1).astype(ml_dtypes.bfloat16)
    ga_rep = np.ascontiguousarray(np.broadcast_to(np.asarray(norm_a_out, np.float32)[None, :], (P, AW)))
    gfin_rep = np.ascontiguousarray(np.broadcast_to(np.asarray(norm_final, np.float32)[None, :], (P, D)))
    wsT = np.ascontiguousarray(np.transpose(np.asarray(w_s, np.float32), (2, 0, 1)).reshape(P, NHA * P))
    return dict(cst=np.ascontiguousarray(cst), cbf=np.ascontiguousarray(cbf), ga_rep=ga_rep, gfin_rep=gfin_rep, wsT=wsT)


def kernel(x, p, norm_mix, w_in, w_s, b_s, norm_a_out, norm_b_out, w_out, norm_ffn, w_up, w_down,
           norm_ple, w_ple_gate, w_ple_proj, norm_final):
    cfg = FULL
    T, TP, D = cfg["T"], cfg["TP"], cfg["D"]
    x = np.asarray(x, np.float32)
    p = np.asarray(p, np.float32)
    B, S, _ = x.shape
    consts = make_consts(cfg, norm_mix[0], norm_ffn[0], norm_ple[0], norm_b_out[0], b_s[0], norm_a_out[0], norm_final, w_s[0])
    shared = dict(w_in=np.ascontiguousarray(w_in[0], dtype=np.float32), w_out=np.ascontiguousarray(w_out[0], dtype=np.float32),
                  w_up=np.ascontiguousarray(w_up[0], dtype=np.float32), w_down=np.ascontiguousarray(w_down[0], dtype=np.float32),
                  w_gate=np.ascontiguousarray(w_ple_gate[0], dtype=np.float32),
                  w_ple=np.ascontiguousarray(w_ple_proj[0], dtype=np.float32), **consts)
    zeros = np.zeros((TP, D), np.float32)
    in_maps = []
    for core in range(8):
        b, s = core // 2, core % 2
        m = dict(shared)
        m["x_own"] = np.ascontiguousarray(x[b, s * T:(s + 1) * T])
        m["x_pre"] = np.ascontiguousarray(x[b, 0:TP]) if s == 1 else zeros
        m["p_own"] = np.ascontiguousarray(p[0, b, s * T:(s + 1) * T])
        in_maps.append(m)
    nc = K(cfg).build()
    res = run_bass_kernel_spmd(nc, in_maps, core_ids=list(range(8)))
    out = np.empty((B, S, D), np.float32)
    for core in range(8):
        b, s = core // 2, core % 2
        out[b, s * T:(s + 1) * T] = np.asarray(res.results[core]["out"], np.float32)
    return out
```

```python
import math
from contextlib import ExitStack

import numpy as np
import ml_dtypes

import concourse.bass as bass
import concourse.mybir as mybir
from concourse.bass_utils import run_bass_kernel_spmd

F32 = mybir.dt.float32
BF16 = mybir.dt.bfloat16
AF = mybir.ActivationFunctionType
ALU = mybir.AluOpType
AX = mybir.AxisListType

EPS = 1e-6
P = 128
TT = 512
NB = TT // P

FULL = dict(T=2048, TP=2048, D=4096, AW=2048, BW=2048, DFF=16384, PLE=256, KC=16, NQ=8)


class Sem:
    _id = 0

    def __init__(self, nc, name):
        self.h = nc.alloc_semaphore(name)
        self.n = 0
        Sem._id += 1
        self.id = Sem._id


class Buf:
    def __init__(self, ap=None, dsem=None):
        self.ap = ap
        self.w = None
        self.r = {}
        self.dsem = dsem


class K:
    def __init__(self, cfg):
        self.cfg = cfg
        self.nc = bass.Bass("TRN2", target_bir_lowering=False)
        nc = self.nc
        self.eng = {"pe": nc.tensor, "act": nc.scalar, "dve": nc.vector, "pool": nc.gpsimd, "sp": nc.sync}
        self.prog = {e: Sem(nc, "p_" + e) for e in ["pe", "act", "dve", "pool"]}
        self.deferred = []
        self.waited = {e: {} for e in self.eng}
        self.allsems = list(self.prog.values())
        self.ps_i = 0
        self.cv_rate = 0
        self.cv_hist = []
        self.cv_boost = [0, 0]
        self.cv_queues = []
        self.pp_i = 0
        self.pa_i = 0

    def newsem(self, name):
        s = Sem(self.nc, name)
        self.allsems.append(s)
        return s

    def _wait(self, e, deps, own_ok):
        for (s, v) in deps:
            if own_ok and e in self.prog and s is self.prog[e]:
                continue
            if self.waited[e].get(s.id, 0) >= v:
                continue
            self.eng[e].wait_ge(s.h, v)
            self.waited[e][s.id] = v

    def _deps(self, reads, writes):
        deps = []
        for b in reads:
            if b.w:
                deps.append(b.w)
        for b in writes:
            if b.w:
                deps.append(b.w)
            deps.extend(b.r.values())
        return deps

    def _mark(self, ev, reads, writes):
        for b in reads:
            b.r[ev[0].id] = ev
        for b in writes:
            b.w = ev
            b.r = {}

    def op(self, e, fn, reads=(), writes=()):
        self._wait(e, self._deps(reads, writes), e == "pe")
        ins = fn()
        s = self.prog[e]
        s.n += 1
        ins.then_inc(s.h, 1)
        self._mark((s, s.n), reads, writes)
        return ins

    def dma(self, q, fns, sem, reads=(), writes=()):
        self._wait(q, self._deps(reads, writes), False)
        for fn in fns:
            ins = fn()
            ins.then_inc(sem.h, 16)
            sem.n += 16
        self._mark((sem, sem.n), reads, writes)

    def barrier(self):
        evs = [(s, s.n) for s in self.allsems if s.n > 0]
        for e in self.eng:
            self._wait(e, evs, False)

    def cv_meter(self, k, queues):
        todo = []
        for q in queues:
            while q and len(todo) < k:
                todo.append(q.pop(0))
        if not todo:
            return
        self._wait("pool", [(self.prog["pe"], self.prog["pe"].n)], False)
        for (fns, s, b) in todo:
            if len(self.cv_hist) >= 3:
                self._wait("pool", [self.cv_hist[-3]], False)
            self.dma("pool", fns, s, writes=[b])
            self.cv_hist.append((s, s.n))

    def cv_emit(self, k):
        for _ in range(min(k, len(self.cvq))):
            fns, s, b = self.cvq.pop(0)
            self.dma("pool", fns, s, writes=[b])

    def ensure_conv(self, bf):
        for q in (self.cvq_in, self.cvq, self.cvq3):
            for item in [it for it in q if it[2] is bf]:
                q.remove(item)
                self.dma("pool", item[0], item[1], writes=[item[2]])
        assert bf.w is not None

    def psum(self):
        b = self.ps[self.ps_i % 8]
        self.ps_i += 1
        return b

    def psum_proj(self):
        b = self.ps[self.pp_i % 6]
        self.pp_i += 1
        return b

    def psum_aux(self):
        b = self.ps[6 + self.pa_i % 2]
        self.pa_i += 1
        return b

    def rsqrt(self, buf, out_ap, in_ap):
        nc = self.nc
        self.op("act", lambda: nc.scalar.activation(out=out_ap, in_=in_ap, func=AF.Sqrt), reads=[buf], writes=[buf])
        self.op("dve", lambda: nc.vector.reciprocal(out=out_ap, in_=out_ap), reads=[buf], writes=[buf])

    def build(self):
        nc, cfg = self.nc, self.cfg
        T, TP, D, AW, BW, DFF, PLE, KC, NQ = (cfg[k] for k in ["T", "TP", "D", "AW", "BW", "DFF", "PLE", "KC", "NQ"])
        self.NCH = NCH = D // P
        NHA, NHB = AW // P, BW // P
        DIN = 2 * AW + 3 * BW
        TK = TP + T
        self.NCS = NCS = 3 * P + 3 * NCH + NHB + NHA
        dt = nc.dram_tensor
        self.x_own = dt("x_own", [T, D], F32, kind="ExternalInput").ap()
        self.x_pre = dt("x_pre", [TP, D], F32, kind="ExternalInput").ap()
        self.p_own = dt("p_own", [T, PLE], F32, kind="ExternalInput").ap()
        wnames = dict(w_in=(D, DIN), w_out=(D, D), w_up=(D, DFF), w_down=(DFF, D), w_gate=(D, D), w_ple=(PLE, D))
        self.wf = {n: dt(n, list(s), F32, kind="ExternalInput").ap() for n, s in wnames.items()}
        self.cst_d = dt("cst", [P, NCS], F32, kind="ExternalInput").ap()
        self.cbf_d = dt("cbf", [P, 2 * P + NB * TT], BF16, kind="ExternalInput").ap()
        self.ga_d = dt("ga_rep", [P, AW], F32, kind="ExternalInput").ap()
        self.gfin_d = dt("gfin_rep", [P, D], F32, kind="ExternalInput").ap()
        self.wsT_d = dt("wsT", [P, NHA * P], F32, kind="ExternalInput").ap()
        self.out_d = dt("out", [T, D], F32, kind="ExternalOutput").ap()
        self.wb = {n: dt("b_" + n, [s[1] // TT, P, s[0] // P, TT], BF16, kind="Internal").ap() for n, s in wnames.items()}
        sk = "ExternalOutput" if cfg.get("debug") else "Internal"
        self.KT_d = dt("KT_s", [NHB, P, TK], BF16, kind=sk).ap()
        self.V_d = dt("V_s", [TK, BW], BF16, kind=sk).ap()
        self.QT_d = dt("QT_s", [NHB, P, T], BF16, kind=sk).ap()
        self.YT_d = dt("YT_s", [NCH, P, T], BF16, kind=sk).ap()

        self.ps = [Buf(nc.alloc_psum_tensor("ps%d" % i, [P, TT], F32).ap()) for i in range(8)]

        self.wev = {}
        self.cvq = []
        self.cvq_in = []
        kv0 = (2 * AW + BW) // TT
        GA_ = AW // TT
        self.cvq3 = []
        CP = 8
        for n in ["w_in", "w_out", "w_up", "w_down", "w_gate", "w_ple"]:
            rows, cols = wnames[n]
            ng, kch = cols // TT, rows // P
            order = list(range(ng))
            if n == "w_in":
                rest = []
                for g in range(GA_):
                    rest += [g, GA_ + g]
                rest += list(range(2 * GA_, kv0))
                order = list(range(kv0, ng)) + rest
            cls = {}
            for g in order:
                key = g if n == "w_in" else "all"
                if key not in cls:
                    cls[key] = (self.newsem("cv_%s_%s" % (n, key)), Buf())
                s, bf = cls[key]
                self.wev[(n, g)] = bf
                for c0 in range(0, kch, CP):
                    c1 = min(kch, c0 + CP)
                    fn = (lambda n=n, g=g, c0=c0, c1=c1: nc.gpsimd.dma_start(
                        out=self.wb[n][g, :, c0:c1, :],
                        in_=self.wf[n][c0 * P:c1 * P, g * TT:(g + 1) * TT].rearrange("(c p) n -> p c n", p=P)))
                    if n == "w_in" and kv0 <= g < kv0 + (ng - kv0) // 2:
                        self.dma("pool", [fn], s, writes=[bf])
                    elif n == "w_in":
                        self.cvq_in.append(([fn], s, bf))
                    elif n == "w_out" or (n == "w_up" and g < (14 * ng) // 32):
                        self.cvq.append(([fn], s, bf))
                    else:
                        self.cvq3.append(([fn], s, bf))
        self.cv_total = len(self.cvq)

        with ExitStack() as glob:
            def sb(name, shape, dtype, stack=glob):
                return stack.enter_context(nc.sbuf_tensor("s_" + name, shape, dtype)).ap()
            self.sb = sb
            cs = self.newsem("d_cst")
            self.cst = Buf(sb("cst", [P, NCS], F32), cs)
            self.cbf = Buf(sb("cbf", [P, 2 * P + NB * TT], BF16), cs)
            self.dma("sp", [lambda: nc.sync.dma_start(out=self.cst.ap, in_=self.cst_d),
                            lambda: nc.sync.dma_start(out=self.cbf.ap, in_=self.cbf_d)], cs,
                     writes=[self.cst, self.cbf])
            c = self.cst.ap
            self.ident = c[:, 0:P]
            self.onesf = c[:, P:2 * P]
            self.trilT = c[:, 2 * P:3 * P]
            o = 3 * P
            self.g_mix = c[:, o:o + NCH]
            self.g_ffn = c[:, o + NCH:o + 2 * NCH]
            self.g_ple = c[:, o + 2 * NCH:o + 3 * NCH]
            o += 3 * NCH
            self.g_b = c[:, o:o + NHB]
            self.bsT = c[:, o + NHB:o + NHB + NHA]
            cb = self.cbf.ap
            self.tinc = cb[:, 0:P]
            self.tcomp = cb[:, P:2 * P]
            self.dmask = cb[:, 2 * P:].rearrange("p (j q) -> p j q", j=NB)
            self.stat = [Buf(sb("stat%d" % i, [P, 8], F32)) for i in range(8)]
            self.stat_i = 0
            self.wring = [Buf(sb("wr%d" % i, [P, KC, TT], BF16), self.newsem("d_wr%d" % i)) for i in range(3)]
            self.wring_i = 0
            self.wqueue = []
            self.wq_next = 0

            self.stage12()
            self.stage12_end()
            self.barrier()
            self.stage3()
            self.barrier()
            self.stage4()
            self.barrier()
        return nc

    def wq_add(self, name, r0, nrows, c0, ncols):
        self.wqueue.append((name, r0, nrows, c0, ncols))
        return len(self.wqueue) - 1

    def wq_get(self, idx, pf=2):
        nc = self.nc
        last = min(len(self.wqueue) - 1, idx + pf)
        while self.wq_next <= last:
            name, r0, nrows, c0, ncols = self.wqueue[self.wq_next]
            slot = self.wring[self.wq_next % 3]
            kc = nrows // P
            g, cl = c0 // TT, r0 // P
            src = self.wb[name][g, :, cl:cl + kc, :]
            fns = [lambda src=src, slot=slot, kc=kc: nc.sync.dma_start(out=slot.ap[:, 0:kc, :], in_=src)]
            dep = self.wev[(name, g)]
            if dep.w is None:
                self.ensure_conv(dep)
            self.dma("sp", fns, slot.dsem, reads=[dep], writes=[slot])
            self.wq_next += 1
        return self.wring[idx % 3]

    def norm_stats(self, src, junks, junk_aps=None):
        nc = self.nc
        D = self.cfg["D"]
        H = D // 2
        st = self.stat[self.stat_i % len(self.stat)]
        self.stat_i += 1
        self.op("dve", lambda: nc.vector.memset(st.ap[:, 0:2], 0.0), writes=[st])
        for hf in range(2):
            jb = junks[hf]
            j_ap = jb.ap[:, 0:H] if junk_aps is None else junk_aps[hf]
            self.op("act", lambda hf=hf, jb=jb, j_ap=j_ap: nc.scalar.activation(out=j_ap, in_=src.ap[:, hf * H:(hf + 1) * H], func=AF.Square,
                                                                    accum_out=st.ap[:, hf:hf + 1]), reads=[src], writes=[jb, st])
        self.op("dve", lambda: nc.vector.tensor_tensor(out=st.ap[:, 2:3], in0=st.ap[:, 0:1], in1=st.ap[:, 1:2], op=ALU.add),
                reads=[st], writes=[st])
        self.op("dve", lambda: nc.vector.tensor_scalar(out=st.ap[:, 2:3], in0=st.ap[:, 2:3], scalar1=1.0 / D, scalar2=EPS,
                                                       op0=ALU.mult, op1=ALU.add), reads=[st], writes=[st])
        self.rsqrt(st, st.ap[:, 3:4], st.ap[:, 2:3])
        return st

    def norm_scale(self, src, st, hf, dstb, d_ap):
        nc = self.nc
        H = self.cfg["D"] // 2
        s_ap = src.ap[:, hf * H:(hf + 1) * H]
        if hf == 0:
            self.op("dve", lambda: nc.vector.tensor_scalar(out=d_ap, in0=s_ap, scalar1=st.ap[:, 3:4], scalar2=None, op0=ALU.mult),
                    reads=[src, st], writes=[dstb])
        else:
            self.op("act", lambda: nc.scalar.activation(out=d_ap, in_=s_ap, func=AF.Copy, scale=st.ap[:, 3:4]),
                    reads=[src, st], writes=[dstb])

    def norm_pre(self, src):
        H = self.cfg["D"] // 2
        st = self.norm_stats(src, [self.junk, self.junk])
        for hf in range(2):
            self.norm_scale(src, st, hf, src, src.ap[:, hf * H:(hf + 1) * H])

    def norm_tr(self, src, gain, dstT, b):
        H = self.cfg["D"] // 2
        for hf in range(2):
            self.transpose_block(src, src.ap[:, hf * H:(hf + 1) * H], self.NCH // 2, dstT, b, gain, c_off=hf * (self.NCH // 2))

    def norm_block(self, src, gain, dstT, b, tmp=None, st=None):
        H = self.cfg["D"] // 2
        if st is None:
            st = self.norm_stats(src, tmp)
        for hf in range(2):
            self.norm_scale(src, st, hf, tmp[hf], tmp[hf].ap[:, 0:H])
            self.transpose_block(tmp[hf], tmp[hf].ap[:, 0:H], self.NCH // 2, dstT, b, gain, c_off=hf * (self.NCH // 2))
        return st

    def norm_blocks_q(self, hbs, gain, dstT, xq, sts):
        nc = self.nc
        D, NCH = self.cfg["D"], self.NCH
        NQ4 = len(xq)
        Q = D // NQ4
        cq = NCH // NQ4

        def scale(b, q):
            src, st = hbs[b], sts[b]
            s_ap = src.ap[:, q * Q:(q + 1) * Q]
            if q % 2 == 0:
                self.op("dve", lambda: nc.vector.tensor_scalar(out=xq[q].ap, in0=s_ap, scalar1=st.ap[:, 3:4], scalar2=None, op0=ALU.mult),
                        reads=[src, st], writes=[xq[q]])
            else:
                self.op("act", lambda: nc.scalar.activation(out=xq[q].ap, in_=s_ap, func=AF.Copy, scale=st.ap[:, 3:4]),
                        reads=[src, st], writes=[xq[q]])
        for q in range(NQ4):
            scale(0, q)
        for b in range(len(hbs)):
            for q in range(NQ4):
                self.transpose_block(xq[q], xq[q].ap, cq, dstT, b, gain, c_off=q * cq)
                if b + 1 < len(hbs):
                    scale(b + 1, q)

    def transpose_block(self, src, src_ap, nch, dstT, b, gain=None, c_off=0, aux=False):
        nc = self.nc
        k = 0
        for c0 in range(0, nch, 4):
            n = min(4, nch - c0)
            ps = self.psum_aux() if aux else self.psum()
            pv = ps.ap.rearrange("p (j q) -> p j q", j=4)

            def tr(c0=c0, n=n, pv=pv):
                for j in range(n):
                    ins = nc.tensor.transpose(pv[:, j, :], src_ap[:, (c0 + j) * P:(c0 + j + 1) * P], self.ident)
                return ins
            self.op("pe", tr, reads=[src, self.cst], writes=[ps])
            if gain is None:
                e = "act" if (k % 2 == 0) else "dve"
                k += 1
                o = dstT.ap[:, c_off + c0:c_off + c0 + n, b * P:(b + 1) * P]
                if e == "act":
                    self.op("act", lambda o=o, pv=pv, n=n: nc.scalar.copy(out=o, in_=pv[:, 0:n, :]), reads=[ps], writes=[dstT])
                else:
                    self.op("dve", lambda o=o, pv=pv, n=n: nc.vector.tensor_copy(out=o, in_=pv[:, 0:n, :]), reads=[ps], writes=[dstT])
            elif (c0 // 4) % 2 == 0:
                cc0 = c_off + c0
                o = dstT.ap[:, cc0:cc0 + n, b * P:(b + 1) * P]
                gb = gain[:, cc0:cc0 + n].unsqueeze(2).broadcast_to([P, n, P])
                self.op("dve", lambda o=o, pv=pv, n=n, gb=gb: nc.vector.tensor_tensor(out=o, in0=pv[:, 0:n, :], in1=gb, op=ALU.mult),
                        reads=[ps, self.cst], writes=[dstT])
            else:
                for j in range(n):
                    cc = c_off + c0 + j
                    o = dstT.ap[:, cc, b * P:(b + 1) * P]
                    self.op("act", lambda o=o, pv=pv, j=j, cc=cc: nc.scalar.activation(
                        out=o, in_=pv[:, j, :], func=AF.Copy, scale=gain[:, cc:cc + 1]), reads=[ps, self.cst], writes=[dstT])

    def proj(self, tiles, actT, nchunks, mode, ncols=TT, ntok=TT):
        nc = self.nc
        KC = self.cfg["KC"]
        nout = (ntok // P) if mode == "tok" else (ncols // P)
        pss = [self.psum_proj() for _ in range(nout)]
        for ti, widx in enumerate(tiles):
            wt = self.wq_get(widx)
            kc = min(KC, nchunks - ti * KC)
            for o in range(nout):
                def mm(o=o, ti=ti, wt=wt, kc=kc):
                    for k in range(kc):
                        c = ti * KC + k
                        first = (c == 0)
                        lastc = (c == nchunks - 1)
                        if mode == "tok":
                            ins = nc.tensor.matmul(pss[o].ap[:, 0:ncols], lhsT=actT.ap[:, c, o * P:(o + 1) * P],
                                                   rhs=wt.ap[:, k, 0:ncols], start=first, stop=lastc)
                        else:
                            ins = nc.tensor.matmul(pss[o].ap[:, 0:ntok], lhsT=wt.ap[:, k, o * P:(o + 1) * P],
                                                   rhs=actT.ap[:, c, 0:ntok], start=first, stop=lastc)
                    return ins
                self.op("pe", mm, reads=[wt, actT], writes=[pss[o]])
            if self.cv_rate:
                if self.cv_boost[0] > 0:
                    k_call = max(self.cv_rate, self.cv_boost[1])
                else:
                    k_call = self.cv_rate if self.cvq_in else 1
                nt = len(tiles)
                k = k_call // nt + (1 if ti < k_call % nt else 0)
                if k:
                    self.cv_meter(k, self.cv_queues)
                if ti == nt - 1 and self.cv_boost[0] > 0:
                    self.cv_boost[0] -= 1
        return pss

    def stage12(self):
        nc, cfg = self.nc, self.cfg
        T, TP, D, AW, BW, KC = cfg["T"], cfg["TP"], cfg["D"], cfg["AW"], cfg["BW"], cfg["KC"]
        NCH = self.NCH
        NHA, NHB = AW // P, BW // P
        GA, GB = AW // TT, BW // TT
        nkt = NCH // KC
        with ExitStack() as st:
            sb = lambda n, s, d: self.sb(n, s, d, st)
            xb = [Buf(sb("xb%d" % i, [P, D], F32), self.newsem("d_xb%d" % i)) for i in range(2)]
            self.junk = Buf(sb("junk", [P, D // 2], BF16))
            aT = Buf(sb("aT", [P, NCH, TT], BF16))
            gus = [Buf(sb("gu%d" % i, [P, NB, TT], F32)) for i in range(2)]
            tmpf = [Buf(sb("tf%d" % i, [P, TT], F32)) for i in range(5)]
            gvs = [tmpf[0], tmpf[1], tmpf[2], Buf(sb("gv3", [P, TT], F32))]
            vn = [Buf(sb("vn%d" % i, [P, TT], BF16)) for i in range(NB)]
            yns = [Buf(sb("yn%d" % i, [P, TT], F32)) for i in range(NB)]
            self.deferred = []
            wsT = Buf(sb("wsTf", [P, NHA * P], F32), self.newsem("d_ws"))
            wsm = Buf(sb("wsm", [P, NHA, P], BF16))
            ga = Buf(sb("ga", [P, AW], F32), wsT.dsem)
            yaT = Buf(sb("yaT", [P, NHA, TT], BF16), self.newsem("d_yaT"))
            qst = [Buf(sb("qst%d" % i, [P, NB, TT], BF16), self.newsem("d_qst%d" % i)) for i in range(2)]
            self.dma("sp", [lambda: nc.sync.dma_start(out=wsT.ap, in_=self.wsT_d),
                            lambda: nc.sync.dma_start(out=ga.ap, in_=self.ga_d)], wsT.dsem, writes=[wsT, ga])
            for h in range(NHA):
                self.op("dve", lambda h=h: nc.vector.tensor_tensor(out=wsm.ap[:, h, :], in0=wsT.ap[:, h * P:(h + 1) * P],
                                                                  in1=self.trilT, op=ALU.mult),
                        reads=[wsT, self.cst], writes=[wsm])

            ntile_pre, ntile_own = TP // TT, T // TT
            jobs = []
            def add_group(kind, g, c0):
                return [self.wq_add("w_in", k * KC * P, KC * P, c0, TT) for k in range(nkt)]
            plan = []
            for ti in range(ntile_pre + ntile_own):
                own = ti >= ntile_pre
                groups = []
                if own:
                    for g in range(GA):
                        groups.append(("u", g, add_group("u", g, g * TT)))
                        groups.append(("v", g, add_group("v", g, AW + g * TT)))
                    for g in range(GB):
                        groups.append(("q", g, add_group("q", g, 2 * AW + g * TT)))
                for g in range(GB):
                    groups.append(("k", g, add_group("k", g, 2 * AW + BW + g * TT)))
                for g in range(GB):
                    groups.append(("vb", g, add_group("vb", g, 2 * AW + 2 * BW + g * TT)))
                plan.append((ti, own, groups))

            nproj = sum(len(gr) for (_, _, gr) in plan)
            npieces = len(self.cvq_in) + len(self.cvq)
            self.cv_rate = 2
            self.cv_queues = [self.cvq_in, self.cvq]
            self.cv_boost = [GB, max(1, NCH // 8)]
            xi = 0
            qi = 0
            C2 = 2.0 * math.sqrt(2.0 / math.pi)
            for (ti, own, groups) in plan:
                xsrc = self.x_own if own else self.x_pre
                t0 = (ti - ntile_pre) * TT if own else ti * TT
                tg = TP + t0 if own else t0
                def pre(pl, b):
                    ti_, own_, _ = pl
                    xs_ = self.x_own if own_ else self.x_pre
                    r0 = ((ti_ - ntile_pre) * TT if own_ else ti_ * TT) + b * P
                    xbuf = xb[b % 2]
                    self.dma("sp", [lambda: nc.sync.dma_start(out=xbuf.ap, in_=xs_[r0:r0 + P, :])], xbuf.dsem, writes=[xbuf])
                    self.norm_pre(xbuf)
                pi = plan.index((ti, own, groups))
                if pi == 0:
                    pre(plan[0], 0)
                    pre(plan[0], 1)
                for b in range(NB):
                    self.norm_tr(xb[b % 2], self.g_mix, aT, b)
                    if b + 2 < NB:
                        pre(plan[pi], b + 2)
                    elif pi + 1 < len(plan):
                        pre(plan[pi + 1], b + 2 - NB)
                for (kind, g, widxs) in groups:
                    if kind in ("u", "v", "vb"):
                        pss = self.proj(widxs, aT, NCH, "tok")
                    else:
                        pss = self.proj(widxs, aT, NCH, "feat")
                    if kind == "u":
                        gu_g = gus[g % 2]
                        for b in range(NB):
                            self.gelu(pss[b], gu_g, gu_g.ap[:, b, :], tmpf, C2)
                    elif kind == "v":
                        HG = TT // P
                        gu_g = gus[g % 2]
                        for b in range(NB):
                            self.gelu(pss[b], gvs[b], gvs[b].ap, tmpf, C2)
                        for b in range(NB):
                            vnb, ynb = vn[b], yns[b]
                            self.sgu_A(gvs[b], g, b, tmpf, vnb, C2)
                            self.defer(2, lambda g=g, b=b, vnb=vnb, ynb=ynb, gu_g=gu_g: self.sgu_B(g, b, gu_g, tmpf, vnb, wsm, ga, ynb))
                            self.defer(4, lambda g=g, b=b, ynb=ynb: self.transpose_block(ynb, ynb.ap, HG, yaT, b, None, c_off=g * HG, aux=True))
                    elif kind in ("q", "k"):
                        stg = qst[qi % 2]
                        qi += 1
                        for e in range(NB):
                            if e % 2 == 0:
                                self.op("act", lambda e=e, stg=stg, pss=pss, kind=kind: nc.scalar.activation(
                                    out=stg.ap[:, e, :], in_=pss[e].ap, func=AF.Copy,
                                    scale=(1.0 / math.sqrt(P)) if kind == "q" else 1.0), reads=[pss[e]], writes=[stg])
                            else:
                                self.op("dve", lambda e=e, stg=stg, pss=pss, kind=kind: nc.vector.tensor_scalar(
                                    out=stg.ap[:, e, :], in0=pss[e].ap, scalar1=(1.0 / math.sqrt(P)) if kind == "q" else 1.0,
                                    scalar2=None, op0=ALU.mult), reads=[pss[e]], writes=[stg])
                        if kind == "q":
                            dst = self.QT_d[g * NB:(g + 1) * NB, :, t0:t0 + TT]
                        else:
                            dst = self.KT_d[g * NB:(g + 1) * NB, :, tg:tg + TT]
                        self.dma("pool", [lambda stg=stg, dst=dst: nc.gpsimd.dma_start(out=dst.rearrange("h p t -> p h t"), in_=stg.ap)],
                                 stg.dsem, reads=[stg])
                    else:
                        stg = qst[qi % 2]
                        qi += 1
                        for b in range(NB):
                            if b % 2 == 0:
                                self.op("act", lambda b=b, stg=stg, pss=pss: nc.scalar.copy(out=stg.ap[:, b, :], in_=pss[b].ap),
                                        reads=[pss[b]], writes=[stg])
                            else:
                                self.op("dve", lambda b=b, stg=stg, pss=pss: nc.vector.tensor_copy(out=stg.ap[:, b, :], in_=pss[b].ap),
                                        reads=[pss[b]], writes=[stg])
                        dst = self.V_d[tg:tg + TT, g * TT:(g + 1) * TT].rearrange("(b p) n -> p b n", p=P)
                        self.dma("pool", [lambda stg=stg, dst=dst: nc.gpsimd.dma_start(out=dst, in_=stg.ap)], stg.dsem, reads=[stg])
                    self.tick()
                self.flush_deferred()
                if own:
                    dst = self.YT_d[0:NHA, :, t0:t0 + TT].rearrange("c p t -> p c t")
                    self.dma("pool", [lambda dst=dst: nc.gpsimd.dma_start(out=dst, in_=yaT.ap)], yaT.dsem, reads=[yaT])

    def stage12_end(self):
        self.cv_meter(len(self.cvq_in) + len(self.cvq), [self.cvq_in, self.cvq])
        self.cv_rate = 0

    def gelu(self, ps, dstbuf, dst_ap, tmpf, C2):
        nc = self.nc
        self.op("act", lambda: nc.scalar.activation(out=dst_ap, in_=ps.ap, func=AF.Gelu_apprx_tanh), reads=[ps], writes=[dstbuf])

    def defer(self, k, fn):
        self.deferred.append([k, fn])

    def tick(self):
        due = []
        for item in self.deferred:
            item[0] -= 1
            if item[0] <= 0:
                due.append(item)
        self.deferred = [it for it in self.deferred if it[0] > 0]
        for it in due:
            it[1]()

    def flush_deferred(self):
        while self.deferred:
            self.tick()

    def sgu_A(self, gv, g, b, tmpf, vnb, C2):
        nc = self.nc
        sq = tmpf[3]
        st = self.stat[self.stat_i % len(self.stat)]
        self.stat_i += 1
        HG = TT // P
        gv3 = gv.ap.rearrange("p (h d) -> p h d", h=HG)
        sq3 = sq.ap.rearrange("p (h d) -> p h d", h=HG)
        self.op("dve", lambda: nc.vector.tensor_reduce(out=st.ap[:, 0:HG], in_=gv3, axis=AX.X, op=ALU.add), reads=[gv], writes=[st])
        self.op("pool", lambda: nc.gpsimd.tensor_tensor(out=sq.ap, in0=gv.ap, in1=gv.ap, op=ALU.mult), reads=[gv], writes=[sq])
        self.op("dve", lambda: nc.vector.tensor_reduce(out=st.ap[:, 4:4 + HG], in_=sq3, axis=AX.X, op=ALU.add), reads=[sq], writes=[st])
        self.op("dve", lambda: nc.vector.tensor_scalar(out=st.ap[:, 0:HG], in0=st.ap[:, 0:HG], scalar1=1.0 / P, scalar2=None,
                                                       op0=ALU.mult), reads=[st], writes=[st])
        st2 = self.stat[self.stat_i % len(self.stat)]
        self.stat_i += 1
        self.op("dve", lambda: nc.vector.tensor_tensor(out=st2.ap[:, 0:HG], in0=st.ap[:, 0:HG], in1=st.ap[:, 0:HG], op=ALU.mult),
                reads=[st], writes=[st2])
        self.op("dve", lambda: nc.vector.scalar_tensor_tensor(out=st.ap[:, 4:4 + HG], in0=st.ap[:, 4:4 + HG], scalar=1.0 / P,
                                                              in1=st2.ap[:, 0:HG], op0=ALU.mult, op1=ALU.subtract),
                reads=[st, st2], writes=[st])
        self.op("dve", lambda: nc.vector.tensor_scalar(out=st.ap[:, 4:4 + HG], in0=st.ap[:, 4:4 + HG], scalar1=EPS, scalar2=None,
                                                       op0=ALU.add), reads=[st], writes=[st])
        self.rsqrt(st, st.ap[:, 4:4 + HG], st.ap[:, 4:4 + HG])
        for h in range(HG):
            self.op("dve", lambda h=h: nc.vector.tensor_scalar(out=vnb.ap[:, h * P:(h + 1) * P], in0=gv3[:, h, :],
                                                               scalar1=st.ap[:, h:h + 1], scalar2=st.ap[:, 4 + h:5 + h],
                                                               op0=ALU.subtract, op1=ALU.mult), reads=[gv, st], writes=[vnb])

    def sgu_B(self, g, b, gu, tmpf, vnb, wsm, ga, yn):
        nc = self.nc
        sq, ya = tmpf[3], tmpf[4]
        HG = TT // P
        sq3 = sq.ap.rearrange("p (h d) -> p h d", h=HG)
        pm = self.psum_aux()

        def mix():
            for h in range(HG):
                ins = nc.tensor.matmul(pm.ap[:, h * P:(h + 1) * P], lhsT=wsm.ap[:, g * HG + h, :], rhs=vnb.ap[:, h * P:(h + 1) * P],
                                       start=True, stop=True)
            return ins
        self.op("pe", mix, reads=[wsm, vnb], writes=[pm])
        for h in range(HG):
            hh = g * HG + h
            self.op("dve", lambda h=h, hh=hh: nc.vector.scalar_tensor_tensor(
                out=ya.ap[:, h * P:(h + 1) * P], in0=pm.ap[:, h * P:(h + 1) * P], scalar=self.bsT[:, hh:hh + 1],
                in1=gu.ap[:, b, h * P:(h + 1) * P], op0=ALU.add, op1=ALU.mult), reads=[pm, gu, self.cst], writes=[ya])
        st3 = self.stat[self.stat_i % len(self.stat)]
        self.stat_i += 1
        self.op("pool", lambda: nc.gpsimd.tensor_tensor(out=sq.ap, in0=ya.ap, in1=ya.ap, op=ALU.mult), reads=[ya], writes=[sq])
        self.op("dve", lambda: nc.vector.tensor_reduce(out=st3.ap[:, 0:HG], in_=sq3, axis=AX.X, op=ALU.add), reads=[sq], writes=[st3])
        self.op("dve", lambda: nc.vector.tensor_scalar(out=st3.ap[:, 0:HG], in0=st3.ap[:, 0:HG], scalar1=1.0 / P, scalar2=EPS,
                                                       op0=ALU.mult, op1=ALU.add), reads=[st3], writes=[st3])
        self.rsqrt(st3, st3.ap[:, 0:HG], st3.ap[:, 0:HG])
        for h in range(HG):
            hh = g * HG + h
            self.op("dve", lambda h=h, hh=hh: nc.vector.scalar_tensor_tensor(
                out=yn.ap[:, h * P:(h + 1) * P], in0=ya.ap[:, h * P:(h + 1) * P], scalar=st3.ap[:, h:h + 1],
                in1=ga.ap[:, hh * P:(hh + 1) * P], op0=ALU.mult, op1=ALU.mult), reads=[ya, st3, ga], writes=[yn])

    def stage3(self):
        nc, cfg = self.nc, self.cfg
        T, TP, BW = cfg["T"], cfg["TP"], cfg["BW"]
        NHB = BW // P
        NHA = cfg["AW"] // P
        TK = TP + T
        NKB = TK // P
        nqt = T // TT
        NS = 2 if NHB % 2 == 0 else 1
        with ExitStack() as st:
            sb = lambda n, s, d: self.sb(n, s, d, st)
            S = []
            for x in range(NS):
                d = dict(
                    KTh=[Buf(sb("KTh%d_%d" % (x, i), [P, TK], BF16), self.newsem("d_KTh%d_%d" % (x, i))) for i in range(2)],
                    Vh=[Buf(sb("Vh%d_%d" % (x, i), [P, NKB, P], BF16), self.newsem("d_Vh%d_%d" % (x, i))) for i in range(2)],
                    QTh=[Buf(sb("QTh%d_%d" % (x, i), [P, T], BF16), self.newsem("d_QTh%d_%d" % (x, i))) for i in range(2)],
                    Eb=[Buf(sb("E%d_%d" % (x, i), [P, TT], F32)) for i in range(3)],
                    Lb=[Buf(sb("L%d_%d" % (x, i), [P, TT], BF16)) for i in range(3)],
                    Gb=[Buf(sb("G%d_%d" % (x, i), [P, TT], F32)) for i in range(2)],
                    Ab=[Buf(sb("A%d_%d" % (x, i), [P, TT], BF16)) for i in range(2)],
                    osb=Buf(sb("osb%d" % x, [P, TT], F32)), osq=Buf(sb("osq%d" % x, [P, TT], F32)), rsd=Buf(sb("rsd%d" % x, [P, TT], F32)),
                    ybs=[Buf(sb("ybs%d_%d" % (x, i), [P, TT], BF16), self.newsem("d_ybs%d_%d" % (x, i))) for i in range(2)],
                    zps=[self.ps[4 * x], self.ps[4 * x + 1]], cp=self.ps[4 * x + 2], op=self.ps[4 * x + 3],
                    heads=list(range(x, NHB, NS)), loaded=set(), its=[], yi=0)
                for hi, h in enumerate(d["heads"]):
                    for qt in range(nqt):
                        kbs = list(range(TP // P + (qt + 1) * NB - 1, -1, -1))
                        for n, kb in enumerate(kbs):
                            j = kb - (TP // P + qt * NB)
                            d["its"].append(dict(h=h, hi=hi, qt=qt, kb=kb, first=(n == 0), last=(n == len(kbs) - 1),
                                                 j=j if j >= 0 else None))
                S.append(d)
            NI = len(S[0]["its"])

            def load_head(d, hi):
                h = d["heads"][hi]
                i = hi % 2
                K_, V_, Q_ = d["KTh"][i], d["Vh"][i], d["QTh"][i]
                self.dma("sp", [lambda: nc.sync.dma_start(out=K_.ap, in_=self.KT_d[h])], K_.dsem, writes=[K_])
                self.dma("sp", [lambda: nc.sync.dma_start(out=V_.ap, in_=self.V_d[:, h * P:(h + 1) * P].rearrange("(n p) d -> p n d", p=P))],
                         V_.dsem, writes=[V_])
                self.dma("sp", [lambda: nc.sync.dma_start(out=Q_.ap, in_=self.QT_d[h])], Q_.dsem, writes=[Q_])
                d["loaded"].add(hi)

            def zT(d, i):
                it = d["its"][i]
                hi, qt, kb = it["hi"], it["qt"], it["kb"]
                if it["first"] and qt == 0 and (hi + 1) < len(d["heads"]) and (hi + 1) not in d["loaded"]:
                    load_head(d, hi + 1)
                z = d["zps"][i % 2]
                k_, q_ = d["KTh"][hi % 2], d["QTh"][hi % 2]
                self.op("pe", lambda: nc.tensor.matmul(z.ap, lhsT=k_.ap[:, kb * P:(kb + 1) * P], rhs=q_.ap[:, qt * TT:(qt + 1) * TT],
                                                       start=True, stop=True), reads=[k_, q_], writes=[z])

            def E(d, i):
                z, e = d["zps"][i % 2], d["Eb"][i % 3]
                self.op("act", lambda: nc.scalar.activation(out=e.ap, in_=z.ap, func=AF.Exp), reads=[z], writes=[e])

            def L(d, i):
                it = d["its"][i]
                e, l = d["Eb"][i % 3], d["Lb"][i % 3]
                self.op("act", lambda: nc.scalar.activation(out=l.ap, in_=e.ap, func=AF.Ln, bias=1.0), reads=[e], writes=[l])
                if it["j"] is not None:
                    j = it["j"]
                    self.op("dve", lambda: nc.vector.tensor_tensor(out=l.ap, in0=l.ap, in1=self.dmask[:, j, :], op=ALU.mult),
                            reads=[l, self.cbf], writes=[l])

            def Tinc(d, i):
                it = d["its"][i]
                cp, l = d["cp"], d["Lb"][i % 3]
                self.op("pe", lambda: nc.tensor.matmul(cp.ap, lhsT=self.tinc, rhs=l.ap, start=it["first"], stop=True,
                                                       skip_group_check=True), reads=[l, self.cbf], writes=[cp])

            def G(d, i):
                cp, g = d["cp"], d["Gb"][i % 2]
                self.op("act", lambda: nc.scalar.activation(out=g.ap, in_=cp.ap, func=AF.Exp, scale=-1.0), reads=[cp], writes=[g])

            def Tcomp(d, i):
                it = d["its"][i]
                if it["last"]:
                    return
                cp, l = d["cp"], d["Lb"][i % 3]
                self.op("pe", lambda: nc.tensor.matmul(cp.ap, lhsT=self.tcomp, rhs=l.ap, start=False, stop=True,
                                                       skip_group_check=True), reads=[l, self.cbf], writes=[cp])

            def A(d, i):
                it = d["its"][i]
                e, g, a = d["Eb"][i % 3], d["Gb"][i % 2], d["Ab"][i % 2]
                if it["j"] is not None:
                    j = it["j"]
                    self.op("dve", lambda: nc.vector.tensor_tensor(out=g.ap, in0=g.ap, in1=self.dmask[:, j, :], op=ALU.mult),
                            reads=[g, self.cbf], writes=[g])
                self.op("dve", lambda: nc.vector.tensor_tensor(out=a.ap, in0=e.ap, in1=g.ap, op=ALU.mult), reads=[e, g], writes=[a])

            def AV(d, i):
                it = d["its"][i]
                h, hi, qt, kb = it["h"], it["hi"], it["qt"], it["kb"]
                a, v_ = d["Ab"][i % 2], d["Vh"][hi % 2]
                op_ = d["op"]
                self.op("pe", lambda: nc.tensor.matmul(op_.ap, lhsT=v_.ap[:, kb, :], rhs=a.ap, start=it["first"], stop=it["last"]),
                        reads=[v_, a], writes=[op_])
                if it["last"]:
                    post(d, h, qt)

            def post(d, h, qt):
                op_, osb, osq, rsd = d["op"], d["osb"], d["osq"], d["rsd"]
                yb = d["ybs"][d["yi"] % 2]
                d["yi"] += 1
                self.op("act", lambda: nc.scalar.copy(out=osb.ap, in_=op_.ap), reads=[op_], writes=[osb])
                self.op("act", lambda: nc.scalar.activation(out=osq.ap, in_=op_.ap, func=AF.Square), reads=[op_], writes=[osq])
                self.op("pe", lambda: nc.tensor.matmul(op_.ap, lhsT=self.onesf, rhs=osq.ap, start=True, stop=True),
                        reads=[osq, self.cst], writes=[op_])
                self.op("dve", lambda: nc.vector.tensor_scalar(out=rsd.ap, in0=op_.ap, scalar1=1.0 / P, scalar2=EPS,
                                                               op0=ALU.mult, op1=ALU.add), reads=[op_], writes=[rsd])
                self.op("act", lambda: nc.scalar.activation(out=rsd.ap, in_=rsd.ap, func=AF.Ln), reads=[rsd], writes=[rsd])
                self.op("act", lambda: nc.scalar.activation(out=rsd.ap, in_=rsd.ap, func=AF.Exp, scale=-0.5), reads=[rsd], writes=[rsd])
                self.op("dve", lambda: nc.vector.scalar_tensor_tensor(out=yb.ap, in0=osb.ap, scalar=self.g_b[:, h:h + 1], in1=rsd.ap,
                                                                      op0=ALU.mult, op1=ALU.mult), reads=[osb, rsd, self.cst], writes=[yb])
                dst = self.YT_d[NHA + h, :, qt * TT:(qt + 1) * TT]
                self.dma("sp", [lambda: nc.sync.dma_start(out=dst, in_=yb.ap)], yb.dsem, reads=[yb])

            for d in S:
                load_head(d, 0)
            ncv = len(self.cvq3)
            cv_every = max(1, (NI * 9 // 10) // max(1, ncv))
            for d in S:
                zT(d, 0)
            for d in S:
                E(d, 0)
            for d in S:
                L(d, 0)
            if NI > 1:
                for d in S:
                    zT(d, 1)
            for d in S:
                Tinc(d, 0)
            for s_ in range(NI):
                if s_ % cv_every == 0:
                    self.cv_meter(1, [self.cvq3])
                if s_ + 2 < NI:
                    for d in S:
                        zT(d, s_ + 2)
                if s_ + 1 < NI:
                    for d in S:
                        E(d, s_ + 1)
                for d in S:
                    G(d, s_)
                if s_ + 1 < NI:
                    for d in S:
                        L(d, s_ + 1)
                for d in S:
                    Tcomp(d, s_)
                if s_ + 1 < NI:
                    for d in S:
                        Tinc(d, s_ + 1)
                for d in S:
                    A(d, s_)
                for d in S:
                    AV(d, s_)
            self.cv_meter(len(self.cvq3), [self.cvq3])

    def stage4(self):
        nc, cfg = self.nc, self.cfg
        T, D, DFF, PLE, KC, NQ = cfg["T"], cfg["D"], cfg["DFF"], cfg["PLE"], cfg["KC"], cfg["NQ"]
        NCH = self.NCH
        nkt = NCH // KC
        FQ = DFF // NQ
        FQC = FQ // P
        nkq = FQC // KC
        NE = D // TT
        NPC = PLE // P
        ntile = T // TT
        with ExitStack() as st:
            sb = lambda n, s, d: self.sb(n, s, d, st)
            hb = [Buf(sb("h%d" % b, [P, D], F32), self.newsem("d_h%d" % b)) for b in range(NB)]
            actT = Buf(sb("actT", [P, NCH, TT], BF16), self.newsem("d_actT"))
            hidT = Buf(sb("hidT", [P, FQC, TT], BF16))
            xq = [Buf(sb("xq%d" % i, [P, D // 4], F32)) for i in range(4)]
            gfin = Buf(sb("gfin", [P, D], F32), self.newsem("d_gfin"))
            pt = Buf(sb("pt", [P, PLE], F32), self.newsem("d_pt"))
            pT = Buf(sb("pT", [P, NPC, TT], BF16))
            tf = [Buf(sb("t4_%d" % i, [P, TT], F32)) for i in range(2)]
            self.dma("sp", [lambda: nc.sync.dma_start(out=gfin.ap, in_=self.gfin_d)], gfin.dsem, writes=[gfin])
            plan = []
            for ti in range(ntile):
                d = {}
                d["out"] = [[self.wq_add("w_out", k * KC * P, KC * P, e * TT, TT) for k in range(nkt)] for e in range(NE)]
                d["ffn"] = []
                for q in range(NQ):
                    ups = [[self.wq_add("w_up", k * KC * P, KC * P, q * FQ + fg * TT, TT) for k in range(nkt)] for fg in range(FQ // TT)]
                    dns = [[self.wq_add("w_down", q * FQ + k * KC * P, KC * P, e * TT, TT) for k in range(nkq)] for e in range(NE)]
                    d["ffn"].append((ups, dns))
                d["gate"] = [([self.wq_add("w_gate", k * KC * P, KC * P, e * TT, TT) for k in range(nkt)],
                              self.wq_add("w_ple", 0, PLE, e * TT, TT)) for e in range(NE)]
                plan.append(d)

            def load_yT(ti_):
                half = max(1, NCH // 2)
                self.dma("sp", [lambda c0=c0: nc.sync.dma_start(out=actT.ap[:, c0:c0 + half, :],
                                                                in_=self.YT_d[c0:c0 + half, :, ti_ * TT:(ti_ + 1) * TT].rearrange("c p t -> p c t"))
                                for c0 in range(0, NCH, half)], actT.dsem, writes=[actT])

            ei = 0
            for ti in range(ntile):
                t0 = ti * TT
                d = plan[ti]
                for b in range(NB):
                    self.dma("sp", [lambda b=b: nc.sync.dma_start(out=hb[b].ap, in_=self.x_own[t0 + b * P:t0 + (b + 1) * P, :])],
                             hb[b].dsem, writes=[hb[b]])
                if ti == 0:
                    load_yT(0)
                for e in range(NE):
                    pss = self.proj(d["out"][e], actT, NCH, "tok")
                    for b in range(NB):
                        self.op("dve", lambda b=b, e=e, pss=pss: nc.vector.tensor_tensor(
                            out=hb[b].ap[:, e * TT:(e + 1) * TT], in0=hb[b].ap[:, e * TT:(e + 1) * TT], in1=pss[b].ap, op=ALU.add),
                            reads=[pss[b], hb[b]], writes=[hb[b]])
                jflat = hidT.ap.rearrange("p c t -> p (c t)")[:, 0:D // 2]
                sts = [self.norm_stats(hb[b], [hidT, hidT], [jflat, jflat]) for b in range(NB)]
                self.norm_blocks_q(hb, self.g_ffn, actT, xq, sts)
                for q in range(NQ):
                    ups, dns = d["ffn"][q]
                    for fg, widxs in enumerate(ups):
                        pss = self.proj(widxs, actT, NCH, "feat")
                        for fc in range(NB):
                            t = tf[ei % 2]
                            ei += 1
                            self.op("act", lambda t=t, fc=fc, pss=pss: nc.scalar.activation(out=t.ap, in_=pss[fc].ap, func=AF.Relu),
                                    reads=[pss[fc]], writes=[t])
                            o = hidT.ap[:, fg * NB + fc, :]
                            eng = "dve" if (fc % 2 == 0) else "pool"
                            if eng == "dve":
                                self.op("dve", lambda t=t, o=o: nc.vector.tensor_tensor(out=o, in0=t.ap, in1=t.ap, op=ALU.mult),
                                        reads=[t], writes=[hidT])
                            else:
                                self.op("pool", lambda t=t, o=o: nc.gpsimd.tensor_tensor(out=o, in0=t.ap, in1=t.ap, op=ALU.mult),
                                        reads=[t], writes=[hidT])
                    for e, widxs in enumerate(dns):
                        pss = self.proj(widxs, hidT, FQC, "tok")
                        for b in range(NB):
                            self.op("dve", lambda b=b, e=e, pss=pss: nc.vector.tensor_tensor(
                                out=hb[b].ap[:, e * TT:(e + 1) * TT], in0=hb[b].ap[:, e * TT:(e + 1) * TT], in1=pss[b].ap, op=ALU.add),
                                reads=[pss[b], hb[b]], writes=[hb[b]])
                sts = [self.norm_stats(hb[b], [hidT, hidT], [jflat, jflat]) for b in range(NB)]
                self.norm_blocks_q(hb, self.g_ple, actT, xq, sts)
                for b in range(NB):
                    self.dma("sp", [lambda b=b: nc.sync.dma_start(out=pt.ap, in_=self.p_own[t0 + b * P:t0 + (b + 1) * P, :])],
                             pt.dsem, writes=[pt])
                    self.transpose_block(pt, pt.ap, NPC, pT, b, None)
                for e in range(NE):
                    pss = self.proj(d["gate"][e][0], actT, NCH, "tok")
                    wple = self.wq_get(d["gate"][e][1])
                    for b in range(NB):
                        pe_ = self.psum_aux()

                        def mm(b=b, e=e, pe_=pe_, wple=wple):
                            for k in range(NPC):
                                ins = nc.tensor.matmul(pe_.ap, lhsT=pT.ap[:, k, b * P:(b + 1) * P], rhs=wple.ap[:, k, :],
                                                       start=(k == 0), stop=(k == NPC - 1))
                            return ins
                        self.op("pe", mm, reads=[pT, wple], writes=[pe_])
                        t = tf[ei % 2]
                        ei += 1
                        self.op("act", lambda t=t, b=b, pss=pss: nc.scalar.activation(out=t.ap, in_=pss[b].ap, func=AF.Sigmoid),
                                reads=[pss[b]], writes=[t])
                        self.op("dve", lambda t=t, pe_=pe_: nc.vector.tensor_tensor(out=t.ap, in0=t.ap, in1=pe_.ap, op=ALU.mult),
                                reads=[t, pe_], writes=[t])
                        self.op("pool", lambda t=t, b=b, e=e: nc.gpsimd.tensor_tensor(
                            out=hb[b].ap[:, e * TT:(e + 1) * TT], in0=hb[b].ap[:, e * TT:(e + 1) * TT], in1=t.ap, op=ALU.add),
                            reads=[t, hb[b]], writes=[hb[b]])
                if ti + 1 < ntile:
                    load_yT(ti + 1)
                for b in range(NB):
                    stt = self.stat[self.stat_i % len(self.stat)]
                    self.stat_i += 1
                    H2 = D // 2
                    self.op("dve", lambda stt=stt: nc.vector.memset(stt.ap[:, 0:8], 0.0), writes=[stt])
                    for hf in range(2):
                        self.op("act", lambda stt=stt, b=b, hf=hf: nc.scalar.activation(
                            out=jflat, in_=hb[b].ap[:, hf * H2:(hf + 1) * H2], func=AF.Square,
                            accum_out=stt.ap[:, 4 + hf:5 + hf]), reads=[hb[b]], writes=[hidT, stt])
                    self.op("dve", lambda stt=stt: nc.vector.tensor_tensor(out=stt.ap[:, 0:1], in0=stt.ap[:, 4:5], in1=stt.ap[:, 5:6],
                                                                           op=ALU.add), reads=[stt], writes=[stt])
                    self.op("dve", lambda stt=stt: nc.vector.tensor_scalar(out=stt.ap[:, 1:2], in0=stt.ap[:, 0:1], scalar1=1.0 / D,
                                                                           scalar2=EPS, op0=ALU.mult, op1=ALU.add), reads=[stt], writes=[stt])
                    self.rsqrt(stt, stt.ap[:, 2:3], stt.ap[:, 1:2])
                    self.op("dve", lambda stt=stt, b=b: nc.vector.scalar_tensor_tensor(
                        out=hb[b].ap, in0=hb[b].ap, scalar=stt.ap[:, 2:3], in1=gfin.ap, op0=ALU.mult, op1=ALU.mult),
                        reads=[hb[b], stt, gfin], writes=[hb[b]])
                    self.dma("pool", [lambda b=b: nc.gpsimd.dma_start(out=self.out_d[t0 + b * P:t0 + (b + 1) * P, :], in_=hb[b].ap)],
                             hb[b].dsem, reads=[hb[b]])


def make_consts(cfg, norm_mix, norm_ffn, norm_ple, norm_b_out, b_s, norm_a_out, norm_final, w_s):
    D, AW, BW = cfg["D"], cfg["AW"], cfg["BW"]
    NCH, NHA, NHB = D // P, AW // P, BW // P
    ar = np.arange(P)
    ident = np.eye(P, dtype=np.float32)
    ones = np.ones((P, P), np.float32)
    trilT = (ar[None, :] >= ar[:, None]).astype(np.float32)
    fm = lambda g, n: np.ascontiguousarray(np.asarray(g, np.float32).reshape(n, P).T)
    cst = np.concatenate([ident, ones, trilT, fm(norm_mix, NCH), fm(norm_ffn, NCH), fm(norm_ple, NCH),
                          fm(norm_b_out, NHB), np.ascontiguousarray(np.asarray(b_s, np.float32).T)], axis=1)
    tinc = (ar[:, None] >= ar[None, :]).astype(np.float32)
    tcomp = 1.0 - tinc
    q = np.arange(TT)
    dm = np.stack([((j * P + ar[:, None]) < q[None, :]).astype(np.float32) for j in range(NB)], axis=1)
    cbf = np.concatenate([tinc, tcomp, dm.reshape(P, NB * TT)], axis=1).astype(ml_dtypes.bfloat16)
    ga_rep = np.ascontiguousarray(np.broadcast_to(np.asarray(norm_a_out, np.float32)[None, :], (P, AW)))
    gfin_rep = np.ascontiguousarray(np.broadcast_to(np.asarray(norm_final, np.float32)[None, :], (P, D)))
    wsT = np.ascontiguousarray(np.transpose(np.asarray(w_s, np.float32), (2, 0, 1)).reshape(P, NHA * P))
    return dict(cst=np.ascontiguousarray(cst), cbf=np.ascontiguousarray(cbf), ga_rep=ga_rep, gfin_rep=gfin_rep, wsT=wsT)


def kernel(x, p, norm_mix, w_in, w_s, b_s, norm_a_out, norm_b_out, w_out, norm_ffn, w_up, w_down,
           norm_ple, w_ple_gate, w_ple_proj, norm_final):
    cfg = FULL
    T, TP, D = cfg["T"], cfg["TP"], cfg["D"]
    x = np.asarray(x, np.float32)
    p = np.asarray(p, np.float32)
    B, S, _ = x.shape
    consts = make_consts(cfg, norm_mix[0], norm_ffn[0], norm_ple[0], norm_b_out[0], b_s[0], norm_a_out[0], norm_final, w_s[0])
    shared = dict(w_in=np.ascontiguousarray(w_in[0], dtype=np.float32), w_out=np.ascontiguousarray(w_out[0], dtype=np.float32),
                  w_up=np.ascontiguousarray(w_up[0], dtype=np.float32), w_down=np.ascontiguousarray(w_down[0], dtype=np.float32),
                  w_gate=np.ascontiguousarray(w_ple_gate[0], dtype=np.float32),
                  w_ple=np.ascontiguousarray(w_ple_proj[0], dtype=np.float32), **consts)
    zeros = np.zeros((TP, D), np.float32)
    in_maps = []
    for core in range(8):
        b, s = core // 2, core % 2
        m = dict(shared)
        m["x_own"] = np.ascontiguousarray(x[b, s * T:(s + 1) * T])
        m["x_pre"] = np.ascontiguousarray(x[b, 0:TP]) if s == 1 else zeros
        m["p_own"] = np.ascontiguousarray(p[0, b, s * T:(s + 1) * T])
        in_maps.append(m)
    nc = K(cfg).build()
    res = run_bass_kernel_spmd(nc, in_maps, core_ids=list(range(8)))
    out = np.empty((B, S, D), np.float32)
    for core in range(8):
        b, s = core // 2, core % 2
        out[b, s * T:(s + 1) * T] = np.asarray(res.results[core]["out"], np.float32)
    return out
```
